# Optimizing a Trainium2 kernel written in Bass

```python
import jax
import jax.numpy as jnp
from jax import lax
import numpy as np

D_MODEL = 1024
BATCH = 8
SEQ = 2048
DEPTH = 1
DEC_BATCH = 16
DEC_SEQ = 32
PAST_LEN = 1024

CHUNK = 64
A_HEADS = 8
A_HEAD_DIM = 64
A_WIDTH = A_HEADS * A_HEAD_DIM
BAND_CHUNKS = 8
MAX_REL = 128
B_HEADS = 4
B_HEAD_DIM = 128
B_WIDTH = B_HEADS * B_HEAD_DIM
CONV_W = 4
IN_WIDTH = 3 * A_WIDTH + 4 * B_WIDTH + 2 * B_HEADS + 2 * D_MODEL
MEM_TOKENS = 256
X_HEADS = 4
X_HEAD_DIM = D_MODEL // X_HEADS
PEER_HEADS = 8
N_KEYS = 128
N_EXPERTS = N_KEYS * N_KEYS
PEER_TOPK = 16
PEER_QDIM = 256
PEER_HALF = PEER_QDIM // 2
PEER_BLOCK = 128
EPS = 1e-6

kernel_name = 'hybrid_streaming_encoder_step'


def rms_norm(x, g):
    xf = x.astype(jnp.float32)
    r = lax.rsqrt(jnp.mean(xf * xf, axis=-1, keepdims=True) + EPS)
    return (xf * r).astype(x.dtype) * g.astype(x.dtype)


def _split_in(z):
    sizes = [A_WIDTH] * 3 + [B_WIDTH] * 4 + [2 * B_HEADS, D_MODEL, D_MODEL]
    bounds = [int(b) for b in np.cumsum(sizes)[:-1]]
    return jnp.split(z, bounds, axis=-1)


def _rel_bias(rel_bias, q_pos, k_pos):
    rel = jnp.clip(q_pos[..., :, None] - k_pos[..., None, :], -MAX_REL, MAX_REL) + MAX_REL
    return rel_bias[:, rel].astype(jnp.float32)


def band_attention_prompt(q, k, v, rel_bias):
    bsz, t, h, dh = q.shape
    nc = t // CHUNK
    width = (BAND_CHUNKS + 1) * CHUNK
    pad = BAND_CHUNKS * CHUNK
    kp = jnp.pad(k, ((0, 0), (pad, 0), (0, 0), (0, 0)))
    vp = jnp.pad(v, ((0, 0), (pad, 0), (0, 0), (0, 0)))
    band = jnp.arange(nc)[:, None] * CHUNK + jnp.arange(width)[None, :]
    kb = kp[:, band]
    vb = vp[:, band]
    q_pos = jnp.arange(nc)[:, None] * CHUNK + jnp.arange(CHUNK)[None, :]
    k_pos = band - pad
    s = jnp.einsum('bcqhd,bckhd->bhcqk', q.reshape(bsz, nc, CHUNK, h, dh), kb).astype(jnp.float32)
    s = s * (dh ** -0.5) + _rel_bias(rel_bias, q_pos, k_pos)[None]
    s = jnp.where((k_pos >= 0)[None, None, :, None, :], s, -jnp.inf)
    p = jax.nn.softmax(s, axis=-1).astype(v.dtype)
    o = jnp.einsum('bhcqk,bckhd->bcqhd', p, vb)
    return o.reshape(bsz, t, h * dh)


def band_attention_cached(q, k, v, k_prev, v_prev, rel_bias):
    bsz, s_len, h, dh = q.shape
    past = k_prev.shape[1]
    kk = jnp.concatenate([k_prev, k], axis=1)
    vv = jnp.concatenate([v_prev, v], axis=1)
    q_pos = past + jnp.arange(s_len)
    k_pos = jnp.arange(past + s_len)
    s = jnp.einsum('bqhd,bkhd->bhqk', q, kk).astype(jnp.float32)
    s = s * (dh ** -0.5) + _rel_bias(rel_bias, q_pos, k_pos)[None]
    p = jax.nn.softmax(s, axis=-1).astype(v.dtype)
    o = jnp.einsum('bhqk,bkhd->bqhd', p, vv)
    return o.reshape(bsz, s_len, h * dh)


def causal_conv(u, prev, w, b):
    t = u.shape[1]
    up = jnp.concatenate([prev, u], axis=1)
    out = b + sum(up[:, j:j + t] * w[j] for j in range(CONV_W))
    return out, up[:, up.shape[1] - (CONV_W - 1):]


def mlstm_chunk(C0, n0, m0, q, k, v, ig, lf):
    L = q.shape[1]
    b = jnp.cumsum(lf, axis=1)
    dm = b[:, :, None, :] - b[:, None, :, :] + ig[:, None, :, :]
    causal = jnp.tril(jnp.ones((L, L), dtype=bool))
    dm = jnp.where(causal[None, :, :, None], dm, -jnp.inf)
    inter = b + m0[:, None, :]
    m = jnp.maximum(dm.max(axis=2), inter)
    w_intra = jnp.exp(dm - m[:, :, None, :])
    w_inter = jnp.exp(inter - m)
    a = jnp.einsum('bthd,bshd->btsh', q, k) * w_intra
    num = jnp.einsum('btsh,bshd->bthd', a, v) + w_inter[..., None] * jnp.einsum('bthk,bhkv->bthv', q, C0)
    den = a.sum(axis=2) + w_inter * jnp.einsum('bthk,bhk->bth', q, n0)
    h = num / jnp.maximum(jnp.abs(den), jnp.exp(-m))[..., None]
    bl = b[:, -1]
    wk = bl[:, None, :] - b + ig
    ml = jnp.maximum(bl + m0, wk.max(axis=1))
    a0 = jnp.exp(bl + m0 - ml)
    ws = jnp.exp(wk - ml[:, None, :])
    C = a0[..., None, None] * C0 + jnp.einsum('bsh,bshk,bshv->bhkv', ws, k, v)
    n = a0[..., None] * n0 + jnp.einsum('bsh,bshk->bhk', ws, k)
    return h, (C, n, ml)


def mlstm_prompt(q, k, v, ig, lf):
    bsz, t, h, d = q.shape
    nc = t // CHUNK

    def to_blocks(a):
        return jnp.moveaxis(a.reshape((bsz, nc, CHUNK) + a.shape[2:]), 1, 0)

    init = (jnp.zeros((bsz, h, d, d), jnp.float32), jnp.zeros((bsz, h, d), jnp.float32),
            jnp.zeros((bsz, h), jnp.float32))

    def step(carry, xs):
        out, carry = mlstm_chunk(*carry, *xs)
        return carry, out

    carry, hs = lax.scan(step, init, (to_blocks(q), to_blocks(k), to_blocks(v), to_blocks(ig), to_blocks(lf)))
    return jnp.moveaxis(hs, 0, 1).reshape(bsz, t, h, d), carry


def memory_kv(mem, g_mem, w_mk, w_mv):
    bsz, m, _ = mem.shape
    mn = rms_norm(mem, g_mem)
    return ((mn @ w_mk).reshape(bsz, m, X_HEADS, X_HEAD_DIM),
            (mn @ w_mv).reshape(bsz, m, X_HEADS, X_HEAD_DIM))


def cross_attention(h, mem_k, mem_v, w_cq, w_co):
    bsz, t, _ = h.shape
    q = (h @ w_cq).reshape(bsz, t, X_HEADS, X_HEAD_DIM)
    s = jnp.einsum('bthd,bmhd->bhtm', q, mem_k).astype(jnp.float32) * (X_HEAD_DIM ** -0.5)
    p = jax.nn.softmax(s, axis=-1).astype(mem_v.dtype)
    o = jnp.einsum('bhtm,bmhd->bthd', p, mem_v).reshape(bsz, t, D_MODEL)
    return o @ w_co


def peer_ffn(h, w_pq, sub_keys, peer_u, peer_v):
    bsz, t, d = h.shape
    xf = h.reshape(bsz * t, d)
    n = xf.shape[0]
    nb = -(-n // PEER_BLOCK)
    xf = jnp.pad(xf, ((0, nb * PEER_BLOCK - n), (0, 0)))

    def block(xb):
        p = xb.shape[0]
        q = (xb @ w_pq).reshape(p, PEER_HEADS, 2, PEER_HALF)
        s = jnp.einsum('phcd,hckd->phck', q, sub_keys)
        sv, si = lax.top_k(s, PEER_TOPK)
        cand = sv[:, :, 0, :, None] + sv[:, :, 1, None, :]
        cidx = si[:, :, 0, :, None] * N_KEYS + si[:, :, 1, None, :]
        cv, ci = lax.top_k(cand.reshape(p, PEER_HEADS, PEER_TOPK * PEER_TOPK), PEER_TOPK)
        eidx = jnp.take_along_axis(cidx.reshape(p, PEER_HEADS, PEER_TOPK * PEER_TOPK), ci, axis=-1)
        g = jax.nn.softmax(cv.astype(jnp.float32), axis=-1).astype(xb.dtype)
        act = jax.nn.gelu(jnp.einsum('phkd,pd->phk', peer_u[eidx], xb), approximate=False)
        return jnp.einsum('phk,phkd->pd', g * act, peer_v[eidx])

    out = lax.map(block, xf.reshape(nb, PEER_BLOCK, d))
    return out.reshape(nb * PEER_BLOCK, d)[:n].reshape(bsz, t, d)


def _layer(x, mem_k, mem_v, a_k_prev, a_v_prev, conv_prev, C0, n0, m0,
           g_mix, w_in, conv_w, conv_b, b_if, g_head, rel_bias, w_a_up, w_b_up, w_out,
           g_cross, w_cq, w_co, g_ffn, w_pq, sub_keys, peer_u, peer_v):
    first = a_k_prev is None
    bsz, t, _ = x.shape
    f32 = jnp.float32
    h = rms_norm(x, g_mix)
    qa, ka, va, qb, kb, vb, ob, if_pre, ga, gb = _split_in(h @ w_in)
    qa = qa.reshape(bsz, t, A_HEADS, A_HEAD_DIM)
    ka = ka.reshape(bsz, t, A_HEADS, A_HEAD_DIM)
    va = va.reshape(bsz, t, A_HEADS, A_HEAD_DIM)
    if first:
        out_a = band_attention_prompt(qa, ka, va, rel_bias)
        keep = min(BAND_CHUNKS * CHUNK, t)
        new_ak, new_av = ka[:, t - keep:], va[:, t - keep:]
        conv_prev = jnp.zeros((bsz, CONV_W - 1, 2 * B_WIDTH), x.dtype)
    else:
        out_a = band_attention_cached(qa, ka, va, a_k_prev, a_v_prev, rel_bias)
        new_ak, new_av = ka, va
    qk_b, new_conv = causal_conv(jnp.concatenate([qb, kb], axis=-1), conv_prev, conv_w, conv_b)
    qb, kb = jnp.split(jax.nn.silu(qk_b), 2, axis=-1)
    qb = qb.reshape(bsz, t, B_HEADS, B_HEAD_DIM).astype(f32)
    kb = kb.reshape(bsz, t, B_HEADS, B_HEAD_DIM).astype(f32) * (B_HEAD_DIM ** -0.5)
    vb = vb.reshape(bsz, t, B_HEADS, B_HEAD_DIM).astype(f32)
    gates = (if_pre + b_if).astype(f32)
    ig = gates[..., :B_HEADS]
    lf = jax.nn.log_sigmoid(gates[..., B_HEADS:])
    if first:
        hb, (C, n, m) = mlstm_prompt(qb, kb, vb, ig, lf)
    else:
        hb, (C, n, m) = mlstm_chunk(C0.astype(f32), n0.astype(f32), m0.astype(f32), qb, kb, vb, ig, lf)
    hb = rms_norm(hb, g_head.reshape(B_HEADS, B_HEAD_DIM).astype(f32)).astype(x.dtype)
    hb = jax.nn.sigmoid(ob) * hb.reshape(bsz, t, B_WIDTH)
    mixed = jax.nn.sigmoid(ga) * (out_a @ w_a_up) + jax.nn.sigmoid(gb) * (hb @ w_b_up)
    x = x + mixed @ w_out
    x = x + cross_attention(rms_norm(x, g_cross), mem_k, mem_v, w_cq, w_co)
    x = x + peer_ffn(rms_norm(x, g_ffn), w_pq, sub_keys, peer_u, peer_v)
    return x, (new_ak, new_av, new_conv, C.astype(x.dtype), n.astype(x.dtype), m.astype(x.dtype))


def setup_inputs(seed: int = 0) -> dict:
    key = jax.random.key(seed)
    ks = iter(jax.random.split(key, 40))
    f32 = jnp.float32

    def nrm(shape, scale):
        return jax.random.normal(next(ks), shape, f32) * scale

    def gain(shape):
        return 1.0 + nrm(shape, 0.02)

    a_cache = min(BAND_CHUNKS * CHUNK, PAST_LEN)
    b_if = jnp.concatenate([nrm((DEPTH, B_HEADS), 0.1),
                            jnp.linspace(3.0, 6.0, B_HEADS, dtype=f32)[None] + nrm((DEPTH, B_HEADS), 0.1)], axis=-1)
    return {
        'x_prompt': nrm((BATCH, SEQ, D_MODEL), 1.0),
        'x_sample': nrm((DEC_BATCH, DEC_SEQ, D_MODEL), 1.0),
        'mem_prompt': nrm((BATCH, MEM_TOKENS, D_MODEL), 1.0),
        'cache_a_k': nrm((DEPTH, DEC_BATCH, a_cache, A_HEADS, A_HEAD_DIM), 1.0),
        'cache_a_v': nrm((DEPTH, DEC_BATCH, a_cache, A_HEADS, A_HEAD_DIM), 1.0),
        'state_b_conv': nrm((DEPTH, DEC_BATCH, CONV_W - 1, 2 * B_WIDTH), 1.0),
        'state_b_C': nrm((DEPTH, DEC_BATCH, B_HEADS, B_HEAD_DIM, B_HEAD_DIM), 0.1),
        'state_b_n': nrm((DEPTH, DEC_BATCH, B_HEADS, B_HEAD_DIM), 0.1),
        'state_b_m': nrm((DEPTH, DEC_BATCH, B_HEADS), 0.5),
        'cache_mem_k': nrm((DEPTH, DEC_BATCH, MEM_TOKENS, X_HEADS, X_HEAD_DIM), 1.0),
        'cache_mem_v': nrm((DEPTH, DEC_BATCH, MEM_TOKENS, X_HEADS, X_HEAD_DIM), 1.0),
        'g_mix': gain((DEPTH, D_MODEL)),
        'w_in': nrm((DEPTH, D_MODEL, IN_WIDTH), D_MODEL ** -0.5),
        'conv_w': nrm((DEPTH, CONV_W, 2 * B_WIDTH), 0.5),
        'conv_b': nrm((DEPTH, 2 * B_WIDTH), 0.02),
        'b_if': b_if,
        'g_head': gain((DEPTH, B_WIDTH)),
        'rel_bias': nrm((DEPTH, A_HEADS, 2 * MAX_REL + 1), 0.1),
        'w_a_up': nrm((DEPTH, A_WIDTH, D_MODEL), A_WIDTH ** -0.5),
        'w_b_up': nrm((DEPTH, B_WIDTH, D_MODEL), B_WIDTH ** -0.5),
        'w_out': nrm((DEPTH, D_MODEL, D_MODEL), D_MODEL ** -0.5),
        'g_mem': gain((DEPTH, D_MODEL)),
        'w_mk': nrm((DEPTH, D_MODEL, D_MODEL), D_MODEL ** -0.5),
        'w_mv': nrm((DEPTH, D_MODEL, D_MODEL), D_MODEL ** -0.5),
        'g_cross': gain((DEPTH, D_MODEL)),
        'w_cq': nrm((DEPTH, D_MODEL, D_MODEL), D_MODEL ** -0.5),
        'w_co': nrm((DEPTH, D_MODEL, D_MODEL), D_MODEL ** -0.5),
        'g_ffn': gain((DEPTH, D_MODEL)),
        'w_pq': nrm((DEPTH, D_MODEL, PEER_HEADS * PEER_QDIM), D_MODEL ** -0.5),
        'sub_keys': nrm((DEPTH, PEER_HEADS, 2, N_KEYS, PEER_HALF), PEER_HALF ** -0.5),
        'peer_u': nrm((DEPTH, N_EXPERTS, D_MODEL), D_MODEL ** -0.5),
        'peer_v': nrm((DEPTH, N_EXPERTS, D_MODEL), 0.1),
        'g_final': gain((D_MODEL,)),
    }


def _stack(states, i):
    return jnp.stack([st[i] for st in states])


def reference(x_prompt, x_sample, mem_prompt, cache_a_k, cache_a_v, state_b_conv, state_b_C, state_b_n,
              state_b_m, cache_mem_k, cache_mem_v, g_mix, w_in, conv_w, conv_b, b_if, g_head, rel_bias,
              w_a_up, w_b_up, w_out, g_mem, w_mk, w_mv, g_cross, w_cq, w_co, g_ffn, w_pq, sub_keys,
              peer_u, peer_v, g_final):
    yp, ys = x_prompt, x_sample
    ps, ss = [], []
    for l in range(DEPTH):
        wl = (g_mix[l], w_in[l], conv_w[l], conv_b[l], b_if[l], g_head[l], rel_bias[l], w_a_up[l], w_b_up[l],
              w_out[l], g_cross[l], w_cq[l], w_co[l], g_ffn[l], w_pq[l], sub_keys[l], peer_u[l], peer_v[l])
        mk_p, mv_p = memory_kv(mem_prompt, g_mem[l], w_mk[l], w_mv[l])
        yp, sp = _layer(yp, mk_p, mv_p, None, None, None, None, None, None, *wl)
        ys, sq = _layer(ys, cache_mem_k[l], cache_mem_v[l], cache_a_k[l], cache_a_v[l], state_b_conv[l],
                        state_b_C[l], state_b_n[l], state_b_m[l], *wl)
        ps.append(sp + (mk_p, mv_p))
        ss.append(sq)
    y_prompt = rms_norm(yp, g_final)
    y_sample = rms_norm(ys, g_final)
    return (y_prompt, y_sample,
            _stack(ps, 0), _stack(ps, 1), _stack(ps, 2), _stack(ps, 3), _stack(ps, 4), _stack(ps, 5),
            _stack(ps, 6), _stack(ps, 7),
            _stack(ss, 0), _stack(ss, 1), _stack(ss, 2), _stack(ss, 3), _stack(ss, 4), _stack(ss, 5))
```

```python
import numpy as np
from contextlib import ExitStack
import concourse.bass as bass
import concourse.mybir as mybir
from concourse.bass_utils import run_bass_kernel_spmd

F32 = mybir.dt.float32
BF16 = mybir.dt.bfloat16
U32 = mybir.dt.uint32
I32 = mybir.dt.int32
AF = mybir.ActivationFunctionType
ALU = mybir.AluOpType
AX = mybir.AxisListType

T = 2112
TP = 2048
EPS = 1e-6
LTOE = 640
NCST = 1184
PHASES = 9
NCORES = 8
DBG = ''


class Res:
    __slots__ = ("name", "w", "rs", "excl")

    def __init__(self, name="", excl=False):
        self.name = name
        self.w = None
        self.rs = []
        self.excl = excl


class _Op:
    __slots__ = ("eng", "fn", "deps", "dma", "sig", "dsem", "dval")


class Prog:
    STREAMS = ("pe", "act", "dve", "pool", "sp")
    NS = 12

    def __init__(self):
        self.ops = []
        self.last = {s: None for s in self.STREAMS}
        self.dmas = []
        self.pend = {s: set() for s in self.STREAMS}

    def op(self, eng, fn, r=(), w=(), dma=False):
        i = len(self.ops)
        deps = set(self.pend[eng])
        self.pend[eng] = set()
        xr = [res for res in r if res.excl]
        if xr:
            r = [res for res in r if not res.excl]
            w = list(w) + [res for res in xr if res not in w]
        for res in r:
            if res.w is not None:
                deps.add(res.w)
        for res in w:
            if res.w is not None:
                deps.add(res.w)
            deps.update(res.rs)
        o = _Op()
        o.eng, o.fn, o.deps, o.dma, o.sig, o.dsem, o.dval = eng, fn, deps, dma, None, None, None
        self.ops.append(o)
        for res in r:
            res.rs.append(i)
        for res in w:
            res.w = i
            res.rs = []
        self.last[eng] = i
        if dma:
            self.dmas.append(i)
        return i

    def dma(self, eng, fn, r=(), w=()):
        return self.op(eng, fn, r, w, dma=True)

    def barrier(self):
        deps = set(self.dmas)
        for s in self.STREAMS:
            if self.last[s] is not None:
                deps.add(self.last[s])
        self.dmas = []
        for s in self.STREAMS:
            self.pend[s] |= deps

    def emit(self, nc):
        ops = self.ops
        for o in ops:
            if o.eng == "pe" and not o.dma:
                o.deps = {d for d in o.deps if not (ops[d].eng == "pe" and not ops[d].dma)}
        needed = set()
        for o in ops:
            needed.update(o.deps)
        with ExitStack() as st:
            esem = {s: st.enter_context(nc.semaphore("e_" + s)) for s in self.STREAMS}
            dsem = {s: [st.enter_context(nc.semaphore("d_%s%d" % (s, k))) for k in range(self.NS)]
                    for s in ("act", "pool", "sp")}
            cnt = {s: 0 for s in self.STREAMS}
            dcnt = {s: 0 for s in self.STREAMS}
            per = {s: [] for s in self.STREAMS}
            for i, o in enumerate(ops):
                per[o.eng].append(i)
                if o.dma:
                    k = dcnt[o.eng]
                    dcnt[o.eng] += 1
                    o.dsem = dsem[o.eng][k % self.NS]
                    o.dval = 16 * (k // self.NS + 1)
                elif i in needed:
                    cnt[o.eng] += 1
                    o.sig = cnt[o.eng]
            engobj = {"pe": "tensor", "act": "scalar", "dve": "vector", "pool": "gpsimd", "sp": "sync"}

            def run_stream(s, e):
                known = {}
                final = {}
                for i in per[s]:
                    o = ops[i]
                    waits = {}
                    for d in o.deps:
                        od = ops[d]
                        if od.dma:
                            sem, val = od.dsem, od.dval
                        else:
                            sem, val = esem[od.eng], od.sig
                        if waits.get(sem, 0) < val:
                            waits[sem] = val
                    if o.dma and o.dval > 16:
                        if waits.get(o.dsem, 0) < o.dval - 16:
                            waits[o.dsem] = o.dval - 16
                    for sem, val in waits.items():
                        if known.get(sem, 0) < val:
                            e.wait_ge(sem, val)
                            known[sem] = val
                    ins = o.fn(e)
                    if o.dma:
                        ins.then_inc(o.dsem, 16)
                        final[o.dsem] = o.dval
                    elif o.sig is not None:
                        ins.then_inc(esem[s], 1)
                for sem, val in final.items():
                    if known.get(sem, 0) < val:
                        e.wait_ge(sem, val)

            with nc.Block() as block:
                for s in self.STREAMS:
                    if not per[s]:
                        continue
                    getattr(block, engobj[s])(lambda e, s=s: run_stream(s, e))
        return cnt, dcnt


def build_program():
    nc = bass.Bass("TRN2", target_bir_lowering=False)
    P = Prog()
    SB_LO = 20480
    ncnt = [0]

    def din(name, shape, dt=F32):
        return nc.dram_tensor(name, list(shape), dt, kind="ExternalInput").ap()

    def dout(name, shape, dt=F32):
        return nc.dram_tensor(name, list(shape), dt, kind="ExternalOutput").ap()

    def sb(off, shape, dt):
        ncnt[0] += 1
        return nc.alloc_sbuf_tensor_at("t%d" % ncnt[0], list(shape), dt, offset=SB_LO + off)

    xT = din("xT", [128, 8, T])
    memT = din("memT", [128, 8, 256])
    ckT = din("ckT", [2, 128, 4, 512])
    cav = din("cav", [2, 512, 512])
    convT = din("convT", [128, 8, 2, 3])
    C0d = din("C0", [2, 4, 128, 128])
    n0T = din("n0T", [128, 2, 4])
    m0T = din("m0T", [2, 4, 1])
    cmkT = din("cmkT", [2, 128, 8, 256])
    cmv = din("cmv", [2, 256, 1024])
    w_in = din("w_in", [1024, 5640])
    w_a_up = din("w_a_up", [512, 1024])
    w_b_up = din("w_b_up", [512, 1024])
    w_out = din("w_out", [1024, 1024])
    w_mk = din("w_mk", [1024, 1024])
    w_mv = din("w_mv", [1024, 1024])
    w_cq = din("w_cq", [1024, 1024])
    w_co = din("w_co", [1024, 1024])
    w_pq = din("w_pq", [1024, 2048])
    subT = din("subT", [128, 16, 128])
    peer_uv = din("peer_uv", [16384, 2048]) if PHASES >= 6 else None
    prm = din("prm", [128, 80])
    gb4 = din("gb4", [4, 2])
    ghead = din("ghead", [1, 512])
    relb = din("relb", [8, 257])
    cst = din("cst", [128, NCST])

    o_yT = dout("yT", [128, 8, T])
    o_pakT = dout("pakT", [128, 4, 512])
    o_pav = dout("pav", [512, 512])
    o_pconvT = dout("pconvT", [128, 8, 3])
    o_pC = dout("pC", [128, 4, 128])
    o_pn = dout("pn", [128, 4])
    o_pm = dout("pm", [4, 1])
    o_pmkT = dout("pmkT", [128, 8, 256])
    o_pmv = dout("pmv", [256, 1024])
    o_sakT = dout("sakT", [128, 4, 64])
    o_sav = dout("sav", [64, 512])
    o_sconvT = dout("sconvT", [128, 8, 2, 3])
    o_sC = dout("sC", [128, 2, 4, 128])
    o_sn = dout("sn", [128, 2, 4])
    o_sm = dout("sm", [2, 4, 1])
    Rtoe = nc.dram_tensor("Rtoe", [8, 128, LTOE], F32, kind="Internal").ap()
    r_Rtoe = Res()

    def MM(out, lhsT, rhs, st, sp, r, w, sgc=False, tp=None):
        if tp is None:
            P.op("pe", lambda e: e.matmul(out, lhsT=lhsT, rhs=rhs, start=st, stop=sp, skip_group_check=sgc), r=r, w=w)
        else:
            P.op("pe", lambda e: e.matmul(out, lhsT=lhsT, rhs=rhs, start=st, stop=sp, skip_group_check=sgc, tile_position=tp), r=r, w=w)

    def ACT(out, in_, func, r, w, **kw):
        P.op("act", lambda e: e.activation(out=out, in_=in_, func=func, **kw), r=r, w=w)

    def V(name, r, w, eng="dve", **kw):
        P.op(eng, lambda e: getattr(e, name)(**kw), r=r, w=w)

    def DMA(out, in_, r=(), w=(), eng="sp"):
        P.dma(eng, lambda e: e.dma_start(out=out, in_=in_), r=r, w=w)

    pb = [nc.alloc_psum_tensor("pb%d" % i, [128, 512], F32) for i in range(8)]
    rpb = [Res("pb%d" % i, excl=True) for i in range(8)]

    o = 0
    CST = sb(o, [128, NCST], F32); o += NCST * 4
    PRM = sb(o, [128, 80], F32); o += 320
    EYEB = sb(o, [128, 128], BF16); o += 256
    ONESB = sb(o, [128, 128], BF16); o += 256
    EB0 = sb(o, [128, 8, 128], BF16); o += 2048
    EB1 = sb(o, [128, 8, 128], BF16); o += 2048
    EBC = sb(o, [128, 8, 128], BF16); o += 2048
    EB4 = sb(o, [128, 8, 128], BF16); o += 2048
    GHD = sb(o, [64, 512], F32); o += 2048
    GB4 = sb(o, [4, 2], F32); o += 32
    MKT = sb(o, [128, 8, 256], BF16); o += 4096
    MV = sb(o, [128, 2, 1024], BF16); o += 4096
    SQ_OFF = o
    SQ = sb(o, [128, 8, 512], BF16); o += 8192
    RS = sb(o, [128, 512], F32); o += 2048
    STG_OFF = o
    STG = [sb(o, [128, 512], F32), sb(o + 2048, [128, 512], F32)]; o += 4096
    WA_OFF = o
    WA = sb(o, [128, 8, 512], BF16); o += 8192
    WB = sb(o, [128, 8, 512], BF16); o += 8192
    XN = sb(o, [128, 8, T], BF16); o += 8 * T * 2
    MIX_OFF = o
    MIX = sb(o, [128, 8, T], BF16); o += 8 * T * 2
    PH = o
    r_cst, r_prm, r_eyeb, r_ones = Res(), Res(), Res(), Res()
    r_eb = Res()
    r_ghd, r_gb4, r_mkt, r_mv, r_sq, r_rs = Res(), Res(), Res(), Res(), Res(), Res()
    r_stg = [Res(), Res()]
    r_wa, r_wb, r_xn, r_mix = Res(), Res(), Res(), Res()
    stg_i = [0]

    EYE = CST[:, 0:128]
    MASK0 = CST[:, 128:256]
    MASK4 = CST[:, 256:384]
    MNEG64 = CST[0:64, 384:640]
    MNEG32 = CST[0:32, 640:768]
    BD64 = CST[0:4, 768:1024]
    BD32 = CST[0:4, 1024:1152]
    IOTA16 = CST[:, 1152:1168]
    IOTA256 = CST[:, 1168:1184]

    def out_stage(src_psum, rsrc, dram_ap, rows=128, cols=512):
        i = stg_i[0] % 2
        stg_i[0] += 1
        V("tensor_copy", r=[rsrc], w=[r_stg[i]], out=STG[i][0:rows, 0:cols], in_=src_psum)
        DMA(dram_ap, STG[i][0:rows, 0:cols], r=[r_stg[i]])

    def load_w(dst, rdst, src2d, c0, ncols, kchunks=8, part=128):
        src = src2d[:, c0:c0 + ncols].rearrange("(k p) n -> p k n", p=part)
        P.dma("pool", lambda e: e.dma_start(out=dst, in_=src), w=[rdst])

    DMA(CST[:], cst, w=[r_cst])
    DMA(PRM[:], prm, w=[r_prm])
    DMA(GB4[:], gb4, w=[r_gb4])
    DMA(GHD[:], ghead.to_broadcast([64, 512]), w=[r_ghd])
    V("tensor_copy", r=[r_cst], w=[r_eyeb], out=EYEB[:], in_=EYE)
    V("memset", r=[], w=[r_ones], ap=ONESB[:], constant=1.0)
    TB = sb(PH, [8, LTOE], F32); r_tb = Res()
    EBS = sb(PH + LTOE * 4, [128, 3, 8, 128], F32); r_ebs = Res()
    DMA(TB[:, 0:257], relb, w=[r_tb])
    V("tensor_copy", r=[r_tb], w=[r_tb], out=TB[:, 257:LTOE], in_=TB[:, 256:257].to_broadcast([8, LTOE - 257]))
    DMA(Rtoe, TB[:].unsqueeze(1).to_broadcast([8, 128, LTOE]), r=[r_tb], w=[r_Rtoe])
    for d in range(3):
        off = 128 + d * 128
        DMA(EBS[:, d, :, :], bass.AP(Rtoe.tensor, off, [[LTOE - 1, 128], [128 * LTOE, 8], [1, 128]]), r=[r_Rtoe], w=[r_ebs])
    ACT(EB0[:], EBS[:, 0, :, :], AF.Exp, r=[r_ebs], w=[r_eb])
    ACT(EB1[:], EBS[:, 1, :, :], AF.Exp, r=[r_ebs], w=[r_eb])
    ACT(EBC[:], EBS[:, 2, :, :], AF.Exp, r=[r_ebs], w=[r_eb])
    V("tensor_tensor", r=[r_eb, r_cst], w=[r_eb], out=EB4[:], in0=EBC[:], in1=MASK4.unsqueeze(1).to_broadcast([128, 8, 128]), op=ALU.mult)
    V("tensor_tensor", r=[r_eb, r_cst], w=[r_eb], out=EB0[:], in0=EB0[:], in1=MASK0.unsqueeze(1).to_broadcast([128, 8, 128]), op=ALU.mult)
    P.barrier()
    if PHASES <= 0:
        P.emit(nc)
        return nc
    def rmsnorm(src, rsrc, dst, rdst, gcol, n, bank=7):
        for k in range(8):
            ACT(SQ[:, k, 0:n], src(k), AF.Square, r=[rsrc], w=[r_sq])
        for k in range(8):
            MM(pb[bank][:, 0:n], ONESB[:], SQ[:, k, 0:n], k == 0, k == 7, r=[r_sq, r_ones], w=[rpb[bank]])
        ACT(RS[:, 0:n], pb[bank][:, 0:n], AF.Ln, r=[rpb[bank]], w=[r_rs], scale=1.0 / 1024, bias=EPS)
        ACT(RS[:, 0:n], RS[:, 0:n], AF.Exp, r=[r_rs], w=[r_rs], scale=-0.5)
        for k in range(8):
            V("scalar_tensor_tensor", r=[rsrc, r_rs, r_prm], w=[rdst], out=dst(k), in0=src(k),
              scalar=PRM[:, gcol + k:gcol + k + 1], in1=RS[:, 0:n], op0=ALU.mult, op1=ALU.mult)

    GROUPS = [(0, 512), (512, 512), (1024, 512), (1536, 512), (2048, 64)]

    MEMF = sb(PH, [128, 8, 256], F32); r_memf = Res()
    MN = sb(PH + 8192, [128, 8, 256], BF16); r_mn = Res()
    DMA(MEMF[:], memT, w=[r_memf])
    rmsnorm(lambda k: MEMF[:, k, :], r_memf, lambda k: MN[:, k, :], r_mn, 32, 256)
    if PHASES <= 0.5:
        P.emit(nc)
        return nc
    load_w(WA[:], r_wa, w_mk, 0, 512)
    load_w(WB[:], r_wb, w_mk, 512, 512)
    for fc in range(8):
        W, rW = (WA, r_wa) if fc < 4 else (WB, r_wb)
        c = (fc % 4) * 128
        b = fc % 2
        for k in range(8):
            MM(pb[b][:, 0:256], W[:, k, c:c + 128], MN[:, k, :], k == 0, k == 7, r=[rW, r_mn], w=[rpb[b]])
        if 'a' not in DBG:
            ACT(MKT[:, fc, :], pb[b][:, 0:256], AF.Copy, r=[rpb[b]], w=[r_mkt])
        if 'b' not in DBG:
            out_stage(pb[b][:, 0:256], rpb[b], o_pmkT[:, fc, :], cols=256)
    if PHASES <= 0.7:
        P.emit(nc)
        return nc
    load_w(WA[:], r_wa, w_mv, 0, 512)
    load_w(WB[:], r_wb, w_mv, 512, 512)
    for mt in range(2):
        for half in range(2):
            W, rW = (WA, r_wa) if half == 0 else (WB, r_wb)
            b = half
            for k in range(8):
                MM(pb[b][:, :], MN[:, k, mt * 128:(mt + 1) * 128], W[:, k, :], k == 0, k == 7, r=[rW, r_mn], w=[rpb[b]])
            ACT(MV[:, mt, half * 512:(half + 1) * 512], pb[b][:, :], AF.Copy, r=[rpb[b]], w=[r_mv])
            out_stage(pb[b][:, :], rpb[b], o_pmv[mt * 128:(mt + 1) * 128, half * 512:(half + 1) * 512])
    P.barrier()
    if PHASES <= 1:
        P.emit(nc)
        return nc

    q = PH
    KA = sb(q, [128, 4, T], BF16); q += 4 * T * 2
    VA = sb(q, [128, 18, 512], BF16); q += 18 * 1024
    QZ = [sb(q, [128, 4, 512], BF16), sb(q + 4096, [128, 4, 512], BF16)]; q += 8192
    XG = sb(q, [128, 8, 512], F32); r_xg = Res()
    WAU = sb(q, [64, 8, 1024], BF16); q += 16384
    OA = sb(q, [64, 8, 512], BF16); q += 8192
    EE = [sb(q, [128, 512], BF16), sb(q + 1024, [128, 512], BF16)]; q += 2048
    RD = sb(q, [64, 512], F32); q += 2048
    r_rd_a = Res()
    CK = sb(q, [128, 4, 512], BF16); q += 4096
    CV = sb(q, [128, 4, 512], BF16); q += 4096
    RD2 = [RD, sb(q, [64, 512], F32)]; q += 2048
    r_rd2 = [r_rd_a, Res()]
    assert SB_LO + q <= 229376, q
    r_ka, r_va, r_qa, r_wau, r_oa, r_rd, r_ck, r_cv = Res(), Res(), Res(), Res(), Res(), Res(), Res(), Res()
    r_ee = [Res(), Res()]

    for (t0, n) in GROUPS:
        DMA(XG[:, :, 0:n], xT[:, :, t0:t0 + n], w=[r_xg])
        rmsnorm(lambda k: XG[:, k, 0:n], r_xg, lambda k: XN[:, k, t0:t0 + n], r_xn, 0, n)
    P.barrier()
    if PHASES <= 1.3:
        P.emit(nc)
        return nc
    load_w(WA[:], r_wa, w_in, 512, 512)
    load_w(WB[:], r_wb, w_in, 1024, 512)
    for gi, (t0, n) in enumerate(GROUPS):
        for pr in range(4):
            b = pr % 2
            for k in range(8):
                MM(pb[b][:, 0:n], WA[:, k, pr * 128:(pr + 1) * 128], XN[:, k, t0:t0 + n], k == 0, k == 7, r=[r_wa, r_xn], w=[rpb[b]])
            ACT(KA[:, pr, t0:t0 + n], pb[b][:, 0:n], AF.Copy, r=[rpb[b]], w=[r_ka])
            if gi == 3:
                out_stage(pb[b][:, 0:n], rpb[b], o_pakT[:, pr, :])
            if gi == 4:
                out_stage(pb[b][:, 0:n], rpb[b], o_sakT[:, pr, :], cols=64)
        ntile = 4 if gi < 4 else 2
        for j in range(ntile):
            ti = gi * 4 + j
            rows = 128 if gi < 4 else 32
            tk0 = ti * 128 if gi < 4 else 2048 + j * 32
            b = 2 + j % 2
            for k in range(8):
                MM(pb[b][0:rows, :], XN[:, k, tk0:tk0 + rows], WB[:, k, :], k == 0, k == 7, r=[r_wb, r_xn], w=[rpb[b]])
            ACT(VA[0:rows, ti, :], pb[b][0:rows, :], AF.Copy, r=[rpb[b]], w=[r_va])
            if 12 <= ti <= 15:
                out_stage(pb[b][:, :], rpb[b], o_pav[(ti - 12) * 128:(ti - 11) * 128, :])
            if ti >= 16:
                out_stage(pb[b][0:32, :], rpb[b], o_sav[(ti - 16) * 32:(ti - 15) * 32, :], rows=32)

    if PHASES <= 1.5:
        P.emit(nc)
        return nc
    load_w(WA[:], r_wa, w_in, 0, 512)
    V("memset", r=[], w=[r_qa], ap=QZ[0][:], constant=0.0)
    V("memset", r=[], w=[r_qa], ap=QZ[1][:], constant=0.0)
    if 'w' not in DBG:
        for hf in range(2):
            P.dma("pool", lambda e, hf=hf: e.dma_start(out=WAU[:, :, hf * 512:(hf + 1) * 512],
                  in_=w_a_up[:, hf * 512:(hf + 1) * 512].rearrange("(h p) n -> p h n", p=64)), w=[r_wau])

    ablk = [0]

    def attn_block(qcols, nq, keys, hg, q_pr_src):
        nkb = len(keys)
        bn, bd = (6, 7) if ablk[0] % 2 == 0 else (2, 3)
        RDt = RD2[ablk[0] % 2]
        r_rdt = r_rd2[ablk[0] % 2]
        ablk[0] += 1

        def scores(idx):
            kfn, nk, vfn, ebfn = keys[idx]
            sb_i = 4 + idx % 2
            for hh in range(4):
                h = hg * 4 + hh
                MM(pb[sb_i][0:nk, hh * 128:hh * 128 + nq], kfn(h), q_pr_src(h), True, True, r=[r_ka, r_qa, r_ck], w=[rpb[sb_i]])

        scores(0)
        for idx, (kfn, nk, vfn, ebfn) in enumerate(keys):
            sb_i = 4 + idx % 2
            if idx + 1 < nkb:
                scores(idx + 1)
            E = EE[idx % 2]
            rE = r_ee[idx % 2]
            Ev = E[0:nk, :].rearrange("p (h q) -> p h q", h=4)[:, :, 0:nq]
            ACT(Ev, pb[sb_i][0:nk, :].rearrange("p (h q) -> p h q", h=4)[:, :, 0:nq], AF.Exp, r=[rpb[sb_i]], w=[rE], scale=0.125)
            V("tensor_tensor", r=[rE, r_eb], w=[rE], out=Ev, in0=Ev, in1=ebfn(hg), op=ALU.mult)
            for hh in range(4):
                h = hg * 4 + hh
                first = (idx == 0 and hh == 0)
                MM(pb[bn][0:64, hh * 128:hh * 128 + nq], vfn(h), E[0:nk, hh * 128:hh * 128 + nq], first, idx == nkb - 1,
                   r=[rE, r_va, r_cv], w=[rpb[bn]], sgc=True)
            for hh in range(4):
                first = (idx == 0 and hh == 0)
                MM(pb[bd][0:64, hh * 128:hh * 128 + nq], ONESB[0:nk, 0:64], E[0:nk, hh * 128:hh * 128 + nq], first, idx == nkb - 1,
                   r=[rE, r_ones], w=[rpb[bd]], sgc=True)
        pn = pb[bn][0:64, :].rearrange("p (h q) -> p h q", h=4)[:, :, 0:nq]
        pd = pb[bd][0:64, :].rearrange("p (h q) -> p h q", h=4)[:, :, 0:nq]
        rdv = RDt[:, :].rearrange("p (h q) -> p h q", h=4)[:, :, 0:nq]
        V("reciprocal", r=[rpb[bd]], w=[r_rdt], out=rdv, in_=pd)
        V("tensor_tensor", r=[rpb[bn], r_rdt], w=[r_oa], out=OA[:, hg * 4:(hg + 1) * 4, qcols:qcols + nq], in0=pn, in1=rdv, op=ALU.mult)

    def up_proj_A(t0, n):
        for fc in range(8):
            b = fc % 2
            for h in range(8):
                MM(pb[b][:, 0:n], WAU[:, h, fc * 128:(fc + 1) * 128], OA[:, h, 0:n], h == 0, h == 7, r=[r_wau, r_oa], w=[rpb[b]])
            ACT(MIX[:, fc, t0:t0 + n], pb[b][:, 0:n], AF.Copy, r=[rpb[b]], w=[r_mix])

    def ebsel(tile, nk, nq):
        return lambda hg: tile[0:nk, hg * 4:(hg + 1) * 4, 0:nq]

    for gi, (t0, n) in enumerate(GROUPS):
        for pr in range(4):
            b = pr % 2
            for k in range(8):
                MM(pb[b][:, 0:n], WA[:, k, pr * 128:(pr + 1) * 128], XN[:, k, t0:t0 + n], k == 0, k == 7, r=[r_wa, r_xn], w=[rpb[b]])
            ACT(QZ[0][0:64, pr, 0:n], pb[b][0:64, 0:n], AF.Copy, r=[rpb[b]], w=[r_qa])
            ACT(QZ[1][64:128, pr, 0:n], pb[b][64:128, 0:n], AF.Copy, r=[rpb[b]], w=[r_qa])
        if gi < 4:
            for j in range(4):
                i = gi * 4 + j
                for hg in range(2):
                    keys = []
                    for kb in range(max(0, i - 4), i + 1):
                        ebt = {0: EB0, 1: EB1, 2: EBC, 3: EBC, 4: EB4}[i - kb]
                        keys.append((lambda h, kb=kb: KA[:, h // 2, kb * 128:(kb + 1) * 128], 128,
                                     lambda h, kb=kb: VA[:, kb, h * 64:(h + 1) * 64], ebsel(ebt, 128, 128)))
                    attn_block(j * 128, 128, keys, hg,
                               lambda h, j=j: QZ[h % 2][:, h // 2, j * 128:(j + 1) * 128])
        else:
            for s in range(2):
                P.dma("pool", lambda e, s=s: e.dma_start(out=CK[:], in_=ckT[s]), w=[r_ck])
                P.dma("pool", lambda e, s=s: e.dma_start(out=CV[:], in_=cav[s].rearrange("(b p) n -> p b n", p=128)), w=[r_cv])
                for hg in range(2):
                    keys = []
                    for kb in range(4):
                        ebt = EB1 if kb == 3 else EBC
                        keys.append((lambda h, kb=kb: CK[:, h // 2, kb * 128:(kb + 1) * 128], 128,
                                     lambda h, kb=kb: CV[:, kb, h * 64:(h + 1) * 64], ebsel(ebt, 128, 32)))
                    tk = 2048 + s * 32
                    keys.append((lambda h, tk=tk: KA[:, h // 2, tk:tk + 32], 32,
                                 lambda h, s=s: VA[0:32, 16 + s, h * 64:(h + 1) * 64], ebsel(EB0, 32, 32)))
                    attn_block(s * 32, 32, keys, hg,
                               lambda h, s=s: QZ[h % 2][:, h // 2, s * 32:s * 32 + 32])
        if 'u' not in DBG:
            up_proj_A(t0, n)
        if PHASES <= 1.7 and gi == 0:
            break
        if PHASES <= 1.8 and gi == 3:
            break
    P.barrier()
    if PHASES <= 1.9:
        P.emit(nc)
        return nc
    SG = sb(PH, [128, 512], BF16); r_sg = Res()
    for blk in range(2):
        load_w(WA[:], r_wa, w_in, 3592 + blk * 512, 512)
        for (t0, n) in GROUPS:
            for c4 in range(4):
                fc = blk * 4 + c4
                b = c4 % 2
                for k in range(8):
                    MM(pb[b][:, 0:n], WA[:, k, c4 * 128:(c4 + 1) * 128], XN[:, k, t0:t0 + n], k == 0, k == 7, r=[r_wa, r_xn], w=[rpb[b]])
                ACT(SG[:, 0:n], pb[b][:, 0:n], AF.Sigmoid, r=[rpb[b]], w=[r_sg])
                V("tensor_tensor", r=[r_sg, r_mix], w=[r_mix], out=MIX[:, fc, t0:t0 + n], in0=MIX[:, fc, t0:t0 + n], in1=SG[:, 0:n], op=ALU.mult)
    P.barrier()
    if PHASES <= 2:
        P.emit(nc)
        return nc

    q = PH
    WQK = sb(q, [128, 8, 1024], BF16); q += 16384
    HB = sb(q, [128, 4, T], BF16); q += 4 * T * 2
    STGC = [sb(q, [128, 520], F32), sb(q + 2080, [128, 520], F32)]; q += 4160
    CT_OFF = q
    CT = sb(q, [128, 512], F32); q += 2048
    HH = sb(CT_OFF, [64, 512], F32)
    QB = sb(q, [128, 4, 512], BF16); q += 4096
    KB = sb(q, [128, 4, 512], BF16); q += 4096
    V1 = sb(q, [64, 8, 4, 132], BF16); q += 8 * 4 * 132 * 2
    SOB = SQ
    WG = sb(q, [128, 8, 8], BF16); q += 128
    HALO = sb(q, [128, 8, 4], F32); q += 128
    IGR = sb(q, [4, 512], F32); q += 2048
    NBR = sb(q, [4, 512], F32); q += 2048
    XR = sb(q, [4, 512], F32); q += 2048
    MR = sb(q, [4, 512], F32); q += 2048
    TR = sb(q, [4, 512], F32); q += 2048
    ONE4 = sb(q, [4, 512], F32); q += 2048
    SM = sb(q, [4, 64], F32); q += 256
    XC = sb(q, [4, 64], F32); q += 256
    NMC = sb(q, [4, 64], F32); q += 256
    WSR = sb(q, [4, 64], F32); q += 256
    ENM = sb(q, [4, 64], F32); q += 256
    BDM = sb(q, [4, 256], F32); q += 1024
    CN = sb(q, [128, 4, 132], F32); q += 2112
    CNB = sb(q, [128, 4, 132], BF16); q += 1056
    WT = sb(q, [64, 256], F32); q += 1024
    AT = sb(q, [64, 256], BF16); q += 512
    COLS = sb(q, [64, 12], F32); q += 64
    A0 = sb(q, [128, 4], F32); q += 32
    NT4 = sb(q, [128, 4], F32); q += 32
    r_nt4 = Res()
    HI_OFF = q
    HI = sb(q, [64, 4, 132], F32); q += 2112
    HSQ = sb(HI_OFF, [64, 512], F32)
    ND = sb(q, [64, 4, 132], F32); q += 2112
    DA = sb(q, [64, 4], F32); q += 32
    SS4 = sb(q, [64, 4], F32); q += 32
    HBT = sb(q, [64, 512], BF16); q += 1024
    KS = sb(q, [64, 512], BF16); q += 1024
    XCs = [XC, sb(q, [4, 64], F32)]; q += 256
    NMCs = [NMC, sb(q, [4, 64], F32)]; q += 256
    WSRs = [WSR, sb(q, [4, 64], F32)]; q += 256
    ENMs = [ENM, sb(q, [4, 64], F32)]; q += 256
    BDMs = [BDM, sb(q, [4, 256], F32)]; q += 1024
    D4s = [sb(q, [4, 8], F32), sb(q + 32, [4, 8], F32)]; q += 64
    WTs = [WT, sb(q, [64, 256], F32)]; q += 1024
    ATs = [AT, sb(q, [64, 256], BF16)]; q += 512
    COLSs = [COLS, sb(q, [64, 12], F32)]; q += 64
    A0s = [A0, sb(q, [128, 4], F32)]; q += 32
    r_crow2, r_wt2, r_at2, r_cols2 = [Res(), Res()], [Res(), Res()], [Res(), Res()], [Res(), Res()]
    assert SB_LO + q <= 229376, q
    (r_wqk, r_hb, r_ct, r_qb, r_kb, r_v1, r_sob, r_wg, r_halo, r_rows, r_sm, r_crow, r_bdm, r_cn, r_cnb, r_wt,
     r_at, r_cols, r_a0, r_hi, r_nd, r_da, r_hh, r_hsq, r_hbt, r_ks, r_one4) = [Res() for _ in range(27)]
    r_stgc = [Res(), Res()]
    r_hsq = r_hi
    r_hh = r_ct
    SOBv = SOB[0:64, :, :]
    EYE4 = CST[0:4, 0:4]
    load_w(WQK[:, :, 0:512], r_wqk, w_in, 1536, 512)
    load_w(WQK[:, :, 512:1024], r_wqk, w_in, 2048, 512)
    load_w(WA[:], r_wa, w_in, 2560, 512)
    load_w(WB[:], r_wb, w_in, 3072, 512)
    P.dma("pool", lambda e: e.dma_start(out=WG[:], in_=w_in[:, 3584:3592].rearrange("(k p) n -> p k n", p=128)), w=[r_wg])
    V("memset", r=[], w=[r_one4], ap=ONE4[:], constant=1.0)
    V("memset", r=[], w=[r_v1], ap=V1[:], constant=1.0)
    V("memset", r=[], w=[r_cn], ap=CN[:], constant=0.0)
    V("memset", r=[], w=[r_cnb], ap=CNB[:], constant=0.0)
    V("memset", r=[], w=[r_halo], ap=HALO[:], constant=0.0)
    V("memset", r=[], w=[r_sm], ap=SM[:], constant=0.0)
    SC = 128 ** -0.5

    def mlstm_group(t0, n, L, first, conv_out):
        nch = n // L
        BD = BD64 if L == 64 else BD32
        MNEG = MNEG64 if L == 64 else MNEG32
        for gi_, dst in ((0, IGR), (1, TR)):
            for k in range(8):
                MM(pb[0][0:4, 0:n], WG[:, k, gi_ * 4:gi_ * 4 + 4], XN[:, k, t0:t0 + n], k == 0, k == 7, r=[r_wg, r_xn], w=[rpb[0]])
            ACT(dst[:, 0:n], pb[0][0:4, 0:n], AF.Identity, r=[rpb[0], r_gb4], w=[r_rows], bias=GB4[:, gi_:gi_ + 1], scale=1.0)
        ACT(TR[:, 0:n], TR[:, 0:n], AF.Exp, r=[r_rows], w=[r_rows], scale=-1.0)
        ACT(TR[:, 0:n], TR[:, 0:n], AF.Ln, r=[r_rows], w=[r_rows], bias=1.0, scale=1.0)
        V("tensor_tensor_scan", r=[r_rows, r_one4, r_sm], w=[r_rows], out=NBR[:, 0:n], data0=ONE4[:, 0:n], data1=TR[:, 0:n],
          initial=(0.0 if first else SM[:, 0:1]), op0=ALU.mult, op1=ALU.add)
        V("tensor_tensor", r=[r_rows], w=[r_rows], out=XR[:, 0:n], in0=IGR[:, 0:n], in1=NBR[:, 0:n], op=ALU.add)
        V("tensor_tensor_scan", r=[r_rows, r_sm], w=[r_rows], out=MR[:, 0:n], data0=XR[:, 0:n], data1=XR[:, 0:n],
          initial=SM[:, 1:2], op0=ALU.max, op1=ALU.max)
        for fc in range(8):
            b = 1 + fc % 2
            sg, rsg = STGC[fc % 2], r_stgc[fc % 2]
            for k in range(8):
                MM(pb[b][:, 0:n], WQK[:, k, fc * 128:(fc + 1) * 128], XN[:, k, t0:t0 + n], k == 0, k == 7, r=[r_wqk, r_xn], w=[rpb[b]])
            V("tensor_copy", r=[r_halo], w=[rsg], out=sg[:, 0:3], in_=HALO[:, fc, 0:3])
            ACT(sg[:, 3:3 + n], pb[b][:, 0:n], AF.Copy, r=[rpb[b]], w=[rsg])
            V("tensor_copy", r=[rsg], w=[r_halo], out=HALO[:, fc, 0:3], in_=sg[:, n:n + 3])
            V("tensor_scalar", r=[rsg, r_prm], w=[r_ct], out=CT[:, 0:n], in0=sg[:, 0:n], scalar1=PRM[:, 48 + fc * 4:49 + fc * 4],
              scalar2=PRM[:, 40 + fc:41 + fc], op0=ALU.mult, op1=ALU.add)
            for j in range(1, 4):
                V("scalar_tensor_tensor", r=[rsg, r_prm, r_ct], w=[r_ct], out=CT[:, 0:n], in0=sg[:, j:j + n],
                  scalar=PRM[:, 48 + fc * 4 + j:49 + fc * 4 + j], in1=CT[:, 0:n], op0=ALU.mult, op1=ALU.add)
            if fc < 4:
                ACT(QB[:, fc, 0:n], CT[:, 0:n], AF.Silu, r=[r_ct], w=[r_qb])
            else:
                ACT(CT[:, 0:n], CT[:, 0:n], AF.Silu, r=[r_ct], w=[r_ct])
                V("tensor_scalar", r=[r_ct], w=[r_kb], out=KB[:, fc - 4, 0:n], in0=CT[:, 0:n], scalar1=SC, scalar2=None, op0=ALU.mult)
        if conv_out is not None:
            DMA(conv_out, HALO[:, :, 0:3], r=[r_halo])
        for c in range(nch):
            tc = t0 + c * L
            for k in range(8):
                MM(pb[1][0:L, :], XN[:, k, tc:tc + L], WA[:, k, :], k == 0, k == 7, r=[r_wa, r_xn], w=[rpb[1]])
            ACT(V1[0:L, c, :, 0:128], pb[1][0:L, :].rearrange("p (h d) -> p h d", h=4), AF.Copy, r=[rpb[1]], w=[r_v1])
            for k in range(8):
                MM(pb[2][0:L, :], XN[:, k, tc:tc + L], WB[:, k, :], k == 0, k == 7, r=[r_wb, r_xn], w=[rpb[2]])
            ACT(SOBv[0:L, c, :], pb[2][0:L, :], AF.Sigmoid, r=[rpb[2]], w=[r_sob])
        def front_a(c):
            pc = c % 2
            cs = slice(c * L, (c + 1) * L)
            Mp = SM[:, 1:2] if c == 0 else MR[:, c * L - 1:c * L]
            xc, nmc, wsr, enm, bdm, d4 = XCs[pc], NMCs[pc], WSRs[pc], ENMs[pc], BDMs[pc], D4s[pc]
            rc = r_crow2[pc]
            V("tensor_scalar", r=[r_rows, r_sm], w=[rc], out=xc[:, 0:L], in0=XR[:, cs], scalar1=Mp, scalar2=None, op0=ALU.subtract)
            V("tensor_scalar", r=[r_rows, r_sm], w=[rc], out=nmc[:, 0:L], in0=MR[:, cs], scalar1=Mp, scalar2=-1.0, op0=ALU.subtract, op1=ALU.mult)
            V("tensor_tensor", r=[rc, r_cst], w=[rc], out=bdm[:, 0:4 * L].rearrange("p (h t) -> p h t", h=4),
              in0=BD.rearrange("p (h t) -> p h t", h=4), in1=nmc[:, 0:L].unsqueeze(1).to_broadcast([4, 4, L]), op=ALU.mult)
            V("tensor_scalar", r=[rc], w=[rc], out=wsr[:, 0:L], in0=xc[:, 0:L], scalar1=nmc[:, L - 1:L], scalar2=None, op0=ALU.add)
            V("tensor_tensor", r=[r_rows], w=[rc], out=enm[:, 0:L], in0=NBR[:, cs], in1=MR[:, cs], op=ALU.subtract)
            V("tensor_scalar", r=[rc, r_cst], w=[rc], out=d4[:, 0:4], in0=EYE4, scalar1=nmc[:, L - 1:L], scalar2=None, op0=ALU.mult)
            MM(pb[0][0:L, 0:4 * L], xc[:, 0:L], BD, True, False, r=[rc, r_cst], w=[rpb[0]])
            MM(pb[0][0:L, 0:4 * L], ONE4[:, 0:L], bdm[:, 0:4 * L], False, False, r=[r_one4, rc], w=[rpb[0]])
            MM(pb[0][0:L, 0:4 * L], EYE[0:L, 0:L], MNEG, False, True, r=[r_cst], w=[rpb[0]])
            ACT(WTs[pc][0:L, 0:4 * L], pb[0][0:L, 0:4 * L], AF.Exp, r=[rpb[0]], w=[r_wt2[pc]])
            MM(pb[7][0:L, 0:4], nmc[:, 0:L], EYE4, True, True, r=[rc, r_cst], w=[rpb[7]])
            MM(pb[7][0:L, 4:8], enm[:, 0:L], EYE4, True, True, r=[rc, r_cst], w=[rpb[7]])
            MM(pb[7][0:L, 8:12], wsr[:, 0:L], EYE4, True, True, r=[rc, r_cst], w=[rpb[7]])
            MM(pb[7][:, 16:20], ONE4[:, 0:128], d4[:, 0:4], True, True, r=[r_one4, rc], w=[rpb[7]])
            ACT(COLSs[pc][0:L, :], pb[7][0:L, 0:12], AF.Exp, r=[rpb[7]], w=[r_cols2[pc]])
            ACT(A0s[pc][:, :], pb[7][:, 16:20], AF.Exp, r=[rpb[7]], w=[r_cols2[pc]])
            for h in range(4):
                MM(pb[1][0:L, h * L:(h + 1) * L], KB[:, h, cs], QB[:, h, cs], True, True, r=[r_kb, r_qb], w=[rpb[1]])

        def front_b(c):
            pc = c % 2
            V("tensor_tensor", r=[rpb[1], r_wt2[pc]], w=[r_at2[pc]], out=ATs[pc][0:L, 0:4 * L], in0=pb[1][0:L, 0:4 * L], in1=WTs[pc][0:L, 0:4 * L], op=ALU.mult)

        def back(c):
            pc = c % 2
            cs = slice(c * L, (c + 1) * L)
            tc = t0 + c * L
            COLS, A0, AT = COLSs[pc], A0s[pc], ATs[pc]
            r_cols, r_at = r_cols2[pc], r_at2[pc]
            for h in range(4):
                bk, c0 = 3 + h // 2, (h % 2) * 256
                MM(pb[bk][0:L, c0:c0 + 129], QB[:, h, cs], CNB[:, h, 0:129], True, True, r=[r_qb, r_cnb], w=[rpb[bk]])
            for h in range(4):
                bk, c0 = 5 + h // 2, (h % 2) * 256
                MM(pb[bk][0:L, c0:c0 + 129], AT[0:L, h * L:(h + 1) * L], V1[0:L, c, h, 0:129], True, True, r=[r_at, r_v1], w=[rpb[bk]])
            for h in range(4):
                bk, c0 = 5 + h // 2, (h % 2) * 256
                ACT(HI[0:L, h, 0:129], pb[bk][0:L, c0:c0 + 129], AF.Copy, r=[rpb[bk]], w=[r_hi])
            for h in range(4):
                bk, c0 = 3 + h // 2, (h % 2) * 256
                V("scalar_tensor_tensor", r=[rpb[bk], r_cols, r_hi], w=[r_nd], out=ND[0:L, h, 0:129], in0=pb[bk][0:L, c0:c0 + 129],
                  scalar=COLS[0:L, h:h + 1], in1=HI[0:L, h, 0:129], op0=ALU.mult, op1=ALU.add)
            V("tensor_scalar", r=[r_nd], w=[r_da], out=DA[0:L, :], in0=ND[0:L, :, 128], scalar1=-1.0, scalar2=None, op0=ALU.mult)
            V("tensor_tensor", r=[r_nd, r_da], w=[r_da], out=DA[0:L, :], in0=DA[0:L, :], in1=ND[0:L, :, 128], op=ALU.max)
            V("tensor_tensor", r=[r_da, r_cols], w=[r_da], out=DA[0:L, :], in0=DA[0:L, :], in1=COLS[0:L, 4:8], op=ALU.max)
            V("reciprocal", r=[r_da], w=[r_da], out=DA[0:L, :], in_=DA[0:L, :])
            HHv = HH[0:L, :].rearrange("p (h d) -> p h d", h=4)
            V("tensor_tensor", r=[r_nd, r_da], w=[r_hh], out=HHv, in0=ND[0:L, :, 0:128], in1=DA[0:L, :].unsqueeze(2).to_broadcast([L, 4, 128]), op=ALU.mult)
            V("tensor_tensor", r=[r_hh], w=[r_hsq], out=HSQ[0:L, :], in0=HH[0:L, :], in1=HH[0:L, :], op=ALU.mult)
            V("tensor_reduce", r=[r_hsq], w=[r_da], out=SS4[0:L, :], in_=HSQ[0:L, :].rearrange("p (h d) -> p h d", h=4), axis=AX.X, op=ALU.add)
            ACT(SS4[0:L, :], SS4[0:L, :], AF.Sqrt, r=[r_da], w=[r_da], scale=1.0 / 128, bias=EPS)
            V("reciprocal", r=[r_da], w=[r_da], out=SS4[0:L, :], in_=SS4[0:L, :])
            V("tensor_tensor", r=[r_hh, r_da], w=[r_hh], out=HHv, in0=HHv, in1=SS4[0:L, :].unsqueeze(2).to_broadcast([L, 4, 128]), op=ALU.mult)
            V("tensor_tensor", r=[r_hh, r_ghd], w=[r_hh], out=HH[0:L, :], in0=HH[0:L, :], in1=GHD[0:L, :], op=ALU.mult)
            V("tensor_tensor", r=[r_hh, r_sob], w=[r_hbt], out=HBT[0:L, :], in0=HH[0:L, :], in1=SOBv[0:L, c, :], op=ALU.mult)
            for h in range(4):
                MM(pb[3][:, h * L:(h + 1) * L], HBT[0:L, h * 128:(h + 1) * 128], EYEB[0:L, 0:L], True, True, r=[r_hbt, r_eyeb], w=[rpb[3]])
            ACT(HB[:, :, tc:tc + L], pb[3][:, 0:4 * L].rearrange("p (h t) -> p h t", h=4), AF.Copy, r=[rpb[3]], w=[r_hb])
            for h in range(4):
                MM(pb[2][0:L, h * 128:(h + 1) * 128], KB[:, h, cs], EYEB[:, :], True, True, r=[r_kb, r_eyeb], w=[rpb[2]])
            V("tensor_tensor", r=[rpb[2], r_cols], w=[r_ks], out=KS[0:L, :].rearrange("p (h d) -> p h d", h=4),
              in0=pb[2][0:L, :].rearrange("p (h d) -> p h d", h=4), in1=COLS[0:L, 8:12].unsqueeze(2).to_broadcast([L, 4, 128]), op=ALU.mult)
            for h in range(4):
                bk, c0 = 5 + h // 2, (h % 2) * 256
                MM(pb[bk][:, c0:c0 + 129], KS[0:L, h * 128:(h + 1) * 128], V1[0:L, c, h, 0:129], True, True, r=[r_ks, r_v1], w=[rpb[bk]])
            for h in range(4):
                bk, c0 = 5 + h // 2, (h % 2) * 256
                V("scalar_tensor_tensor", r=[r_cn, r_cols, rpb[bk]], w=[r_cn], out=CN[:, h, 0:129], in0=CN[:, h, 0:129],
                  scalar=A0[:, h:h + 1], in1=pb[bk][:, c0:c0 + 129], op0=ALU.mult, op1=ALU.add)
            ACT(CNB[:], CN[:], AF.Copy, r=[r_cn], w=[r_cnb])

        front_a(0)
        front_b(0)
        for c in range(nch):
            if c + 1 < nch:
                front_a(c + 1)
            back(c)
            if c + 1 < nch:
                front_b(c + 1)
        V("tensor_copy", r=[r_rows], w=[r_sm], out=SM[:, 0:1], in_=NBR[:, n - 1:n])
        V("tensor_copy", r=[r_rows], w=[r_sm], out=SM[:, 1:2], in_=MR[:, n - 1:n])

    def state_out(oC, on, om):
        DMA(oC, CN[:, :, 0:128], r=[r_cn])
        V("tensor_copy", r=[r_cn], w=[r_nt4], out=NT4[:, :], in_=CN[:, :, 128])
        DMA(on, NT4[:, :], r=[r_nt4])
        V("tensor_tensor", r=[r_sm], w=[r_sm], out=SM[:, 16:17], in0=SM[:, 1:2], in1=SM[:, 0:1], op=ALU.subtract)
        DMA(om, SM[:, 16:17], r=[r_sm])

    for gi, (t0, n) in enumerate(GROUPS[:4]):
        mlstm_group(t0, n, 64, gi == 0, o_pconvT if gi == 3 else None)
    state_out(o_pC, o_pn, o_pm)
    for s in range(2):
        DMA(CN[:, :, 0:128], C0d[s].rearrange("h k v -> k h v"), w=[r_cn])
        DMA(NT4[:, :], n0T[:, s, :], w=[r_nt4])
        V("tensor_copy", r=[r_nt4], w=[r_cn], out=CN[:, :, 128], in_=NT4[:, :])
        ACT(CNB[:], CN[:], AF.Copy, r=[r_cn], w=[r_cnb])
        DMA(HALO[:, :, 0:3], convT[:, :, s, :], w=[r_halo])
        V("memset", r=[], w=[r_sm], ap=SM[:, 0:1], constant=0.0)
        DMA(SM[:, 1:2], m0T[s], w=[r_sm])
        mlstm_group(2048 + s * 32, 32, 32, True, o_sconvT[:, :, s, :])
        state_out(o_sC[:, s], o_sn[:, s, :], o_sm[s])
    P.barrier()
    if PHASES <= 3:
        P.emit(nc)
        return nc
    SG2 = CT
    for blk in range(2):
        P.dma("pool", lambda e, blk=blk: e.dma_start(out=WA[:, 0:4, :], in_=w_b_up[:, blk * 512:(blk + 1) * 512].rearrange("(k p) n -> p k n", p=128)), w=[r_wa])
        load_w(WB[:], r_wb, w_in, 4616 + blk * 512, 512)
        for (t0, n) in GROUPS:
            for c4 in range(4):
                fc = blk * 4 + c4
                for k in range(8):
                    MM(pb[0][:, 0:n], WB[:, k, c4 * 128:(c4 + 1) * 128], XN[:, k, t0:t0 + n], k == 0, k == 7, r=[r_wb, r_xn], w=[rpb[0]])
                ACT(SG2[:, 0:n], pb[0][:, 0:n], AF.Sigmoid, r=[rpb[0]], w=[r_ct])
                for k in range(4):
                    MM(pb[1][:, 0:n], WA[:, k, c4 * 128:(c4 + 1) * 128], HB[:, k, t0:t0 + n], k == 0, k == 3, r=[r_wa, r_hb], w=[rpb[1]])
                V("tensor_tensor", r=[r_ct, rpb[1]], w=[r_ct], out=SG2[:, 0:n], in0=SG2[:, 0:n], in1=pb[1][:, 0:n], op=ALU.mult)
                V("tensor_tensor", r=[r_ct, r_mix], w=[r_mix], out=MIX[:, fc, t0:t0 + n], in0=MIX[:, fc, t0:t0 + n], in1=SG2[:, 0:n], op=ALU.add)
    P.barrier()

    X1 = sb(PH, [128, 8, T], F32); r_x1 = Res()
    DMA(X1[:], xT, w=[r_x1])
    for blk in range(2):
        load_w(WA[:], r_wa, w_out, blk * 512, 512)
        for (t0, n) in GROUPS:
            for c4 in range(4):
                fc = blk * 4 + c4
                b = c4 % 2
                for k in range(8):
                    MM(pb[b][:, 0:n], WA[:, k, c4 * 128:(c4 + 1) * 128], MIX[:, k, t0:t0 + n], k == 0, k == 7, r=[r_wa, r_mix], w=[rpb[b]])
                V("tensor_tensor", r=[r_x1, rpb[b]], w=[r_x1], out=X1[:, fc, t0:t0 + n], in0=X1[:, fc, t0:t0 + n], in1=pb[b][:, 0:n], op=ALU.add)
    P.barrier()

    for (t0, n) in GROUPS:
        rmsnorm(lambda k: X1[:, k, t0:t0 + n], r_x1, lambda k: XN[:, k, t0:t0 + n], r_xn, 8, n)
    P.barrier()
    mixbase = MIX_OFF
    WCQ = sb(mixbase, [128, 8, 1024], BF16); r_wcq = Res()
    WCO = sb(mixbase + 16384, [128, 8, 1024], BF16); r_wco = Res()
    q = PH + 8 * T * 4
    QC = sb(q, [128, 8, 512], BF16); q += 8192
    OC = sb(q, [128, 8, 512], BF16); q += 8192
    EC = [sb(q, [128, 512], BF16), sb(q + 1024, [128, 512], BF16)]; q += 2048
    RDC = RS
    CMK = sb(SQ_OFF, [128, 8, 256], BF16)
    CMV = sb(SQ_OFF + 4096, [128, 2, 1024], BF16)
    assert SB_LO + q <= 229376, q
    r_qc, r_oc, r_rdc, r_cmk, r_cmv = Res(), Res(), Res(), Res(), Res()
    r_ec = [Res(), Res()]
    for hf in range(2):
        load_w(WCQ[:, :, hf * 512:(hf + 1) * 512], r_wcq, w_cq, hf * 512, 512)
        load_w(WCO[:, :, hf * 512:(hf + 1) * 512], r_wco, w_co, hf * 512, 512)

    def xattn(cols0, nq, MKx, rmk, MVx, rmv):
        for h in range(4):
            for mc in range(2):
                for dc in range(2):
                    MM(pb[4 + mc][:, 0:nq], MKx[:, 2 * h + dc, mc * 128:(mc + 1) * 128], QC[:, 2 * h + dc, cols0:cols0 + nq], dc == 0, dc == 1,
                       r=[rmk, r_qc], w=[rpb[4 + mc]])
                ACT(EC[mc][:, 0:nq], pb[4 + mc][:, 0:nq], AF.Exp, r=[rpb[4 + mc]], w=[r_ec[mc]], scale=1.0 / 16)
            for mc in range(2):
                MM(pb[7][:, 0:nq], ONESB[:, :], EC[mc][:, 0:nq], mc == 0, mc == 1, r=[r_ones, r_ec[mc]], w=[rpb[7]])
            V("reciprocal", r=[rpb[7]], w=[r_rdc], out=RDC[:, 0:nq], in_=pb[7][:, 0:nq])
            for dvc in range(2):
                for mc in range(2):
                    MM(pb[6][:, 0:nq], MVx[:, mc, h * 256 + dvc * 128:h * 256 + (dvc + 1) * 128], EC[mc][:, 0:nq], mc == 0, mc == 1,
                       r=[rmv, r_ec[mc]], w=[rpb[6]])
                V("tensor_tensor", r=[rpb[6], r_rdc], w=[r_oc], out=OC[:, 2 * h + dvc, cols0:cols0 + nq], in0=pb[6][:, 0:nq], in1=RDC[:, 0:nq], op=ALU.mult)

    for gi, (t0, n) in enumerate(GROUPS):
        for fc in range(8):
            b = fc % 2
            for k in range(8):
                MM(pb[b][:, 0:n], WCQ[:, k, fc * 128:(fc + 1) * 128], XN[:, k, t0:t0 + n], k == 0, k == 7, r=[r_wcq, r_xn], w=[rpb[b]])
            ACT(QC[:, fc, 0:n], pb[b][:, 0:n], AF.Copy, r=[rpb[b]], w=[r_qc])
        if gi < 4:
            xattn(0, n, MKT, r_mkt, MV, r_mv)
        else:
            for s in range(2):
                P.dma("pool", lambda e, s=s: e.dma_start(out=CMK[:], in_=cmkT[s]), w=[r_cmk])
                for hf in range(2):
                    P.dma("pool", lambda e, s=s, hf=hf: e.dma_start(out=CMV[:, :, hf * 512:(hf + 1) * 512],
                          in_=cmv[s][:, hf * 512:(hf + 1) * 512].rearrange("(b p) n -> p b n", p=128)), w=[r_cmv])
                xattn(s * 32, 32, CMK, r_cmk, CMV, r_cmv)
        for fc in range(8):
            b = fc % 2
            for k in range(8):
                MM(pb[b][:, 0:n], WCO[:, k, fc * 128:(fc + 1) * 128], OC[:, k, 0:n], k == 0, k == 7, r=[r_wco, r_oc], w=[rpb[b]])
            V("tensor_tensor", r=[r_x1, rpb[b]], w=[r_x1], out=X1[:, fc, t0:t0 + n], in0=X1[:, fc, t0:t0 + n], in1=pb[b][:, 0:n], op=ALU.add)
    P.barrier()

    for (t0, n) in GROUPS:
        rmsnorm(lambda k: X1[:, k, t0:t0 + n], r_x1, lambda k: XN[:, k, t0:t0 + n], r_xn, 16, n)
    X1d = nc.dram_tensor("X1d", [128, 8, T], F32, kind="Internal").ap()
    r_x1d = Res()
    DMA(X1d, X1[:], r=[r_x1], w=[r_x1d])
    P.barrier()
    WPQ = sb(MIX_OFF, [128, 8, 2048], BF16); r_wpq = Res()
    QP = sb(SQ_OFF, [128, 16, 512], BF16); r_qp = Res()
    SUBT = sb(SQ_OFF + 16384, [128, 16, 128], BF16); r_subt = Res()
    SCO = sb(SQ_OFF + 20480, [128, 16, 128], F32); r_sc = Res()
    q = PH
    EIDX = sb(q, [128, 17, 128], I32); q += 8704
    GALL = sb(q, [128, 17, 128], F32); q += 8704
    SC2 = sb(q, [128, 2048], F32); q += 8192
    EQ = sb(q, [128, 2048], F32); q += 8192
    SV = sb(q, [128, 16, 16], F32); q += 1024
    SI = sb(q, [128, 16, 16], U32); q += 1024
    SIF = sb(q, [128, 16, 16], F32); q += 1024
    CVP = sb(q, [128, 8, 16], F32); q += 512
    CI = sb(q, [128, 8, 16], U32); q += 512
    CIF = sb(q, [128, 8, 16], F32); q += 512
    AF_ = sb(q, [128, 8, 16], F32); q += 512
    BF_ = sb(q, [128, 8, 16], F32); q += 512
    ISEL = sb(q, [128, 8, 16], F32); q += 512
    JSEL = sb(q, [128, 8, 16], F32); q += 512
    ZZ = sb(q, [128, 8], F32); q += 32
    Q6A_END = q
    Q6B = PH + 17408
    (r_eidx, r_gall, r_sc2, r_eq, r_sv, r_si, r_cv, r_ci, r_ab, r_ij, r_zz) = [Res() for _ in range(11)]
    r_svp, r_sip, r_sc2p, r_cvp, r_cip, r_eqp = [[Res(), Res()] for _ in range(6)]
    for hf in range(4):
        load_w(WPQ[:, :, hf * 512:(hf + 1) * 512], r_wpq, w_pq, hf * 512, 512)
    P.dma("pool", lambda e: e.dma_start(out=SUBT[:], in_=subT), w=[r_subt])
    PUVB = nc.dram_tensor("PUVB", [16384, 2048], BF16, kind="Internal").ap()
    r_puvb = Res()
    NCAST = 16
    r_cast = [Res() for _ in range(NCAST)]
    for ic in range(NCAST):
        rows = 16384 // NCAST
        P.dma("pool", lambda e, ic=ic, rows=rows: e.dma_start(
            out=PUVB[ic * rows:(ic + 1) * rows, :].rearrange("r (a b) -> r a b", b=512),
            in_=peer_uv[ic * rows:(ic + 1) * rows, :].rearrange("r (a b) -> r a b", b=512)), w=[r_cast[ic]])
    V("memset", r=[], w=[r_eidx], ap=EIDX[:], constant=0)
    V("memset", r=[], w=[r_gall], ap=GALL[:], constant=0.0)
    CAND = SC2
    QPb = sb(Q6A_END, [128, 16, 512], BF16)
    SCOb = sb(Q6A_END + 16384, [128, 16, 128], F32)
    assert SB_LO + Q6A_END + 16384 + 8192 <= 229376
    QP2, r_qp2 = [QP, QPb], [r_qp, Res()]
    SCO2, r_scb = [SCO, SCOb], [r_sc, Res()]
    TILES = []
    for gi, (t0, n) in enumerate(GROUPS):
        for jt in range(4 if gi < 4 else 1):
            TILES.append((gi, jt, gi * 4 + jt, 128 if gi < 4 else 64))

    def do_qp(gi):
        t0, n = GROUPS[gi]
        for j in range(16):
            b = j % 2
            for k in range(8):
                MM(pb[b][:, 0:n], WPQ[:, k, j * 128:(j + 1) * 128], XN[:, k, t0:t0 + n], k == 0, k == 7, r=[r_wpq, r_xn], w=[rpb[b]])
            ACT(QP2[gi % 2][:, j, 0:n], pb[b][:, 0:n], AF.Copy, r=[rpb[b]], w=[r_qp2[gi % 2]])

    def do_scores(tq):
        gi, jt, ti, nt = TILES[tq]
        c0 = jt * 128
        for j4 in range(4):
            bk = 2 + j4
            for jj in range(4):
                j = j4 * 4 + jj
                MM(pb[bk][0:nt, jj * 128:(jj + 1) * 128], QP2[gi % 2][:, j, c0:c0 + nt], SUBT[:, j, :], True, True, r=[r_qp2[gi % 2], r_subt], w=[rpb[bk]])
            ACT(SCO2[tq % 2][0:nt, j4 * 4:(j4 + 1) * 4, :], pb[bk][0:nt, :].rearrange("p (j k) -> p j k", j=4), AF.Copy, r=[rpb[bk]], w=[r_scb[tq % 2]])

    do_qp(0)
    do_scores(0)
    for tq, (gi, jt, ti, nt) in enumerate(TILES):
        if True:
            SCOx, r_scx = SCO2[tq % 2], r_scb[tq % 2]
            if jt == 0 and gi + 1 < len(GROUPS):
                do_qp(gi + 1)
            if tq + 1 < len(TILES):
                do_scores(tq + 1)
            for j2 in range(0, 16, 2):
                js = (j2, j2 + 1)
                for pj, j in enumerate(js):
                    V("max", r=[r_scx], w=[r_svp[pj]], out=SV[0:nt, j, 0:8], in_=SCOx[0:nt, j, :])
                for pj, j in enumerate(js):
                    V("max_index", r=[r_scx, r_svp[pj]], w=[r_sip[pj]], out=SI[0:nt, j, 0:8], in_max=SV[0:nt, j, 0:8], in_values=SCOx[0:nt, j, :])
                for pj, j in enumerate(js):
                    V("match_replace", r=[r_scx, r_svp[pj]], w=[r_sc2p[pj]], out=SC2[0:nt, pj * 128:(pj + 1) * 128], in_to_replace=SV[0:nt, j, 0:8], in_values=SCOx[0:nt, j, :], imm_value=-1e30)
                for pj, j in enumerate(js):
                    V("max", r=[r_sc2p[pj]], w=[r_svp[pj]], out=SV[0:nt, j, 8:16], in_=SC2[0:nt, pj * 128:(pj + 1) * 128])
                for pj, j in enumerate(js):
                    V("max_index", r=[r_sc2p[pj], r_svp[pj]], w=[r_sip[pj]], out=SI[0:nt, j, 8:16], in_max=SV[0:nt, j, 8:16], in_values=SC2[0:nt, pj * 128:(pj + 1) * 128])
            V("tensor_copy", r=[r_sip[0], r_sip[1], r_si], w=[r_si], out=SIF[0:nt, :, 0:1], in_=SIF[0:nt, :, 0:1])
            V("tensor_copy", r=[r_si], w=[r_si], out=SIF[0:nt], in_=SI[0:nt])
            SV4 = SV[0:nt].rearrange("p (h c) a -> p h c a", c=2)
            SIF4 = SIF[0:nt].rearrange("p (h c) a -> p h c a", c=2)
            V("tensor_tensor", r=[r_sv, r_svp[0], r_svp[1], r_sc2p[0], r_sc2p[1], r_sip[0], r_sip[1]], w=[r_sc2], out=CAND[0:nt, :].rearrange("p (h a b) -> p h a b", h=8, a=16),
              in0=SV4[:, :, 0, :].unsqueeze(3).to_broadcast([nt, 8, 16, 16]), in1=SV4[:, :, 1, :].unsqueeze(2).to_broadcast([nt, 8, 16, 16]), op=ALU.add)
            for h2 in range(0, 8, 2):
                hs = (h2, h2 + 1)
                chs = [CAND[0:nt, h * 256:(h + 1) * 256] for h in hs]
                for ph, h in enumerate(hs):
                    V("max", r=[r_sc2], w=[r_cvp[ph]], out=CVP[0:nt, h, 0:8], in_=chs[ph])
                for ph, h in enumerate(hs):
                    V("max_index", r=[r_sc2, r_cvp[ph]], w=[r_cip[ph]], out=CI[0:nt, h, 0:8], in_max=CVP[0:nt, h, 0:8], in_values=chs[ph])
                for ph, h in enumerate(hs):
                    V("match_replace", r=[r_sc2, r_cvp[ph]], w=[r_eqp[ph]], out=EQ[0:nt, ph * 256:(ph + 1) * 256], in_to_replace=CVP[0:nt, h, 0:8], in_values=chs[ph], imm_value=-1e30)
                for ph, h in enumerate(hs):
                    V("max", r=[r_eqp[ph]], w=[r_cvp[ph]], out=CVP[0:nt, h, 8:16], in_=EQ[0:nt, ph * 256:(ph + 1) * 256])
                for ph, h in enumerate(hs):
                    V("max_index", r=[r_eqp[ph], r_cvp[ph]], w=[r_cip[ph]], out=CI[0:nt, h, 8:16], in_max=CVP[0:nt, h, 8:16], in_values=EQ[0:nt, ph * 256:(ph + 1) * 256])
            V("tensor_copy", r=[r_cip[0], r_cip[1], r_cvp[0], r_cvp[1], r_eqp[0], r_eqp[1], r_ci, r_cv, r_eq], w=[r_ci, r_cv, r_eq], out=CIF[0:nt, :, 0:1], in_=CIF[0:nt, :, 0:1])
            V("tensor_copy", r=[r_ci], w=[r_ci], out=CIF[0:nt], in_=CI[0:nt])
            EQ4 = EQ[0:nt, :].rearrange("p (h k a) -> p h k a", h=8, k=16)
            io16 = IOTA16[0:nt, :].unsqueeze(1).unsqueeze(1).to_broadcast([nt, 8, 16, 16])
            io256 = IOTA256[0:nt, :].unsqueeze(1).unsqueeze(1).to_broadcast([nt, 8, 16, 16])
            V("tensor_tensor", r=[r_ci, r_cst], w=[r_eq], out=EQ4, in0=CIF[0:nt].unsqueeze(3).to_broadcast([nt, 8, 16, 16]), in1=io256, op=ALU.is_ge)
            V("tensor_reduce", r=[r_eq], w=[r_ab], out=AF_[0:nt], in_=EQ4, axis=AX.X, op=ALU.add)
            V("tensor_scalar", r=[r_ab], w=[r_ab], out=AF_[0:nt], in0=AF_[0:nt], scalar1=-1.0, scalar2=None, op0=ALU.add)
            V("scalar_tensor_tensor", r=[r_ab, r_ci], w=[r_ab], out=BF_[0:nt].rearrange("p h k -> p (h k)"), in0=AF_[0:nt].rearrange("p h k -> p (h k)"), scalar=-16.0,
              in1=CIF[0:nt].rearrange("p h k -> p (h k)"), op0=ALU.mult, op1=ALU.add)
            for (sel, src, c_) in ((ISEL, AF_, 0), (JSEL, BF_, 1)):
                V("tensor_tensor", r=[r_ab, r_cst], w=[r_eq], out=EQ4, in0=src[0:nt].unsqueeze(3).to_broadcast([nt, 8, 16, 16]), in1=io16, op=ALU.is_equal)
                V("tensor_tensor", r=[r_eq, r_si], w=[r_eq], out=EQ4, in0=EQ4, in1=SIF4[:, :, c_, :].unsqueeze(2).to_broadcast([nt, 8, 16, 16]), op=ALU.mult)
                V("tensor_reduce", r=[r_eq], w=[r_ij], out=sel[0:nt], in_=EQ4, axis=AX.X, op=ALU.add)
            V("scalar_tensor_tensor", r=[r_ij], w=[r_ij], out=ISEL[0:nt].rearrange("p h k -> p (h k)"), in0=ISEL[0:nt].rearrange("p h k -> p (h k)"), scalar=128.0,
              in1=JSEL[0:nt].rearrange("p h k -> p (h k)"), op0=ALU.mult, op1=ALU.add)
            V("tensor_copy", r=[r_ij], w=[r_eidx], out=EIDX[0:nt, ti, :], in_=ISEL[0:nt].rearrange("p h k -> p (h k)"))
            V("tensor_copy", r=[r_cv], w=[r_zz], out=ZZ[0:nt], in_=CVP[0:nt, :, 0])
            V("tensor_tensor", r=[r_cv, r_zz], w=[r_cv], out=CVP[0:nt], in0=CVP[0:nt], in1=ZZ[0:nt].unsqueeze(2).to_broadcast([nt, 8, 16]), op=ALU.subtract)
            ACT(CVP[0:nt], CVP[0:nt], AF.Exp, r=[r_cv], w=[r_cv])
            V("tensor_reduce", r=[r_cv], w=[r_zz], out=ZZ[0:nt], in_=CVP[0:nt], axis=AX.X, op=ALU.add)
            V("reciprocal", r=[r_zz], w=[r_zz], out=ZZ[0:nt], in_=ZZ[0:nt])
            V("tensor_tensor", r=[r_cv, r_zz], w=[r_gall], out=GALL[0:nt, ti, :].rearrange("p (h k) -> p h k", h=8), in0=CVP[0:nt],
              in1=ZZ[0:nt].unsqueeze(2).to_broadcast([nt, 8, 16]), op=ALU.mult)
    P.barrier()
    if PHASES <= 6.5:
        P.emit(nc)
        return nc
    q = Q6B
    NGB = 6
    GBUF = [sb(MIX_OFF, [128, 4, 2048], BF16), sb(MIX_OFF + 16384, [128, 4, 2048], BF16), sb(WA_OFF, [128, 4, 2048], BF16)]
    for _ in range(NGB - 3):
        GBUF.append(sb(q, [128, 4, 2048], BF16)); q += 16384
    XTOK = [sb(q, [128, 1024], F32), sb(q + 4096, [128, 1024], F32)]; q += 8192
    JUNK = sb(q, [128, 1024], BF16); q += 2048
    X1T = sb(q, [128, 8, 128], F32); q += 4096
    OUTS = sb(q, [128, 1024], F32); q += 4096
    ACTV = sb(STG_OFF, [128, 128], F32)
    GLU = sb(STG_OFF + 512, [128, 128], F32)
    GW = sb(STG_OFF + 1024, [128, 128], F32)
    DG = [sb(STG_OFF + 1536 + i * 256, [128, 128], BF16) for i in range(8)]
    assert SB_LO + q <= 229376, q
    r_gbuf = [[Res() for _ in range(4)] for _ in range(NGB)]
    r_av = [Res() for _ in range(NGB)]
    r_gl = [Res() for _ in range(NGB)]
    r_gw = [Res() for _ in range(NGB)]
    r_dg = [Res() for _ in range(8)]
    r_xtok = [Res(), Res()]
    r_x1t, r_outs = Res(), Res()
    V("memset", r=[], w=[r_xtok[0]], ap=XTOK[0][:], constant=0.0)
    V("memset", r=[], w=[r_xtok[1]], ap=XTOK[1][:], constant=0.0)
    NG = 32
    gcount = [0]
    dgc = [0]

    def make_xtok(ti):
        nt = 128 if ti < 16 else 64
        tk0 = ti * 128
        xt, rxt = XTOK[ti % 2], r_xtok[ti % 2]
        for k in range(8):
            bk = k // 4
            MM(pb[bk][0:nt, (k % 4) * 128:(k % 4 + 1) * 128], XN[:, k, tk0:tk0 + nt], EYEB[:, :], True, True, r=[r_xn, r_eyeb], w=[rpb[bk]])
        for bk in range(2):
            ACT(xt[0:nt, bk * 512:(bk + 1) * 512], pb[bk][0:nt, :], AF.Copy, r=[rpb[bk]], w=[rxt])

    def epilogue(ti):
        nt = 128 if ti < 16 else 64
        tk0 = ti * 128
        DMA(X1T[:, :, 0:nt], X1d[:, :, tk0:tk0 + nt], r=[r_x1d], w=[r_x1t])
        for k in range(8):
            bk = 4 + k // 4
            MM(pb[bk][:, (k % 4) * 128:(k % 4) * 128 + nt], OUTS[0:nt, k * 128:(k + 1) * 128], EYE[0:nt, 0:nt], True, True, r=[r_outs, r_cst], w=[rpb[bk]])
        for bk in range(2):
            V("tensor_tensor", r=[r_x1t, rpb[4 + bk]], w=[r_x1t], out=X1T[:, bk * 4:(bk + 1) * 4, 0:nt], in0=X1T[:, bk * 4:(bk + 1) * 4, 0:nt],
              in1=pb[4 + bk][:, :].rearrange("p (k t) -> p k t", k=4)[:, :, 0:nt], op=ALU.add)
        YTv = OUTS[:, :].rearrange("p (k t) -> p k t", k=8)
        rmsnorm(lambda k: X1T[:, k, 0:nt], r_x1t, lambda k: YTv[:, k, 0:nt], r_outs, 24, nt, bank=6)
        DMA(o_yT[:, :, tk0:tk0 + nt], YTv[:, :, 0:nt], r=[r_outs])

    pending = None
    make_xtok(0)
    for ti in range(17):
        xt, rxt = XTOK[ti % 2], r_xtok[ti % 2]
        gbase = gcount[0]
        for g in range(NG + 1):
            if g < NG:
                bi = (gbase + g) % NGB
                e0 = 4 * g
                for i in range(4):
                    P.dma("pool", lambda e, bi=bi, i=i, e0=e0, ti=ti: e.indirect_dma_start(
                        out=GBUF[bi][:, i, :], out_offset=None, in_=PUVB,
                        in_offset=bass.IndirectOffsetOnAxis(ap=EIDX[:, ti, e0 + i:e0 + i + 1], axis=0)), r=[r_eidx] + r_cast, w=[r_gbuf[bi][i]])
                for i in range(4):
                    V("scalar_tensor_tensor", r=[r_gbuf[bi][i], rxt], w=[r_av[bi]], out=JUNK[:, :], in0=GBUF[bi][:, i, 0:1024], scalar=1.0,
                      in1=xt[:, :], op0=ALU.mult, op1=ALU.mult, accum_out=ACTV[:, e0 + i:e0 + i + 1])
                ACT(GLU[:, e0:e0 + 4], ACTV[:, e0:e0 + 4], AF.Gelu, r=[r_av[bi]], w=[r_gl[bi]])
            if g >= 1:
                gg = g - 1
                bi = (gbase + gg) % NGB
                e0 = 4 * gg
                V("tensor_tensor", r=[r_gl[bi], r_gall], w=[r_gw[bi]], out=GW[:, e0:e0 + 4], in0=GLU[:, e0:e0 + 4], in1=GALL[:, ti, e0:e0 + 4], op=ALU.mult)
                for i in range(4):
                    e_ = e0 + i
                    di = dgc[0] % 8
                    dgc[0] += 1
                    ACT(DG[di][:, :], EYEB[:, :], AF.Copy, r=[r_eyeb, r_gw[bi]], w=[r_dg[di]], scale=GW[:, e_:e_ + 1])
                    for hf in range(2):
                        MM(pb[2 + hf][:, :], DG[di][:, :], GBUF[bi][:, i, 1024 + hf * 512:1024 + (hf + 1) * 512], e_ == 0, e_ == 127,
                           r=[r_dg[di], r_gbuf[bi][i]], w=[rpb[2 + hf]])
            if g == 6 and pending is not None:
                epilogue(pending)
                pending = None
            if g == 20 and ti + 1 < 17:
                make_xtok(ti + 1)
        gcount[0] += NG
        for hf in range(2):
            ACT(OUTS[:, hf * 512:(hf + 1) * 512], pb[2 + hf][:, :], AF.Copy, r=[rpb[2 + hf]], w=[r_outs])
        pending = ti
    epilogue(pending)
    P.emit(nc)
    return nc


def _consts():
    c = np.zeros((128, NCST), np.float32)
    c[:, 0:128] = np.eye(128, dtype=np.float32)
    k = np.arange(128)[:, None]
    qq = np.arange(128)[None, :]
    c[:, 128:256] = 1.0 - ((qq < 64) & (k >= 64))
    c[:, 256:384] = 1.0 - ((qq >= 64) & (k < 64))
    s = np.arange(64)[:, None]
    t = np.arange(64)[None, :]
    m64 = np.where(s <= t, 0.0, -30000.0).astype(np.float32)
    c[0:64, 384:640] = np.tile(m64, (1, 4))
    s = np.arange(32)[:, None]
    t = np.arange(32)[None, :]
    m32 = np.where(s <= t, 0.0, -30000.0).astype(np.float32)
    c[0:32, 640:768] = np.tile(m32, (1, 4))
    for h in range(4):
        c[h, 768 + h * 64:768 + (h + 1) * 64] = 1.0
        c[h, 1024 + h * 32:1024 + (h + 1) * 32] = 1.0
    c[:, 1152:1168] = np.arange(16, dtype=np.float32)[None, :]
    c[:, 1168:1184] = 16.0 * np.arange(16, dtype=np.float32)[None, :]
    return c


def _fm(a):
    return np.ascontiguousarray(a.reshape(a.shape[0], 8, 128).transpose(2, 1, 0))


def _fm_inv(a):
    return np.ascontiguousarray(a.transpose(2, 1, 0).reshape(a.shape[2], 1024))


_NC_CACHE = {}


def kernel(x_prompt, x_sample, mem_prompt, cache_a_k, cache_a_v, state_b_conv, state_b_C, state_b_n,
           state_b_m, cache_mem_k, cache_mem_v, g_mix, w_in, conv_w, conv_b, b_if, g_head, rel_bias,
           w_a_up, w_b_up, w_out, g_mem, w_mk, w_mv, g_cross, w_cq, w_co, g_ffn, w_pq, sub_keys,
           peer_u, peer_v, g_final):
    f = lambda a: np.asarray(a, dtype=np.float32)
    x_prompt, x_sample, mem_prompt = f(x_prompt), f(x_sample), f(mem_prompt)
    prm = np.zeros((128, 80), np.float32)
    for i, g in enumerate([g_mix[0], g_cross[0], g_ffn[0], g_final, g_mem[0], conv_b[0]]):
        prm[:, i * 8:(i + 1) * 8] = f(g).reshape(8, 128).T
    cw = f(conv_w[0])
    prm[:, 48:80] = cw.reshape(4, 8, 128).transpose(2, 1, 0).reshape(128, 32)
    gb4 = np.ascontiguousarray(f(b_if[0]).reshape(2, 4).T)
    shared = dict(
        w_in=f(w_in[0]), w_a_up=f(w_a_up[0]), w_b_up=f(w_b_up[0]), w_out=f(w_out[0]), w_mk=f(w_mk[0]),
        w_mv=f(w_mv[0]), w_cq=f(w_cq[0]), w_co=f(w_co[0]), w_pq=f(w_pq[0]),
        subT=np.ascontiguousarray(f(sub_keys[0]).reshape(16, 128, 128).transpose(2, 0, 1)),
        prm=prm, gb4=gb4, ghead=f(g_head[0]).reshape(1, 512), relb=f(rel_bias[0]), cst=_consts(),
    )
    if PHASES >= 6:
        shared['peer_uv'] = np.ascontiguousarray(np.concatenate([f(peer_u[0]), f(peer_v[0])], axis=1))
    in_maps = []
    for c in range(NCORES):
        ss = [2 * c, 2 * c + 1]
        X = np.concatenate([x_prompt[c], x_sample[ss[0]], x_sample[ss[1]]], axis=0)
        ck = f(cache_a_k[0])[ss]
        ckT_ = np.ascontiguousarray(ck.reshape(2, 512, 4, 128).transpose(0, 3, 2, 1))
        cmk = f(cache_mem_k[0])[ss]
        cmkT_ = np.ascontiguousarray(cmk.reshape(2, 256, 8, 128).transpose(0, 3, 2, 1))
        conv = f(state_b_conv[0])[ss]
        convT_ = np.ascontiguousarray(conv.reshape(2, 3, 8, 128).transpose(3, 2, 0, 1))
        d = dict(shared)
        d.update(
            xT=_fm(X), memT=_fm(mem_prompt[c]), ckT=ckT_,
            cav=np.ascontiguousarray(f(cache_a_v[0])[ss].reshape(2, 512, 512)),
            convT=convT_, C0=np.ascontiguousarray(f(state_b_C[0])[ss]),
            n0T=np.ascontiguousarray(f(state_b_n[0])[ss].transpose(2, 0, 1)),
            m0T=np.ascontiguousarray(f(state_b_m[0])[ss].reshape(2, 4, 1)),
            cmkT=cmkT_, cmv=np.ascontiguousarray(f(cache_mem_v[0])[ss].reshape(2, 256, 1024)),
        )
        in_maps.append(d)
    if "nc" not in _NC_CACHE:
        _NC_CACHE["nc"] = build_program()
    nc = _NC_CACHE["nc"]
    res = run_bass_kernel_spmd(nc, in_maps, core_ids=list(range(NCORES)))
    R = res.results
    y_prompt = np.zeros((8, 2048, 1024), np.float32)
    y_sample = np.zeros((16, 32, 1024), np.float32)
    p_a_k = np.zeros((1, 8, 512, 8, 64), np.float32)
    p_a_v = np.zeros((1, 8, 512, 8, 64), np.float32)
    p_b_conv = np.zeros((1, 8, 3, 1024), np.float32)
    p_b_C = np.zeros((1, 8, 4, 128, 128), np.float32)
    p_b_n = np.zeros((1, 8, 4, 128), np.float32)
    p_b_m = np.zeros((1, 8, 4), np.float32)
    p_mem_k = np.zeros((1, 8, 256, 4, 256), np.float32)
    p_mem_v = np.zeros((1, 8, 256, 4, 256), np.float32)
    s_a_k = np.zeros((1, 16, 32, 8, 64), np.float32)
    s_a_v = np.zeros((1, 16, 32, 8, 64), np.float32)
    s_b_conv = np.zeros((1, 16, 3, 1024), np.float32)
    s_b_C = np.zeros((1, 16, 4, 128, 128), np.float32)
    s_b_n = np.zeros((1, 16, 4, 128), np.float32)
    s_b_m = np.zeros((1, 16, 4), np.float32)
    for c in range(NCORES):
        r = R[c]
        y = _fm_inv(r["yT"])
        y_prompt[c] = y[:2048]
        y_sample[2 * c] = y[2048:2080]
        y_sample[2 * c + 1] = y[2080:2112]
        p_a_k[0, c] = r["pakT"].transpose(2, 1, 0).reshape(512, 8, 64)
        p_a_v[0, c] = r["pav"].reshape(512, 8, 64)
        p_b_conv[0, c] = r["pconvT"].transpose(2, 1, 0).reshape(3, 1024)
        p_b_C[0, c] = r["pC"].transpose(1, 0, 2)
        p_b_n[0, c] = r["pn"].T
        p_b_m[0, c] = r["pm"][:, 0]
        p_mem_k[0, c] = r["pmkT"].transpose(2, 1, 0).reshape(256, 4, 256)
        p_mem_v[0, c] = r["pmv"].reshape(256, 4, 256)
        sak = r["sakT"].transpose(2, 1, 0).reshape(64, 8, 64)
        sav = r["sav"].reshape(64, 8, 64)
        for s in range(2):
            s_a_k[0, 2 * c + s] = sak[s * 32:(s + 1) * 32]
            s_a_v[0, 2 * c + s] = sav[s * 32:(s + 1) * 32]
            s_b_conv[0, 2 * c + s] = r["sconvT"][:, :, s, :].transpose(2, 1, 0).reshape(3, 1024)
            s_b_C[0, 2 * c + s] = r["sC"][:, s].transpose(1, 0, 2)
            s_b_n[0, 2 * c + s] = r["sn"][:, s, :].T
            s_b_m[0, 2 * c + s] = r["sm"][s, :, 0]
    return (y_prompt, y_sample, p_a_k, p_a_v, p_b_conv, p_b_C, p_b_n, p_b_m, p_mem_k, p_mem_v,
            s_a_k, s_a_v, s_b_conv, s_b_C, s_b_n, s_b_m)
```

```python
import numpy as np
from contextlib import ExitStack
import concourse.bass as bass
import concourse.mybir as mybir
from concourse.bass_utils import run_bass_kernel_spmd

F32 = mybir.dt.float32
BF16 = mybir.dt.bfloat16
U32 = mybir.dt.uint32
I32 = mybir.dt.int32
AF = mybir.ActivationFunctionType
ALU = mybir.AluOpType
AX = mybir.AxisListType

T = 2112
TP = 2048
EPS = 1e-6
LTOE = 640
NCST = 1184
PHASES = 9
NCORES = 8
DBG = ''


class Res:
    __slots__ = ("name", "w", "rs", "excl")

    def __init__(self, name="", excl=False):
        self.name = name
        self.w = None
        self.rs = []
        self.excl = excl


class _Op:
    __slots__ = ("eng", "fn", "deps", "dma", "sig", "dsem", "dval")


class Prog:
    STREAMS = ("pe", "act", "dve", "pool", "sp")
    NS = 12

    def __init__(self):
        self.ops = []
        self.last = {s: None for s in self.STREAMS}
        self.dmas = []
        self.pend = {s: set() for s in self.STREAMS}

    def op(self, eng, fn, r=(), w=(), dma=False):
        i = len(self.ops)
        deps = set(self.pend[eng])
        self.pend[eng] = set()
        xr = [res for res in r if res.excl]
        if xr:
            r = [res for res in r if not res.excl]
            w = list(w) + [res for res in xr if res not in w]
        for res in r:
            if res.w is not None:
                deps.add(res.w)
        for res in w:
            if res.w is not None:
                deps.add(res.w)
            deps.update(res.rs)
        o = _Op()
        o.eng, o.fn, o.deps, o.dma, o.sig, o.dsem, o.dval = eng, fn, deps, dma, None, None, None
        self.ops.append(o)
        for res in r:
            res.rs.append(i)
        for res in w:
            res.w = i
            res.rs = []
        self.last[eng] = i
        if dma:
            self.dmas.append(i)
        return i

    def dma(self, eng, fn, r=(), w=()):
        return self.op(eng, fn, r, w, dma=True)

    def barrier(self):
        deps = set(self.dmas)
        for s in self.STREAMS:
            if self.last[s] is not None:
                deps.add(self.last[s])
        self.dmas = []
        for s in self.STREAMS:
            self.pend[s] |= deps

    def emit(self, nc):
        ops = self.ops
        for o in ops:
            if o.eng == "pe" and not o.dma:
                o.deps = {d for d in o.deps if not (ops[d].eng == "pe" and not ops[d].dma)}
        needed = set()
        for o in ops:
            needed.update(o.deps)
        with ExitStack() as st:
            esem = {s: st.enter_context(nc.semaphore("e_" + s)) for s in self.STREAMS}
            dsem = {s: [st.enter_context(nc.semaphore("d_%s%d" % (s, k))) for k in range(self.NS)]
                    for s in ("act", "pool", "sp")}
            cnt = {s: 0 for s in self.STREAMS}
            dcnt = {s: 0 for s in self.STREAMS}
            per = {s: [] for s in self.STREAMS}
            for i, o in enumerate(ops):
                per[o.eng].append(i)
                if o.dma:
                    k = dcnt[o.eng]
                    dcnt[o.eng] += 1
                    o.dsem = dsem[o.eng][k % self.NS]
                    o.dval = 16 * (k // self.NS + 1)
                elif i in needed:
                    cnt[o.eng] += 1
                    o.sig = cnt[o.eng]
            engobj = {"pe": "tensor", "act": "scalar", "dve": "vector", "pool": "gpsimd", "sp": "sync"}

            def run_stream(s, e):
                known = {}
                final = {}
                for i in per[s]:
                    o = ops[i]
                    waits = {}
                    for d in o.deps:
                        od = ops[d]
                        if od.dma:
                            sem, val = od.dsem, od.dval
                        else:
                            sem, val = esem[od.eng], od.sig
                        if waits.get(sem, 0) < val:
                            waits[sem] = val
                    if o.dma and o.dval > 16:
                        if waits.get(o.dsem, 0) < o.dval - 16:
                            waits[o.dsem] = o.dval - 16
                    for sem, val in waits.items():
                        if known.get(sem, 0) < val:
                            e.wait_ge(sem, val)
                            known[sem] = val
                    ins = o.fn(e)
                    if o.dma:
                        ins.then_inc(o.dsem, 16)
                        final[o.dsem] = o.dval
                    elif o.sig is not None:
                        ins.then_inc(esem[s], 1)
                for sem, val in final.items():
                    if known.get(sem, 0) < val:
                        e.wait_ge(sem, val)

            with nc.Block() as block:
                for s in self.STREAMS:
                    if not per[s]:
                        continue
                    getattr(block, engobj[s])(lambda e, s=s: run_stream(s, e))
        return cnt, dcnt


def build_program():
    nc = bass.Bass("TRN2", target_bir_lowering=False)
    P = Prog()
    SB_LO = 20480
    ncnt = [0]

    def din(name, shape, dt=F32):
        return nc.dram_tensor(name, list(shape), dt, kind="ExternalInput").ap()

    def dout(name, shape, dt=F32):
        return nc.dram_tensor(name, list(shape), dt, kind="ExternalOutput").ap()

    def sb(off, shape, dt):
        ncnt[0] += 1
        return nc.alloc_sbuf_tensor_at("t%d" % ncnt[0], list(shape), dt, offset=SB_LO + off)

    xT = din("xT", [128, 8, T])
    memT = din("memT", [128, 8, 256])
    ckT = din("ckT", [2, 128, 4, 512])
    cav = din("cav", [2, 512, 512])
    convT = din("convT", [128, 8, 2, 3])
    C0d = din("C0", [2, 4, 128, 128])
    n0T = din("n0T", [128, 2, 4])
    m0T = din("m0T", [2, 4, 1])
    cmkT = din("cmkT", [2, 128, 8, 256])
    cmv = din("cmv", [2, 256, 1024])
    w_in = din("w_in", [1024, 5640])
    w_a_up = din("w_a_up", [512, 1024])
    w_b_up = din("w_b_up", [512, 1024])
    w_out = din("w_out", [1024, 1024])
    w_mk = din("w_mk", [1024, 1024])
    w_mv = din("w_mv", [1024, 1024])
    w_cq = din("w_cq", [1024, 1024])
    w_co = din("w_co", [1024, 1024])
    w_pq = din("w_pq", [1024, 2048])
    subT = din("subT", [128, 16, 128])
    peer_uv = din("peer_uv", [16384, 2048]) if PHASES >= 6 else None
    prm = din("prm", [128, 80])
    gb4 = din("gb4", [4, 2])
    ghead = din("ghead", [1, 512])
    relb = din("relb", [8, 257])
    cst = din("cst", [128, NCST])

    o_yT = dout("yT", [128, 8, T])
    o_pakT = dout("pakT", [128, 4, 512])
    o_pav = dout("pav", [512, 512])
    o_pconvT = dout("pconvT", [128, 8, 3])
    o_pC = dout("pC", [128, 4, 128])
    o_pn = dout("pn", [128, 4])
    o_pm = dout("pm", [4, 1])
    o_pmkT = dout("pmkT", [128, 8, 256])
    o_pmv = dout("pmv", [256, 1024])
    o_sakT = dout("sakT", [128, 4, 64])
    o_sav = dout("sav", [64, 512])
    o_sconvT = dout("sconvT", [128, 8, 2, 3])
    o_sC = dout("sC", [128, 2, 4, 128])
    o_sn = dout("sn", [128, 2, 4])
    o_sm = dout("sm", [2, 4, 1])
    Rtoe = nc.dram_tensor("Rtoe", [8, 128, LTOE], F32, kind="Internal").ap()
    r_Rtoe = Res()

    def MM(out, lhsT, rhs, st, sp, r, w, sgc=False, tp=None):
        if tp is None:
            P.op("pe", lambda e: e.matmul(out, lhsT=lhsT, rhs=rhs, start=st, stop=sp, skip_group_check=sgc), r=r, w=w)
        else:
            P.op("pe", lambda e: e.matmul(out, lhsT=lhsT, rhs=rhs, start=st, stop=sp, skip_group_check=sgc, tile_position=tp), r=r, w=w)

    def ACT(out, in_, func, r, w, **kw):
        P.op("act", lambda e: e.activation(out=out, in_=in_, func=func, **kw), r=r, w=w)

    def V(name, r, w, eng="dve", **kw):
        P.op(eng, lambda e: getattr(e, name)(**kw), r=r, w=w)

    def DMA(out, in_, r=(), w=(), eng="sp"):
        P.dma(eng, lambda e: e.dma_start(out=out, in_=in_), r=r, w=w)

    pb = [nc.alloc_psum_tensor("pb%d" % i, [128, 512], F32) for i in range(8)]
    rpb = [Res("pb%d" % i, excl=True) for i in range(8)]

    o = 0
    CST = sb(o, [128, NCST], F32); o += NCST * 4
    PRM = sb(o, [128, 80], F32); o += 320
    EYEB = sb(o, [128, 128], BF16); o += 256
    ONESB = sb(o, [128, 128], BF16); o += 256
    EB0 = sb(o, [128, 8, 128], BF16); o += 2048
    EB1 = sb(o, [128, 8, 128], BF16); o += 2048
    EBC = sb(o, [128, 8, 128], BF16); o += 2048
    EB4 = sb(o, [128, 8, 128], BF16); o += 2048
    GHD = sb(o, [64, 512], F32); o += 2048
    GB4 = sb(o, [4, 2], F32); o += 32
    MKT = sb(o, [128, 8, 256], BF16); o += 4096
    MV = sb(o, [128, 2, 1024], BF16); o += 4096
    SQ_OFF = o
    SQ = sb(o, [128, 8, 512], BF16); o += 8192
    RS = sb(o, [128, 512], F32); o += 2048
    STG_OFF = o
    STG = [sb(o, [128, 512], F32), sb(o + 2048, [128, 512], F32)]; o += 4096
    WA_OFF = o
    WA = sb(o, [128, 8, 512], BF16); o += 8192
    WB = sb(o, [128, 8, 512], BF16); o += 8192
    XN = sb(o, [128, 8, T], BF16); o += 8 * T * 2
    MIX_OFF = o
    MIX = sb(o, [128, 8, T], BF16); o += 8 * T * 2
    PH = o
    r_cst, r_prm, r_eyeb, r_ones = Res(), Res(), Res(), Res()
    r_eb = Res()
    r_ghd, r_gb4, r_mkt, r_mv, r_sq, r_rs = Res(), Res(), Res(), Res(), Res(), Res()
    r_stg = [Res(), Res()]
    r_wa, r_wb, r_xn, r_mix = Res(), Res(), Res(), Res()
    stg_i = [0]

    EYE = CST[:, 0:128]
    MASK0 = CST[:, 128:256]
    MASK4 = CST[:, 256:384]
    MNEG64 = CST[0:64, 384:640]
    MNEG32 = CST[0:32, 640:768]
    BD64 = CST[0:4, 768:1024]
    BD32 = CST[0:4, 1024:1152]
    IOTA16 = CST[:, 1152:1168]
    IOTA256 = CST[:, 1168:1184]

    def out_stage(src_psum, rsrc, dram_ap, rows=128, cols=512):
        i = stg_i[0] % 2
        stg_i[0] += 1
        V("tensor_copy", r=[rsrc], w=[r_stg[i]], out=STG[i][0:rows, 0:cols], in_=src_psum)
        DMA(dram_ap, STG[i][0:rows, 0:cols], r=[r_stg[i]])

    def load_w(dst, rdst, src2d, c0, ncols, kchunks=8, part=128):
        src = src2d[:, c0:c0 + ncols].rearrange("(k p) n -> p k n", p=part)
        P.dma("pool", lambda e: e.dma_start(out=dst, in_=src), w=[rdst])

    DMA(CST[:], cst, w=[r_cst])
    DMA(PRM[:], prm, w=[r_prm])
    DMA(GB4[:], gb4, w=[r_gb4])
    DMA(GHD[:], ghead.to_broadcast([64, 512]), w=[r_ghd])
    V("tensor_copy", r=[r_cst], w=[r_eyeb], out=EYEB[:], in_=EYE)
    V("memset", r=[], w=[r_ones], ap=ONESB[:], constant=1.0)
    TB = sb(PH, [8, LTOE], F32); r_tb = Res()
    EBS = sb(PH + LTOE * 4, [128, 3, 8, 128], F32); r_ebs = Res()
    DMA(TB[:, 0:257], relb, w=[r_tb])
    V("tensor_copy", r=[r_tb], w=[r_tb], out=TB[:, 257:LTOE], in_=TB[:, 256:257].to_broadcast([8, LTOE - 257]))
    DMA(Rtoe, TB[:].unsqueeze(1).to_broadcast([8, 128, LTOE]), r=[r_tb], w=[r_Rtoe])
    for d in range(3):
        off = 128 + d * 128
        DMA(EBS[:, d, :, :], bass.AP(Rtoe.tensor, off, [[LTOE - 1, 128], [128 * LTOE, 8], [1, 128]]), r=[r_Rtoe], w=[r_ebs])
    ACT(EB0[:], EBS[:, 0, :, :], AF.Exp, r=[r_ebs], w=[r_eb])
    ACT(EB1[:], EBS[:, 1, :, :], AF.Exp, r=[r_ebs], w=[r_eb])
    ACT(EBC[:], EBS[:, 2, :, :], AF.Exp, r=[r_ebs], w=[r_eb])
    V("tensor_tensor", r=[r_eb, r_cst], w=[r_eb], out=EB4[:], in0=EBC[:], in1=MASK4.unsqueeze(1).to_broadcast([128, 8, 128]), op=ALU.mult)
    V("tensor_tensor", r=[r_eb, r_cst], w=[r_eb], out=EB0[:], in0=EB0[:], in1=MASK0.unsqueeze(1).to_broadcast([128, 8, 128]), op=ALU.mult)
    P.barrier()
    if PHASES <= 0:
        P.emit(nc)
        return nc
    def rmsnorm(src, rsrc, dst, rdst, gcol, n, bank=7):
        for k in range(8):
            ACT(SQ[:, k, 0:n], src(k), AF.Square, r=[rsrc], w=[r_sq])
        for k in range(8):
            MM(pb[bank][:, 0:n], ONESB[:], SQ[:, k, 0:n], k == 0, k == 7, r=[r_sq, r_ones], w=[rpb[bank]])
        ACT(RS[:, 0:n], pb[bank][:, 0:n], AF.Ln, r=[rpb[bank]], w=[r_rs], scale=1.0 / 1024, bias=EPS)
        ACT(RS[:, 0:n], RS[:, 0:n], AF.Exp, r=[r_rs], w=[r_rs], scale=-0.5)
        for k in range(8):
            V("scalar_tensor_tensor", r=[rsrc, r_rs, r_prm], w=[rdst], out=dst(k), in0=src(k),
              scalar=PRM[:, gcol + k:gcol + k + 1], in1=RS[:, 0:n], op0=ALU.mult, op1=ALU.mult)

    GROUPS = [(0, 512), (512, 512), (1024, 512), (1536, 512), (2048, 64)]

    MEMF = sb(PH, [128, 8, 256], F32); r_memf = Res()
    MN = sb(PH + 8192, [128, 8, 256], BF16); r_mn = Res()
    DMA(MEMF[:], memT, w=[r_memf])
    rmsnorm(lambda k: MEMF[:, k, :], r_memf, lambda k: MN[:, k, :], r_mn, 32, 256)
    if PHASES <= 0.5:
        P.emit(nc)
        return nc
    load_w(WA[:], r_wa, w_mk, 0, 512)
    load_w(WB[:], r_wb, w_mk, 512, 512)
    for fc in range(8):
        W, rW = (WA, r_wa) if fc < 4 else (WB, r_wb)
        c = (fc % 4) * 128
        b = fc % 2
        for k in range(8):
            MM(pb[b][:, 0:256], W[:, k, c:c + 128], MN[:, k, :], k == 0, k == 7, r=[rW, r_mn], w=[rpb[b]])
        if 'a' not in DBG:
            ACT(MKT[:, fc, :], pb[b][:, 0:256], AF.Copy, r=[rpb[b]], w=[r_mkt])
        if 'b' not in DBG:
            out_stage(pb[b][:, 0:256], rpb[b], o_pmkT[:, fc, :], cols=256)
    if PHASES <= 0.7:
        P.emit(nc)
        return nc
    load_w(WA[:], r_wa, w_mv, 0, 512)
    load_w(WB[:], r_wb, w_mv, 512, 512)
    for mt in range(2):
        for half in range(2):
            W, rW = (WA, r_wa) if half == 0 else (WB, r_wb)
            b = half
            for k in range(8):
                MM(pb[b][:, :], MN[:, k, mt * 128:(mt + 1) * 128], W[:, k, :], k == 0, k == 7, r=[rW, r_mn], w=[rpb[b]])
            ACT(MV[:, mt, half * 512:(half + 1) * 512], pb[b][:, :], AF.Copy, r=[rpb[b]], w=[r_mv])
            out_stage(pb[b][:, :], rpb[b], o_pmv[mt * 128:(mt + 1) * 128, half * 512:(half + 1) * 512])
    P.barrier()
    if PHASES <= 1:
        P.emit(nc)
        return nc

    q = PH
    KA = sb(q, [128, 4, T], BF16); q += 4 * T * 2
    VA = sb(q, [128, 18, 512], BF16); q += 18 * 1024
    QZ = [sb(q, [128, 4, 512], BF16), sb(q + 4096, [128, 4, 512], BF16)]; q += 8192
    XG = sb(q, [128, 8, 512], F32); r_xg = Res()
    WAU = sb(q, [64, 8, 1024], BF16); q += 16384
    OA = sb(q, [64, 8, 512], BF16); q += 8192
    EE = [sb(q, [128, 512], BF16), sb(q + 1024, [128, 512], BF16)]; q += 2048
    RD = sb(q, [64, 512], F32); q += 2048
    r_rd_a = Res()
    CK = sb(q, [128, 4, 512], BF16); q += 4096
    CV = sb(q, [128, 4, 512], BF16); q += 4096
    RD2 = [RD, sb(q, [64, 512], F32)]; q += 2048
    r_rd2 = [r_rd_a, Res()]
    assert SB_LO + q <= 229376, q
    r_ka, r_va, r_qa, r_wau, r_oa, r_rd, r_ck, r_cv = Res(), Res(), Res(), Res(), Res(), Res(), Res(), Res()
    r_ee = [Res(), Res()]

    for (t0, n) in GROUPS:
        DMA(XG[:, :, 0:n], xT[:, :, t0:t0 + n], w=[r_xg])
        rmsnorm(lambda k: XG[:, k, 0:n], r_xg, lambda k: XN[:, k, t0:t0 + n], r_xn, 0, n)
    P.barrier()
    if PHASES <= 1.3:
        P.emit(nc)
        return nc
    load_w(WA[:], r_wa, w_in, 512, 512)
    load_w(WB[:], r_wb, w_in, 1024, 512)
    for gi, (t0, n) in enumerate(GROUPS):
        for pr in range(4):
            b = pr % 2
            for k in range(8):
                MM(pb[b][:, 0:n], WA[:, k, pr * 128:(pr + 1) * 128], XN[:, k, t0:t0 + n], k == 0, k == 7, r=[r_wa, r_xn], w=[rpb[b]])
            ACT(KA[:, pr, t0:t0 + n], pb[b][:, 0:n], AF.Copy, r=[rpb[b]], w=[r_ka])
            if gi == 3:
                out_stage(pb[b][:, 0:n], rpb[b], o_pakT[:, pr, :])
            if gi == 4:
                out_stage(pb[b][:, 0:n], rpb[b], o_sakT[:, pr, :], cols=64)
        ntile = 4 if gi < 4 else 2
        for j in range(ntile):
            ti = gi * 4 + j
            rows = 128 if gi < 4 else 32
            tk0 = ti * 128 if gi < 4 else 2048 + j * 32
            b = 2 + j % 2
            for k in range(8):
                MM(pb[b][0:rows, :], XN[:, k, tk0:tk0 + rows], WB[:, k, :], k == 0, k == 7, r=[r_wb, r_xn], w=[rpb[b]])
            ACT(VA[0:rows, ti, :], pb[b][0:rows, :], AF.Copy, r=[rpb[b]], w=[r_va])
            if 12 <= ti <= 15:
                out_stage(pb[b][:, :], rpb[b], o_pav[(ti - 12) * 128:(ti - 11) * 128, :])
            if ti >= 16:
                out_stage(pb[b][0:32, :], rpb[b], o_sav[(ti - 16) * 32:(ti - 15) * 32, :], rows=32)

    if PHASES <= 1.5:
        P.emit(nc)
        return nc
    load_w(WA[:], r_wa, w_in, 0, 512)
    V("memset", r=[], w=[r_qa], ap=QZ[0][:], constant=0.0)
    V("memset", r=[], w=[r_qa], ap=QZ[1][:], constant=0.0)
    if 'w' not in DBG:
        for hf in range(2):
            P.dma("pool", lambda e, hf=hf: e.dma_start(out=WAU[:, :, hf * 512:(hf + 1) * 512],
                  in_=w_a_up[:, hf * 512:(hf + 1) * 512].rearrange("(h p) n -> p h n", p=64)), w=[r_wau])

    ablk = [0]

    def attn_block(qcols, nq, keys, hg, q_pr_src):
        nkb = len(keys)
        bn, bd = (6, 7) if ablk[0] % 2 == 0 else (2, 3)
        RDt = RD2[ablk[0] % 2]
        r_rdt = r_rd2[ablk[0] % 2]
        ablk[0] += 1

        def scores(idx):
            kfn, nk, vfn, ebfn = keys[idx]
            sb_i = 4 + idx % 2
            for hh in range(4):
                h = hg * 4 + hh
                MM(pb[sb_i][0:nk, hh * 128:hh * 128 + nq], kfn(h), q_pr_src(h), True, True, r=[r_ka, r_qa, r_ck], w=[rpb[sb_i]])

        scores(0)
        for idx, (kfn, nk, vfn, ebfn) in enumerate(keys):
            sb_i = 4 + idx % 2
            if idx + 1 < nkb:
                scores(idx + 1)
            E = EE[idx % 2]
            rE = r_ee[idx % 2]
            Ev = E[0:nk, :].rearrange("p (h q) -> p h q", h=4)[:, :, 0:nq]
            ACT(Ev, pb[sb_i][0:nk, :].rearrange("p (h q) -> p h q", h=4)[:, :, 0:nq], AF.Exp, r=[rpb[sb_i]], w=[rE], scale=0.125)
            V("tensor_tensor", r=[rE, r_eb], w=[rE], out=Ev, in0=Ev, in1=ebfn(hg), op=ALU.mult)
            for hh in range(4):
                h = hg * 4 + hh
                first = (idx == 0 and hh == 0)
                MM(pb[bn][0:64, hh * 128:hh * 128 + nq], vfn(h), E[0:nk, hh * 128:hh * 128 + nq], first, idx == nkb - 1,
                   r=[rE, r_va, r_cv], w=[rpb[bn]], sgc=True)
            for hh in range(4):
                first = (idx == 0 and hh == 0)
                MM(pb[bd][0:64, hh * 128:hh * 128 + nq], ONESB[0:nk, 0:64], E[0:nk, hh * 128:hh * 128 + nq], first, idx == nkb - 1,
                   r=[rE, r_ones], w=[rpb[bd]], sgc=True)
        pn = pb[bn][0:64, :].rearrange("p (h q) -> p h q", h=4)[:, :, 0:nq]
        pd = pb[bd][0:64, :].rearrange("p (h q) -> p h q", h=4)[:, :, 0:nq]
        rdv = RDt[:, :].rearrange("p (h q) -> p h q", h=4)[:, :, 0:nq]
        V("reciprocal", r=[rpb[bd]], w=[r_rdt], out=rdv, in_=pd)
        V("tensor_tensor", r=[rpb[bn], r_rdt], w=[r_oa], out=OA[:, hg * 4:(hg + 1) * 4, qcols:qcols + nq], in0=pn, in1=rdv, op=ALU.mult)

    def up_proj_A(t0, n):
        for fc in range(8):
            b = fc % 2
            for h in range(8):
                MM(pb[b][:, 0:n], WAU[:, h, fc * 128:(fc + 1) * 128], OA[:, h, 0:n], h == 0, h == 7, r=[r_wau, r_oa], w=[rpb[b]])
            ACT(MIX[:, fc, t0:t0 + n], pb[b][:, 0:n], AF.Copy, r=[rpb[b]], w=[r_mix])

    def ebsel(tile, nk, nq):
        return lambda hg: tile[0:nk, hg * 4:(hg + 1) * 4, 0:nq]

    for gi, (t0, n) in enumerate(GROUPS):
        for pr in range(4):
            b = pr % 2
            for k in range(8):
                MM(pb[b][:, 0:n], WA[:, k, pr * 128:(pr + 1) * 128], XN[:, k, t0:t0 + n], k == 0, k == 7, r=[r_wa, r_xn], w=[rpb[b]])
            ACT(QZ[0][0:64, pr, 0:n], pb[b][0:64, 0:n], AF.Copy, r=[rpb[b]], w=[r_qa])
            ACT(QZ[1][64:128, pr, 0:n], pb[b][64:128, 0:n], AF.Copy, r=[rpb[b]], w=[r_qa])
        if gi < 4:
            for j in range(4):
                i = gi * 4 + j
                for hg in range(2):
                    keys = []
                    for kb in range(max(0, i - 4), i + 1):
                        ebt = {0: EB0, 1: EB1, 2: EBC, 3: EBC, 4: EB4}[i - kb]
                        keys.append((lambda h, kb=kb: KA[:, h // 2, kb * 128:(kb + 1) * 128], 128,
                                     lambda h, kb=kb: VA[:, kb, h * 64:(h + 1) * 64], ebsel(ebt, 128, 128)))
                    attn_block(j * 128, 128, keys, hg,
                               lambda h, j=j: QZ[h % 2][:, h // 2, j * 128:(j + 1) * 128])
        else:
            for s in range(2):
                P.dma("pool", lambda e, s=s: e.dma_start(out=CK[:], in_=ckT[s]), w=[r_ck])
                P.dma("pool", lambda e, s=s: e.dma_start(out=CV[:], in_=cav[s].rearrange("(b p) n -> p b n", p=128)), w=[r_cv])
                for hg in range(2):
                    keys = []
                    for kb in range(4):
                        ebt = EB1 if kb == 3 else EBC
                        keys.append((lambda h, kb=kb: CK[:, h // 2, kb * 128:(kb + 1) * 128], 128,
                                     lambda h, kb=kb: CV[:, kb, h * 64:(h + 1) * 64], ebsel(ebt, 128, 32)))
                    tk = 2048 + s * 32
                    keys.append((lambda h, tk=tk: KA[:, h // 2, tk:tk + 32], 32,
                                 lambda h, s=s: VA[0:32, 16 + s, h * 64:(h + 1) * 64], ebsel(EB0, 32, 32)))
                    attn_block(s * 32, 32, keys, hg,
                               lambda h, s=s: QZ[h % 2][:, h // 2, s * 32:s * 32 + 32])
        if 'u' not in DBG:
            up_proj_A(t0, n)
        if PHASES <= 1.7 and gi == 0:
            break
        if PHASES <= 1.8 and gi == 3:
            break
    P.barrier()
    if PHASES <= 1.9:
        P.emit(nc)
        return nc
    SG = sb(PH, [128, 512], BF16); r_sg = Res()
    for blk in range(2):
        load_w(WA[:], r_wa, w_in, 3592 + blk * 512, 512)
        for (t0, n) in GROUPS:
            for c4 in range(4):
                fc = blk * 4 + c4
                b = c4 % 2
                for k in range(8):
                    MM(pb[b][:, 0:n], WA[:, k, c4 * 128:(c4 + 1) * 128], XN[:, k, t0:t0 + n], k == 0, k == 7, r=[r_wa, r_xn], w=[rpb[b]])
                ACT(SG[:, 0:n], pb[b][:, 0:n], AF.Sigmoid, r=[rpb[b]], w=[r_sg])
                V("tensor_tensor", r=[r_sg, r_mix], w=[r_mix], out=MIX[:, fc, t0:t0 + n], in0=MIX[:, fc, t0:t0 + n], in1=SG[:, 0:n], op=ALU.mult)
    P.barrier()
    if PHASES <= 2:
        P.emit(nc)
        return nc

    q = PH
    WQK = sb(q, [128, 8, 1024], BF16); q += 16384
    HB = sb(q, [128, 4, T], BF16); q += 4 * T * 2
    STGC = [sb(q, [128, 520], F32), sb(q + 2080, [128, 520], F32)]; q += 4160
    CT_OFF = q
    CT = sb(q, [128, 512], F32); q += 2048
    HH = sb(CT_OFF, [64, 512], F32)
    QB = sb(q, [128, 4, 512], BF16); q += 4096
    KB = sb(q, [128, 4, 512], BF16); q += 4096
    V1 = sb(q, [64, 8, 4, 132], BF16); q += 8 * 4 * 132 * 2
    SOB = SQ
    WG = sb(q, [128, 8, 8], BF16); q += 128
    HALO = sb(q, [128, 8, 4], F32); q += 128
    IGR = sb(q, [4, 512], F32); q += 2048
    NBR = sb(q, [4, 512], F32); q += 2048
    XR = sb(q, [4, 512], F32); q += 2048
    MR = sb(q, [4, 512], F32); q += 2048
    TR = sb(q, [4, 512], F32); q += 2048
    ONE4 = sb(q, [4, 512], F32); q += 2048
    SM = sb(q, [4, 64], F32); q += 256
    XC = sb(q, [4, 64], F32); q += 256
    NMC = sb(q, [4, 64], F32); q += 256
    WSR = sb(q, [4, 64], F32); q += 256
    ENM = sb(q, [4, 64], F32); q += 256
    BDM = sb(q, [4, 256], F32); q += 1024
    CN = sb(q, [128, 4, 132], F32); q += 2112
    CNB = sb(q, [128, 4, 132], BF16); q += 1056
    WT = sb(q, [64, 256], F32); q += 1024
    AT = sb(q, [64, 256], BF16); q += 512
    COLS = sb(q, [64, 12], F32); q += 64
    A0 = sb(q, [128, 4], F32); q += 32
    NT4 = sb(q, [128, 4], F32); q += 32
    r_nt4 = Res()
    HI_OFF = q
    HI = sb(q, [64, 4, 132], F32); q += 2112
    HSQ = sb(HI_OFF, [64, 512], F32)
    ND = sb(q, [64, 4, 132], F32); q += 2112
    DA = sb(q, [64, 4], F32); q += 32
    SS4 = sb(q, [64, 4], F32); q += 32
    HBT = sb(q, [64, 512], BF16); q += 1024
    KS = sb(q, [64, 512], BF16); q += 1024
    XCs = [XC, sb(q, [4, 64], F32)]; q += 256
    NMCs = [NMC, sb(q, [4, 64], F32)]; q += 256
    WSRs = [WSR, sb(q, [4, 64], F32)]; q += 256
    ENMs = [ENM, sb(q, [4, 64], F32)]; q += 256
    BDMs = [BDM, sb(q, [4, 256], F32)]; q += 1024
    D4s = [sb(q, [4, 8], F32), sb(q + 32, [4, 8], F32)]; q += 64
    WTs = [WT, sb(q, [64, 256], F32)]; q += 1024
    ATs = [AT, sb(q, [64, 256], BF16)]; q += 512
    COLSs = [COLS, sb(q, [64, 12], F32)]; q += 64
    A0s = [A0, sb(q, [128, 4], F32)]; q += 32
    r_crow2, r_wt2, r_at2, r_cols2 = [Res(), Res()], [Res(), Res()], [Res(), Res()], [Res(), Res()]
    assert SB_LO + q <= 229376, q
    (r_wqk, r_hb, r_ct, r_qb, r_kb, r_v1, r_sob, r_wg, r_halo, r_rows, r_sm, r_crow, r_bdm, r_cn, r_cnb, r_wt,
     r_at, r_cols, r_a0, r_hi, r_nd, r_da, r_hh, r_hsq, r_hbt, r_ks, r_one4) = [Res() for _ in range(27)]
    r_stgc = [Res(), Res()]
    r_hsq = r_hi
    r_hh = r_ct
    SOBv = SOB[0:64, :, :]
    EYE4 = CST[0:4, 0:4]
    r_wqk2 = [Res(), Res()]
    load_w(WQK[:, :, 0:512], r_wqk2[0], w_in, 1536, 512)
    load_w(WQK[:, :, 512:1024], r_wqk2[1], w_in, 2048, 512)
    load_w(WA[:], r_wa, w_in, 2560, 512)
    load_w(WB[:], r_wb, w_in, 3072, 512)
    P.dma("pool", lambda e: e.dma_start(out=WG[:], in_=w_in[:, 3584:3592].rearrange("(k p) n -> p k n", p=128)), w=[r_wg])
    V("memset", r=[], w=[r_one4], ap=ONE4[:], constant=1.0)
    V("memset", r=[], w=[r_v1], ap=V1[:], constant=1.0)
    V("memset", r=[], w=[r_cn], ap=CN[:], constant=0.0)
    V("memset", r=[], w=[r_cnb], ap=CNB[:], constant=0.0)
    V("memset", r=[], w=[r_halo], ap=HALO[:], constant=0.0)
    V("memset", r=[], w=[r_sm], ap=SM[:], constant=0.0)
    SC = 128 ** -0.5

    def mlstm_group(t0, n, L, first, conv_out):
        nch = n // L
        BD = BD64 if L == 64 else BD32
        MNEG = MNEG64 if L == 64 else MNEG32
        for gi_, dst in ((0, IGR), (1, TR)):
            for k in range(8):
                MM(pb[0][0:4, 0:n], WG[:, k, gi_ * 4:gi_ * 4 + 4], XN[:, k, t0:t0 + n], k == 0, k == 7, r=[r_wg, r_xn], w=[rpb[0]])
            ACT(dst[:, 0:n], pb[0][0:4, 0:n], AF.Identity, r=[rpb[0], r_gb4], w=[r_rows], bias=GB4[:, gi_:gi_ + 1], scale=1.0)
        ACT(TR[:, 0:n], TR[:, 0:n], AF.Exp, r=[r_rows], w=[r_rows], scale=-1.0)
        ACT(TR[:, 0:n], TR[:, 0:n], AF.Ln, r=[r_rows], w=[r_rows], bias=1.0, scale=1.0)
        V("tensor_tensor_scan", r=[r_rows, r_one4, r_sm], w=[r_rows], out=NBR[:, 0:n], data0=ONE4[:, 0:n], data1=TR[:, 0:n],
          initial=(0.0 if first else SM[:, 0:1]), op0=ALU.mult, op1=ALU.add)
        V("tensor_tensor", r=[r_rows], w=[r_rows], out=XR[:, 0:n], in0=IGR[:, 0:n], in1=NBR[:, 0:n], op=ALU.add)
        V("tensor_tensor_scan", r=[r_rows, r_sm], w=[r_rows], out=MR[:, 0:n], data0=XR[:, 0:n], data1=XR[:, 0:n],
          initial=SM[:, 1:2], op0=ALU.max, op1=ALU.max)
        for fc in range(8):
            b = 1 + fc % 2
            sg, rsg = STGC[fc % 2], r_stgc[fc % 2]
            for k in range(8):
                MM(pb[b][:, 0:n], WQK[:, k, fc * 128:(fc + 1) * 128], XN[:, k, t0:t0 + n], k == 0, k == 7, r=[r_wqk2[fc // 4], r_xn], w=[rpb[b]])
            V("tensor_copy", r=[r_halo], w=[rsg], out=sg[:, 0:3], in_=HALO[:, fc, 0:3])
            ACT(sg[:, 3:3 + n], pb[b][:, 0:n], AF.Copy, r=[rpb[b]], w=[rsg])
            V("tensor_copy", r=[rsg], w=[r_halo], out=HALO[:, fc, 0:3], in_=sg[:, n:n + 3])
            V("tensor_scalar", r=[rsg, r_prm], w=[r_ct], out=CT[:, 0:n], in0=sg[:, 0:n], scalar1=PRM[:, 48 + fc * 4:49 + fc * 4],
              scalar2=PRM[:, 40 + fc:41 + fc], op0=ALU.mult, op1=ALU.add)
            for j in range(1, 4):
                V("scalar_tensor_tensor", r=[rsg, r_prm, r_ct], w=[r_ct], out=CT[:, 0:n], in0=sg[:, j:j + n],
                  scalar=PRM[:, 48 + fc * 4 + j:49 + fc * 4 + j], in1=CT[:, 0:n], op0=ALU.mult, op1=ALU.add)
            if fc < 4:
                ACT(QB[:, fc, 0:n], CT[:, 0:n], AF.Silu, r=[r_ct], w=[r_qb])
            else:
                ACT(CT[:, 0:n], CT[:, 0:n], AF.Silu, r=[r_ct], w=[r_ct])
                V("tensor_scalar", r=[r_ct], w=[r_kb], out=KB[:, fc - 4, 0:n], in0=CT[:, 0:n], scalar1=SC, scalar2=None, op0=ALU.mult)
        if conv_out is not None:
            DMA(conv_out, HALO[:, :, 0:3], r=[r_halo])
        for c in range(nch):
            tc = t0 + c * L
            for k in range(8):
                MM(pb[1][0:L, :], XN[:, k, tc:tc + L], WA[:, k, :], k == 0, k == 7, r=[r_wa, r_xn], w=[rpb[1]])
            ACT(V1[0:L, c, :, 0:128], pb[1][0:L, :].rearrange("p (h d) -> p h d", h=4), AF.Copy, r=[rpb[1]], w=[r_v1])
            for k in range(8):
                MM(pb[2][0:L, :], XN[:, k, tc:tc + L], WB[:, k, :], k == 0, k == 7, r=[r_wb, r_xn], w=[rpb[2]])
            ACT(SOBv[0:L, c, :], pb[2][0:L, :], AF.Sigmoid, r=[rpb[2]], w=[r_sob])
        def front_a(c):
            pc = c % 2
            cs = slice(c * L, (c + 1) * L)
            Mp = SM[:, 1:2] if c == 0 else MR[:, c * L - 1:c * L]
            xc, nmc, wsr, enm, bdm, d4 = XCs[pc], NMCs[pc], WSRs[pc], ENMs[pc], BDMs[pc], D4s[pc]
            rc = r_crow2[pc]
            V("tensor_scalar", r=[r_rows, r_sm], w=[rc], out=xc[:, 0:L], in0=XR[:, cs], scalar1=Mp, scalar2=None, op0=ALU.subtract)
            V("tensor_scalar", r=[r_rows, r_sm], w=[rc], out=nmc[:, 0:L], in0=MR[:, cs], scalar1=Mp, scalar2=-1.0, op0=ALU.subtract, op1=ALU.mult)
            V("tensor_tensor", r=[rc, r_cst], w=[rc], out=bdm[:, 0:4 * L].rearrange("p (h t) -> p h t", h=4),
              in0=BD.rearrange("p (h t) -> p h t", h=4), in1=nmc[:, 0:L].unsqueeze(1).to_broadcast([4, 4, L]), op=ALU.mult)
            V("tensor_scalar", r=[rc], w=[rc], out=wsr[:, 0:L], in0=xc[:, 0:L], scalar1=nmc[:, L - 1:L], scalar2=None, op0=ALU.add)
            V("tensor_tensor", r=[r_rows], w=[rc], out=enm[:, 0:L], in0=NBR[:, cs], in1=MR[:, cs], op=ALU.subtract)
            V("tensor_scalar", r=[rc, r_cst], w=[rc], out=d4[:, 0:4], in0=EYE4, scalar1=nmc[:, L - 1:L], scalar2=None, op0=ALU.mult)
            MM(pb[0][0:L, 0:4 * L], xc[:, 0:L], BD, True, False, r=[rc, r_cst], w=[rpb[0]])
            MM(pb[0][0:L, 0:4 * L], ONE4[:, 0:L], bdm[:, 0:4 * L], False, False, r=[r_one4, rc], w=[rpb[0]])
            MM(pb[0][0:L, 0:4 * L], EYE[0:L, 0:L], MNEG, False, True, r=[r_cst], w=[rpb[0]])
            ACT(WTs[pc][0:L, 0:4 * L], pb[0][0:L, 0:4 * L], AF.Exp, r=[rpb[0]], w=[r_wt2[pc]])
            MM(pb[7][0:L, 0:4], nmc[:, 0:L], EYE4, True, True, r=[rc, r_cst], w=[rpb[7]])
            MM(pb[7][0:L, 4:8], enm[:, 0:L], EYE4, True, True, r=[rc, r_cst], w=[rpb[7]])
            MM(pb[7][0:L, 8:12], wsr[:, 0:L], EYE4, True, True, r=[rc, r_cst], w=[rpb[7]])
            MM(pb[7][:, 16:20], ONE4[:, 0:128], d4[:, 0:4], True, True, r=[r_one4, rc], w=[rpb[7]])
            ACT(COLSs[pc][0:L, :], pb[7][0:L, 0:12], AF.Exp, r=[rpb[7]], w=[r_cols2[pc]])
            ACT(A0s[pc][:, :], pb[7][:, 16:20], AF.Exp, r=[rpb[7]], w=[r_cols2[pc]])
            for h in range(4):
                MM(pb[1][0:L, h * L:(h + 1) * L], KB[:, h, cs], QB[:, h, cs], True, True, r=[r_kb, r_qb], w=[rpb[1]])

        def front_b(c):
            pc = c % 2
            V("tensor_tensor", r=[rpb[1], r_wt2[pc]], w=[r_at2[pc]], out=ATs[pc][0:L, 0:4 * L], in0=pb[1][0:L, 0:4 * L], in1=WTs[pc][0:L, 0:4 * L], op=ALU.mult)

        def back(c):
            pc = c % 2
            cs = slice(c * L, (c + 1) * L)
            tc = t0 + c * L
            COLS, A0, AT = COLSs[pc], A0s[pc], ATs[pc]
            r_cols, r_at = r_cols2[pc], r_at2[pc]
            for h in range(4):
                bk, c0 = 3 + h // 2, (h % 2) * 256
                MM(pb[bk][0:L, c0:c0 + 129], QB[:, h, cs], CNB[:, h, 0:129], True, True, r=[r_qb, r_cnb], w=[rpb[bk]])
            for h in range(4):
                bk, c0 = 5 + h // 2, (h % 2) * 256
                MM(pb[bk][0:L, c0:c0 + 129], AT[0:L, h * L:(h + 1) * L], V1[0:L, c, h, 0:129], True, True, r=[r_at, r_v1], w=[rpb[bk]])
            for h in range(4):
                bk, c0 = 5 + h // 2, (h % 2) * 256
                ACT(HI[0:L, h, 0:129], pb[bk][0:L, c0:c0 + 129], AF.Copy, r=[rpb[bk]], w=[r_hi])
            for h in range(4):
                bk, c0 = 3 + h // 2, (h % 2) * 256
                V("scalar_tensor_tensor", r=[rpb[bk], r_cols, r_hi], w=[r_nd], out=ND[0:L, h, 0:129], in0=pb[bk][0:L, c0:c0 + 129],
                  scalar=COLS[0:L, h:h + 1], in1=HI[0:L, h, 0:129], op0=ALU.mult, op1=ALU.add)
            V("tensor_scalar", r=[r_nd], w=[r_da], out=DA[0:L, :], in0=ND[0:L, :, 128], scalar1=-1.0, scalar2=None, op0=ALU.mult)
            V("tensor_tensor", r=[r_nd, r_da], w=[r_da], out=DA[0:L, :], in0=DA[0:L, :], in1=ND[0:L, :, 128], op=ALU.max)
            V("tensor_tensor", r=[r_da, r_cols], w=[r_da], out=DA[0:L, :], in0=DA[0:L, :], in1=COLS[0:L, 4:8], op=ALU.max)
            V("reciprocal", r=[r_da], w=[r_da], out=DA[0:L, :], in_=DA[0:L, :])
            HHv = HH[0:L, :].rearrange("p (h d) -> p h d", h=4)
            V("tensor_tensor", r=[r_nd, r_da], w=[r_hh], out=HHv, in0=ND[0:L, :, 0:128], in1=DA[0:L, :].unsqueeze(2).to_broadcast([L, 4, 128]), op=ALU.mult)
            V("tensor_tensor", r=[r_hh], w=[r_hsq], out=HSQ[0:L, :], in0=HH[0:L, :], in1=HH[0:L, :], op=ALU.mult)
            V("tensor_reduce", r=[r_hsq], w=[r_da], out=SS4[0:L, :], in_=HSQ[0:L, :].rearrange("p (h d) -> p h d", h=4), axis=AX.X, op=ALU.add)
            ACT(SS4[0:L, :], SS4[0:L, :], AF.Sqrt, r=[r_da], w=[r_da], scale=1.0 / 128, bias=EPS)
            V("reciprocal", r=[r_da], w=[r_da], out=SS4[0:L, :], in_=SS4[0:L, :])
            V("tensor_tensor", r=[r_hh, r_da], w=[r_hh], out=HHv, in0=HHv, in1=SS4[0:L, :].unsqueeze(2).to_broadcast([L, 4, 128]), op=ALU.mult)
            V("tensor_tensor", r=[r_hh, r_ghd], w=[r_hh], out=HH[0:L, :], in0=HH[0:L, :], in1=GHD[0:L, :], op=ALU.mult)
            V("tensor_tensor", r=[r_hh, r_sob], w=[r_hbt], out=HBT[0:L, :], in0=HH[0:L, :], in1=SOBv[0:L, c, :], op=ALU.mult)
            for h in range(4):
                MM(pb[3][:, h * L:(h + 1) * L], HBT[0:L, h * 128:(h + 1) * 128], EYEB[0:L, 0:L], True, True, r=[r_hbt, r_eyeb], w=[rpb[3]])
            ACT(HB[:, :, tc:tc + L], pb[3][:, 0:4 * L].rearrange("p (h t) -> p h t", h=4), AF.Copy, r=[rpb[3]], w=[r_hb])
            for h in range(4):
                MM(pb[2][0:L, h * 128:(h + 1) * 128], KB[:, h, cs], EYEB[:, :], True, True, r=[r_kb, r_eyeb], w=[rpb[2]])
            V("tensor_tensor", r=[rpb[2], r_cols], w=[r_ks], out=KS[0:L, :].rearrange("p (h d) -> p h d", h=4),
              in0=pb[2][0:L, :].rearrange("p (h d) -> p h d", h=4), in1=COLS[0:L, 8:12].unsqueeze(2).to_broadcast([L, 4, 128]), op=ALU.mult)
            for h in range(4):
                bk, c0 = 5 + h // 2, (h % 2) * 256
                MM(pb[bk][:, c0:c0 + 129], KS[0:L, h * 128:(h + 1) * 128], V1[0:L, c, h, 0:129], True, True, r=[r_ks, r_v1], w=[rpb[bk]])
            for h in range(4):
                bk, c0 = 5 + h // 2, (h % 2) * 256
                V("scalar_tensor_tensor", r=[r_cn, r_cols, rpb[bk]], w=[r_cn], out=CN[:, h, 0:129], in0=CN[:, h, 0:129],
                  scalar=A0[:, h:h + 1], in1=pb[bk][:, c0:c0 + 129], op0=ALU.mult, op1=ALU.add)
            ACT(CNB[:], CN[:], AF.Copy, r=[r_cn], w=[r_cnb])

        front_a(0)
        front_b(0)
        for c in range(nch):
            if c + 1 < nch:
                front_a(c + 1)
            back(c)
            if c + 1 < nch:
                front_b(c + 1)
        V("tensor_copy", r=[r_rows], w=[r_sm], out=SM[:, 0:1], in_=NBR[:, n - 1:n])
        V("tensor_copy", r=[r_rows], w=[r_sm], out=SM[:, 1:2], in_=MR[:, n - 1:n])

    def state_out(oC, on, om):
        DMA(oC, CN[:, :, 0:128], r=[r_cn])
        V("tensor_copy", r=[r_cn], w=[r_nt4], out=NT4[:, :], in_=CN[:, :, 128])
        DMA(on, NT4[:, :], r=[r_nt4])
        V("tensor_tensor", r=[r_sm], w=[r_sm], out=SM[:, 16:17], in0=SM[:, 1:2], in1=SM[:, 0:1], op=ALU.subtract)
        DMA(om, SM[:, 16:17], r=[r_sm])

    for gi, (t0, n) in enumerate(GROUPS[:4]):
        mlstm_group(t0, n, 64, gi == 0, o_pconvT if gi == 3 else None)
    state_out(o_pC, o_pn, o_pm)
    for s in range(2):
        DMA(CN[:, :, 0:128], C0d[s].rearrange("h k v -> k h v"), w=[r_cn])
        DMA(NT4[:, :], n0T[:, s, :], w=[r_nt4])
        V("tensor_copy", r=[r_nt4], w=[r_cn], out=CN[:, :, 128], in_=NT4[:, :])
        ACT(CNB[:], CN[:], AF.Copy, r=[r_cn], w=[r_cnb])
        DMA(HALO[:, :, 0:3], convT[:, :, s, :], w=[r_halo])
        V("memset", r=[], w=[r_sm], ap=SM[:, 0:1], constant=0.0)
        DMA(SM[:, 1:2], m0T[s], w=[r_sm])
        mlstm_group(2048 + s * 32, 32, 32, True, o_sconvT[:, :, s, :])
        state_out(o_sC[:, s], o_sn[:, s, :], o_sm[s])
    P.barrier()
    if PHASES <= 3:
        P.emit(nc)
        return nc
    SG2 = CT
    for blk in range(2):
        P.dma("pool", lambda e, blk=blk: e.dma_start(out=WA[:, 0:4, :], in_=w_b_up[:, blk * 512:(blk + 1) * 512].rearrange("(k p) n -> p k n", p=128)), w=[r_wa])
        load_w(WB[:], r_wb, w_in, 4616 + blk * 512, 512)
        for (t0, n) in GROUPS:
            for c4 in range(4):
                fc = blk * 4 + c4
                for k in range(8):
                    MM(pb[0][:, 0:n], WB[:, k, c4 * 128:(c4 + 1) * 128], XN[:, k, t0:t0 + n], k == 0, k == 7, r=[r_wb, r_xn], w=[rpb[0]])
                ACT(SG2[:, 0:n], pb[0][:, 0:n], AF.Sigmoid, r=[rpb[0]], w=[r_ct])
                for k in range(4):
                    MM(pb[1][:, 0:n], WA[:, k, c4 * 128:(c4 + 1) * 128], HB[:, k, t0:t0 + n], k == 0, k == 3, r=[r_wa, r_hb], w=[rpb[1]])
                V("tensor_tensor", r=[r_ct, rpb[1]], w=[r_ct], out=SG2[:, 0:n], in0=SG2[:, 0:n], in1=pb[1][:, 0:n], op=ALU.mult)
                V("tensor_tensor", r=[r_ct, r_mix], w=[r_mix], out=MIX[:, fc, t0:t0 + n], in0=MIX[:, fc, t0:t0 + n], in1=SG2[:, 0:n], op=ALU.add)
    P.barrier()

    X1 = sb(PH, [128, 8, T], F32); r_x1 = Res()
    DMA(X1[:], xT, w=[r_x1])
    for blk in range(2):
        load_w(WA[:], r_wa, w_out, blk * 512, 512)
        for (t0, n) in GROUPS:
            for c4 in range(4):
                fc = blk * 4 + c4
                b = c4 % 2
                for k in range(8):
                    MM(pb[b][:, 0:n], WA[:, k, c4 * 128:(c4 + 1) * 128], MIX[:, k, t0:t0 + n], k == 0, k == 7, r=[r_wa, r_mix], w=[rpb[b]])
                V("tensor_tensor", r=[r_x1, rpb[b]], w=[r_x1], out=X1[:, fc, t0:t0 + n], in0=X1[:, fc, t0:t0 + n], in1=pb[b][:, 0:n], op=ALU.add)
    P.barrier()

    for (t0, n) in GROUPS:
        rmsnorm(lambda k: X1[:, k, t0:t0 + n], r_x1, lambda k: XN[:, k, t0:t0 + n], r_xn, 8, n)
    P.barrier()
    mixbase = MIX_OFF
    WCQ = sb(mixbase, [128, 8, 1024], BF16); r_wcq = Res()
    WCO = sb(mixbase + 16384, [128, 8, 1024], BF16); r_wco = Res()
    q = PH + 8 * T * 4
    QC = sb(q, [128, 8, 512], BF16); q += 8192
    OC = sb(q, [128, 8, 512], BF16); q += 8192
    EC = [sb(q, [128, 512], BF16), sb(q + 1024, [128, 512], BF16)]; q += 2048
    RDC = RS
    CMK = sb(SQ_OFF, [128, 8, 256], BF16)
    CMV = sb(SQ_OFF + 4096, [128, 2, 1024], BF16)
    assert SB_LO + q <= 229376, q
    r_qc, r_oc, r_rdc, r_cmk, r_cmv = Res(), Res(), Res(), Res(), Res()
    r_ec = [Res(), Res()]
    r_wcq2 = [Res(), Res()]
    r_wco2 = [Res(), Res()]
    for hf in range(2):
        load_w(WCQ[:, :, hf * 512:(hf + 1) * 512], r_wcq2[hf], w_cq, hf * 512, 512)
        load_w(WCO[:, :, hf * 512:(hf + 1) * 512], r_wco2[hf], w_co, hf * 512, 512)

    def xattn(cols0, nq, MKx, rmk, MVx, rmv):
        for h in range(4):
            for mc in range(2):
                for dc in range(2):
                    MM(pb[4 + mc][:, 0:nq], MKx[:, 2 * h + dc, mc * 128:(mc + 1) * 128], QC[:, 2 * h + dc, cols0:cols0 + nq], dc == 0, dc == 1,
                       r=[rmk, r_qc], w=[rpb[4 + mc]])
                ACT(EC[mc][:, 0:nq], pb[4 + mc][:, 0:nq], AF.Exp, r=[rpb[4 + mc]], w=[r_ec[mc]], scale=1.0 / 16)
            for mc in range(2):
                MM(pb[7][:, 0:nq], ONESB[:, :], EC[mc][:, 0:nq], mc == 0, mc == 1, r=[r_ones, r_ec[mc]], w=[rpb[7]])
            V("reciprocal", r=[rpb[7]], w=[r_rdc], out=RDC[:, 0:nq], in_=pb[7][:, 0:nq])
            for dvc in range(2):
                for mc in range(2):
                    MM(pb[6][:, 0:nq], MVx[:, mc, h * 256 + dvc * 128:h * 256 + (dvc + 1) * 128], EC[mc][:, 0:nq], mc == 0, mc == 1,
                       r=[rmv, r_ec[mc]], w=[rpb[6]])
                V("tensor_tensor", r=[rpb[6], r_rdc], w=[r_oc], out=OC[:, 2 * h + dvc, cols0:cols0 + nq], in0=pb[6][:, 0:nq], in1=RDC[:, 0:nq], op=ALU.mult)

    for gi, (t0, n) in enumerate(GROUPS):
        for fc in range(8):
            b = fc % 2
            for k in range(8):
                MM(pb[b][:, 0:n], WCQ[:, k, fc * 128:(fc + 1) * 128], XN[:, k, t0:t0 + n], k == 0, k == 7, r=[r_wcq2[fc // 4], r_xn], w=[rpb[b]])
            ACT(QC[:, fc, 0:n], pb[b][:, 0:n], AF.Copy, r=[rpb[b]], w=[r_qc])
        if gi < 4:
            xattn(0, n, MKT, r_mkt, MV, r_mv)
        else:
            for s in range(2):
                P.dma("pool", lambda e, s=s: e.dma_start(out=CMK[:], in_=cmkT[s]), w=[r_cmk])
                for hf in range(2):
                    P.dma("pool", lambda e, s=s, hf=hf: e.dma_start(out=CMV[:, :, hf * 512:(hf + 1) * 512],
                          in_=cmv[s][:, hf * 512:(hf + 1) * 512].rearrange("(b p) n -> p b n", p=128)), w=[r_cmv])
                xattn(s * 32, 32, CMK, r_cmk, CMV, r_cmv)
        for fc in range(8):
            b = fc % 2
            for k in range(8):
                MM(pb[b][:, 0:n], WCO[:, k, fc * 128:(fc + 1) * 128], OC[:, k, 0:n], k == 0, k == 7, r=[r_wco2[fc // 4], r_oc], w=[rpb[b]])
            V("tensor_tensor", r=[r_x1, rpb[b]], w=[r_x1], out=X1[:, fc, t0:t0 + n], in0=X1[:, fc, t0:t0 + n], in1=pb[b][:, 0:n], op=ALU.add)
    P.barrier()

    for (t0, n) in GROUPS:
        rmsnorm(lambda k: X1[:, k, t0:t0 + n], r_x1, lambda k: XN[:, k, t0:t0 + n], r_xn, 16, n)
    X1d = nc.dram_tensor("X1d", [128, 8, T], F32, kind="Internal").ap()
    r_x1d = Res()
    DMA(X1d, X1[:], r=[r_x1], w=[r_x1d])
    P.barrier()
    WPQ = sb(MIX_OFF, [128, 8, 2048], BF16); r_wpq = Res()
    QP = sb(SQ_OFF, [128, 16, 512], BF16); r_qp = Res()
    SUBT = sb(SQ_OFF + 16384, [128, 16, 128], BF16); r_subt = Res()
    SCO = sb(SQ_OFF + 20480, [128, 16, 128], F32); r_sc = Res()
    q = PH
    EIDX = sb(q, [128, 17, 128], I32); q += 8704
    GALL = sb(q, [128, 17, 128], F32); q += 8704
    SC2 = sb(q, [128, 2048], F32); q += 8192
    EQ = sb(q, [128, 2048], F32); q += 8192
    SV = sb(q, [128, 16, 16], F32); q += 1024
    SI = sb(q, [128, 16, 16], U32); q += 1024
    SIF = sb(q, [128, 16, 16], F32); q += 1024
    CVP = sb(q, [128, 8, 16], F32); q += 512
    CI = sb(q, [128, 8, 16], U32); q += 512
    CIF = sb(q, [128, 8, 16], F32); q += 512
    AF_ = sb(q, [128, 8, 16], F32); q += 512
    BF_ = sb(q, [128, 8, 16], F32); q += 512
    ISEL = sb(q, [128, 8, 16], F32); q += 512
    JSEL = sb(q, [128, 8, 16], F32); q += 512
    ZZ = sb(q, [128, 8], F32); q += 32
    Q6A_END = q
    Q6B = PH + 17408
    (r_eidx, r_gall, r_sc2, r_eq, r_sv, r_si, r_cv, r_ci, r_ab, r_ij, r_zz) = [Res() for _ in range(11)]
    r_svp, r_sip, r_sc2p, r_cvp, r_cip, r_eqp = [[Res(), Res()] for _ in range(6)]
    r_wpq4 = [Res() for _ in range(4)]
    for hf in range(4):
        load_w(WPQ[:, :, hf * 512:(hf + 1) * 512], r_wpq4[hf], w_pq, hf * 512, 512)
    P.dma("pool", lambda e: e.dma_start(out=SUBT[:], in_=subT), w=[r_subt])
    PUVB = nc.dram_tensor("PUVB", [16384, 2048], BF16, kind="Internal").ap()
    r_puvb = Res()
    NCAST = 16
    r_cast = [Res() for _ in range(NCAST)]
    for ic in range(NCAST):
        rows = 16384 // NCAST
        P.dma("pool", lambda e, ic=ic, rows=rows: e.dma_start(
            out=PUVB[ic * rows:(ic + 1) * rows, :].rearrange("r (a b) -> r a b", b=512),
            in_=peer_uv[ic * rows:(ic + 1) * rows, :].rearrange("r (a b) -> r a b", b=512)), w=[r_cast[ic]])
    V("memset", r=[], w=[r_eidx], ap=EIDX[:], constant=0)
    V("memset", r=[], w=[r_gall], ap=GALL[:], constant=0.0)
    CAND = SC2
    QPb = sb(Q6A_END, [128, 16, 512], BF16)
    SCOb = sb(Q6A_END + 16384, [128, 16, 128], F32)
    assert SB_LO + Q6A_END + 16384 + 8192 <= 229376
    QP2, r_qp2 = [QP, QPb], [r_qp, Res()]
    SCO2, r_scb = [SCO, SCOb], [r_sc, Res()]
    TILES = []
    for gi, (t0, n) in enumerate(GROUPS):
        for jt in range(4 if gi < 4 else 1):
            TILES.append((gi, jt, gi * 4 + jt, 128 if gi < 4 else 64))

    def do_qp(gi):
        t0, n = GROUPS[gi]
        for j in range(16):
            b = j % 2
            for k in range(8):
                MM(pb[b][:, 0:n], WPQ[:, k, j * 128:(j + 1) * 128], XN[:, k, t0:t0 + n], k == 0, k == 7, r=[r_wpq4[j // 4], r_xn], w=[rpb[b]])
            ACT(QP2[gi % 2][:, j, 0:n], pb[b][:, 0:n], AF.Copy, r=[rpb[b]], w=[r_qp2[gi % 2]])

    def do_scores(tq):
        gi, jt, ti, nt = TILES[tq]
        c0 = jt * 128
        for j4 in range(4):
            bk = 2 + j4
            for jj in range(4):
                j = j4 * 4 + jj
                MM(pb[bk][0:nt, jj * 128:(jj + 1) * 128], QP2[gi % 2][:, j, c0:c0 + nt], SUBT[:, j, :], True, True, r=[r_qp2[gi % 2], r_subt], w=[rpb[bk]])
            ACT(SCO2[tq % 2][0:nt, j4 * 4:(j4 + 1) * 4, :], pb[bk][0:nt, :].rearrange("p (j k) -> p j k", j=4), AF.Copy, r=[rpb[bk]], w=[r_scb[tq % 2]])

    do_qp(0)
    do_scores(0)
    for tq, (gi, jt, ti, nt) in enumerate(TILES):
        if True:
            SCOx, r_scx = SCO2[tq % 2], r_scb[tq % 2]
            if jt == 0 and gi + 1 < len(GROUPS):
                do_qp(gi + 1)
            if tq + 1 < len(TILES):
                do_scores(tq + 1)
            for j2 in range(0, 16, 2):
                js = (j2, j2 + 1)
                for pj, j in enumerate(js):
                    V("max", r=[r_scx], w=[r_svp[pj]], out=SV[0:nt, j, 0:8], in_=SCOx[0:nt, j, :])
                for pj, j in enumerate(js):
                    V("max_index", r=[r_scx, r_svp[pj]], w=[r_sip[pj]], out=SI[0:nt, j, 0:8], in_max=SV[0:nt, j, 0:8], in_values=SCOx[0:nt, j, :])
                for pj, j in enumerate(js):
                    V("match_replace", r=[r_scx, r_svp[pj]], w=[r_sc2p[pj]], out=SC2[0:nt, pj * 128:(pj + 1) * 128], in_to_replace=SV[0:nt, j, 0:8], in_values=SCOx[0:nt, j, :], imm_value=-1e30)
                for pj, j in enumerate(js):
                    V("max", r=[r_sc2p[pj]], w=[r_svp[pj]], out=SV[0:nt, j, 8:16], in_=SC2[0:nt, pj * 128:(pj + 1) * 128])
                for pj, j in enumerate(js):
                    V("max_index", r=[r_sc2p[pj], r_svp[pj]], w=[r_sip[pj]], out=SI[0:nt, j, 8:16], in_max=SV[0:nt, j, 8:16], in_values=SC2[0:nt, pj * 128:(pj + 1) * 128])
            V("tensor_copy", r=[r_sip[0], r_sip[1], r_si], w=[r_si], out=SIF[0:nt, :, 0:1], in_=SIF[0:nt, :, 0:1])
            V("tensor_copy", r=[r_si], w=[r_si], out=SIF[0:nt], in_=SI[0:nt])
            SV4 = SV[0:nt].rearrange("p (h c) a -> p h c a", c=2)
            SIF4 = SIF[0:nt].rearrange("p (h c) a -> p h c a", c=2)
            V("tensor_tensor", r=[r_sv, r_svp[0], r_svp[1], r_sc2p[0], r_sc2p[1], r_sip[0], r_sip[1]], w=[r_sc2], out=CAND[0:nt, :].rearrange("p (h a b) -> p h a b", h=8, a=16),
              in0=SV4[:, :, 0, :].unsqueeze(3).to_broadcast([nt, 8, 16, 16]), in1=SV4[:, :, 1, :].unsqueeze(2).to_broadcast([nt, 8, 16, 16]), op=ALU.add)
            for h2 in range(0, 8, 2):
                hs = (h2, h2 + 1)
                chs = [CAND[0:nt, h * 256:(h + 1) * 256] for h in hs]
                for ph, h in enumerate(hs):
                    V("max", r=[r_sc2], w=[r_cvp[ph]], out=CVP[0:nt, h, 0:8], in_=chs[ph])
                for ph, h in enumerate(hs):
                    V("max_index", r=[r_sc2, r_cvp[ph]], w=[r_cip[ph]], out=CI[0:nt, h, 0:8], in_max=CVP[0:nt, h, 0:8], in_values=chs[ph])
                for ph, h in enumerate(hs):
                    V("match_replace", r=[r_sc2, r_cvp[ph]], w=[r_eqp[ph]], out=EQ[0:nt, ph * 256:(ph + 1) * 256], in_to_replace=CVP[0:nt, h, 0:8], in_values=chs[ph], imm_value=-1e30)
                for ph, h in enumerate(hs):
                    V("max", r=[r_eqp[ph]], w=[r_cvp[ph]], out=CVP[0:nt, h, 8:16], in_=EQ[0:nt, ph * 256:(ph + 1) * 256])
                for ph, h in enumerate(hs):
                    V("max_index", r=[r_eqp[ph], r_cvp[ph]], w=[r_cip[ph]], out=CI[0:nt, h, 8:16], in_max=CVP[0:nt, h, 8:16], in_values=EQ[0:nt, ph * 256:(ph + 1) * 256])
            V("tensor_copy", r=[r_cip[0], r_cip[1], r_cvp[0], r_cvp[1], r_eqp[0], r_eqp[1], r_ci, r_cv, r_eq], w=[r_ci, r_cv, r_eq], out=CIF[0:nt, :, 0:1], in_=CIF[0:nt, :, 0:1])
            V("tensor_copy", r=[r_ci], w=[r_ci], out=CIF[0:nt], in_=CI[0:nt])
            EQ4 = EQ[0:nt, :].rearrange("p (h k a) -> p h k a", h=8, k=16)
            io16 = IOTA16[0:nt, :].unsqueeze(1).unsqueeze(1).to_broadcast([nt, 8, 16, 16])
            io256 = IOTA256[0:nt, :].unsqueeze(1).unsqueeze(1).to_broadcast([nt, 8, 16, 16])
            V("tensor_tensor", r=[r_ci, r_cst], w=[r_eq], out=EQ4, in0=CIF[0:nt].unsqueeze(3).to_broadcast([nt, 8, 16, 16]), in1=io256, op=ALU.is_ge)
            V("tensor_reduce", r=[r_eq], w=[r_ab], out=AF_[0:nt], in_=EQ4, axis=AX.X, op=ALU.add)
            V("tensor_scalar", r=[r_ab], w=[r_ab], out=AF_[0:nt], in0=AF_[0:nt], scalar1=-1.0, scalar2=None, op0=ALU.add)
            V("scalar_tensor_tensor", r=[r_ab, r_ci], w=[r_ab], out=BF_[0:nt].rearrange("p h k -> p (h k)"), in0=AF_[0:nt].rearrange("p h k -> p (h k)"), scalar=-16.0,
              in1=CIF[0:nt].rearrange("p h k -> p (h k)"), op0=ALU.mult, op1=ALU.add)
            for (sel, src, c_) in ((ISEL, AF_, 0), (JSEL, BF_, 1)):
                V("tensor_tensor", r=[r_ab, r_cst], w=[r_eq], out=EQ4, in0=src[0:nt].unsqueeze(3).to_broadcast([nt, 8, 16, 16]), in1=io16, op=ALU.is_equal)
                V("tensor_tensor", r=[r_eq, r_si], w=[r_eq], out=EQ4, in0=EQ4, in1=SIF4[:, :, c_, :].unsqueeze(2).to_broadcast([nt, 8, 16, 16]), op=ALU.mult)
                V("tensor_reduce", r=[r_eq], w=[r_ij], out=sel[0:nt], in_=EQ4, axis=AX.X, op=ALU.add)
            V("scalar_tensor_tensor", r=[r_ij], w=[r_ij], out=ISEL[0:nt].rearrange("p h k -> p (h k)"), in0=ISEL[0:nt].rearrange("p h k -> p (h k)"), scalar=128.0,
              in1=JSEL[0:nt].rearrange("p h k -> p (h k)"), op0=ALU.mult, op1=ALU.add)
            V("tensor_copy", r=[r_ij], w=[r_eidx], out=EIDX[0:nt, ti, :], in_=ISEL[0:nt].rearrange("p h k -> p (h k)"))
            V("tensor_copy", r=[r_cv], w=[r_zz], out=ZZ[0:nt], in_=CVP[0:nt, :, 0])
            V("tensor_tensor", r=[r_cv, r_zz], w=[r_cv], out=CVP[0:nt], in0=CVP[0:nt], in1=ZZ[0:nt].unsqueeze(2).to_broadcast([nt, 8, 16]), op=ALU.subtract)
            ACT(CVP[0:nt], CVP[0:nt], AF.Exp, r=[r_cv], w=[r_cv])
            V("tensor_reduce", r=[r_cv], w=[r_zz], out=ZZ[0:nt], in_=CVP[0:nt], axis=AX.X, op=ALU.add)
            V("reciprocal", r=[r_zz], w=[r_zz], out=ZZ[0:nt], in_=ZZ[0:nt])
            V("tensor_tensor", r=[r_cv, r_zz], w=[r_gall], out=GALL[0:nt, ti, :].rearrange("p (h k) -> p h k", h=8), in0=CVP[0:nt],
              in1=ZZ[0:nt].unsqueeze(2).to_broadcast([nt, 8, 16]), op=ALU.mult)
    P.barrier()
    if PHASES <= 6.5:
        P.emit(nc)
        return nc
    q = Q6B
    NGB = 6
    GBUF = [sb(MIX_OFF, [128, 4, 2048], BF16), sb(MIX_OFF + 16384, [128, 4, 2048], BF16), sb(WA_OFF, [128, 4, 2048], BF16)]
    for _ in range(NGB - 3):
        GBUF.append(sb(q, [128, 4, 2048], BF16)); q += 16384
    XTOK = [sb(q, [128, 1024], F32), sb(q + 4096, [128, 1024], F32)]; q += 8192
    JUNK = sb(q, [128, 1024], BF16); q += 2048
    X1T = sb(q, [128, 8, 128], F32); q += 4096
    OUTS = sb(q, [128, 1024], F32); q += 4096
    ACTV = sb(STG_OFF, [128, 128], F32)
    GLU = sb(STG_OFF + 512, [128, 128], F32)
    GW = sb(STG_OFF + 1024, [128, 128], F32)
    DG = [sb(STG_OFF + 1536 + i * 256, [128, 128], BF16) for i in range(8)]
    assert SB_LO + q <= 229376, q
    r_gbuf = [[Res() for _ in range(4)] for _ in range(NGB)]
    r_av = [Res() for _ in range(NGB)]
    r_gl = [Res() for _ in range(NGB)]
    r_gw = [Res() for _ in range(NGB)]
    r_dg = [Res() for _ in range(8)]
    r_xtok = [Res(), Res()]
    r_x1t, r_outs = Res(), Res()
    V("memset", r=[], w=[r_xtok[0]], ap=XTOK[0][:], constant=0.0)
    V("memset", r=[], w=[r_xtok[1]], ap=XTOK[1][:], constant=0.0)
    NG = 32
    gcount = [0]
    dgc = [0]

    def make_xtok(ti):
        nt = 128 if ti < 16 else 64
        tk0 = ti * 128
        xt, rxt = XTOK[ti % 2], r_xtok[ti % 2]
        for k in range(8):
            bk = k // 4
            MM(pb[bk][0:nt, (k % 4) * 128:(k % 4 + 1) * 128], XN[:, k, tk0:tk0 + nt], EYEB[:, :], True, True, r=[r_xn, r_eyeb], w=[rpb[bk]])
        for bk in range(2):
            ACT(xt[0:nt, bk * 512:(bk + 1) * 512], pb[bk][0:nt, :], AF.Copy, r=[rpb[bk]], w=[rxt])

    def epilogue(ti):
        nt = 128 if ti < 16 else 64
        tk0 = ti * 128
        DMA(X1T[:, :, 0:nt], X1d[:, :, tk0:tk0 + nt], r=[r_x1d], w=[r_x1t])
        for k in range(8):
            bk = 4 + k // 4
            MM(pb[bk][:, (k % 4) * 128:(k % 4) * 128 + nt], OUTS[0:nt, k * 128:(k + 1) * 128], EYE[0:nt, 0:nt], True, True, r=[r_outs, r_cst], w=[rpb[bk]])
        for bk in range(2):
            V("tensor_tensor", r=[r_x1t, rpb[4 + bk]], w=[r_x1t], out=X1T[:, bk * 4:(bk + 1) * 4, 0:nt], in0=X1T[:, bk * 4:(bk + 1) * 4, 0:nt],
              in1=pb[4 + bk][:, :].rearrange("p (k t) -> p k t", k=4)[:, :, 0:nt], op=ALU.add)
        YTv = OUTS[:, :].rearrange("p (k t) -> p k t", k=8)
        rmsnorm(lambda k: X1T[:, k, 0:nt], r_x1t, lambda k: YTv[:, k, 0:nt], r_outs, 24, nt, bank=6)
        DMA(o_yT[:, :, tk0:tk0 + nt], YTv[:, :, 0:nt], r=[r_outs])

    pending = None
    make_xtok(0)
    for ti in range(17):
        xt, rxt = XTOK[ti % 2], r_xtok[ti % 2]
        gbase = gcount[0]
        for g in range(NG + 1):
            if g < NG:
                bi = (gbase + g) % NGB
                e0 = 4 * g
                for i in range(4):
                    P.dma("pool", lambda e, bi=bi, i=i, e0=e0, ti=ti: e.indirect_dma_start(
                        out=GBUF[bi][:, i, :], out_offset=None, in_=PUVB,
                        in_offset=bass.IndirectOffsetOnAxis(ap=EIDX[:, ti, e0 + i:e0 + i + 1], axis=0)), r=[r_eidx] + r_cast, w=[r_gbuf[bi][i]])
                for i in range(4):
                    V("scalar_tensor_tensor", r=[r_gbuf[bi][i], rxt], w=[r_av[bi]], out=JUNK[:, :], in0=GBUF[bi][:, i, 0:1024], scalar=1.0,
                      in1=xt[:, :], op0=ALU.mult, op1=ALU.mult, accum_out=ACTV[:, e0 + i:e0 + i + 1])
                ACT(GLU[:, e0:e0 + 4], ACTV[:, e0:e0 + 4], AF.Gelu, r=[r_av[bi]], w=[r_gl[bi]])
            if g >= 1:
                gg = g - 1
                bi = (gbase + gg) % NGB
                e0 = 4 * gg
                V("tensor_tensor", r=[r_gl[bi], r_gall], w=[r_gw[bi]], out=GW[:, e0:e0 + 4], in0=GLU[:, e0:e0 + 4], in1=GALL[:, ti, e0:e0 + 4], op=ALU.mult)
                for i in range(4):
                    e_ = e0 + i
                    di = dgc[0] % 8
                    dgc[0] += 1
                    ACT(DG[di][:, :], EYEB[:, :], AF.Copy, r=[r_eyeb, r_gw[bi]], w=[r_dg[di]], scale=GW[:, e_:e_ + 1])
                    for hf in range(2):
                        MM(pb[2 + hf][:, :], DG[di][:, :], GBUF[bi][:, i, 1024 + hf * 512:1024 + (hf + 1) * 512], e_ == 0, e_ == 127,
                           r=[r_dg[di], r_gbuf[bi][i]], w=[rpb[2 + hf]])
            if g == 6 and pending is not None:
                epilogue(pending)
                pending = None
            if g == 20 and ti + 1 < 17:
                make_xtok(ti + 1)
        gcount[0] += NG
        for hf in range(2):
            ACT(OUTS[:, hf * 512:(hf + 1) * 512], pb[2 + hf][:, :], AF.Copy, r=[rpb[2 + hf]], w=[r_outs])
        pending = ti
    epilogue(pending)
    P.emit(nc)
    return nc


def _consts():
    c = np.zeros((128, NCST), np.float32)
    c[:, 0:128] = np.eye(128, dtype=np.float32)
    k = np.arange(128)[:, None]
    qq = np.arange(128)[None, :]
    c[:, 128:256] = 1.0 - ((qq < 64) & (k >= 64))
    c[:, 256:384] = 1.0 - ((qq >= 64) & (k < 64))
    s = np.arange(64)[:, None]
    t = np.arange(64)[None, :]
    m64 = np.where(s <= t, 0.0, -30000.0).astype(np.float32)
    c[0:64, 384:640] = np.tile(m64, (1, 4))
    s = np.arange(32)[:, None]
    t = np.arange(32)[None, :]
    m32 = np.where(s <= t, 0.0, -30000.0).astype(np.float32)
    c[0:32, 640:768] = np.tile(m32, (1, 4))
    for h in range(4):
        c[h, 768 + h * 64:768 + (h + 1) * 64] = 1.0
        c[h, 1024 + h * 32:1024 + (h + 1) * 32] = 1.0
    c[:, 1152:1168] = np.arange(16, dtype=np.float32)[None, :]
    c[:, 1168:1184] = 16.0 * np.arange(16, dtype=np.float32)[None, :]
    return c


def _fm(a):
    return np.ascontiguousarray(a.reshape(a.shape[0], 8, 128).transpose(2, 1, 0))


def _fm_inv(a):
    return np.ascontiguousarray(a.transpose(2, 1, 0).reshape(a.shape[2], 1024))


_NC_CACHE = {}


def kernel(x_prompt, x_sample, mem_prompt, cache_a_k, cache_a_v, state_b_conv, state_b_C, state_b_n,
           state_b_m, cache_mem_k, cache_mem_v, g_mix, w_in, conv_w, conv_b, b_if, g_head, rel_bias,
           w_a_up, w_b_up, w_out, g_mem, w_mk, w_mv, g_cross, w_cq, w_co, g_ffn, w_pq, sub_keys,
           peer_u, peer_v, g_final):
    f = lambda a: np.asarray(a, dtype=np.float32)
    x_prompt, x_sample, mem_prompt = f(x_prompt), f(x_sample), f(mem_prompt)
    prm = np.zeros((128, 80), np.float32)
    for i, g in enumerate([g_mix[0], g_cross[0], g_ffn[0], g_final, g_mem[0], conv_b[0]]):
        prm[:, i * 8:(i + 1) * 8] = f(g).reshape(8, 128).T
    cw = f(conv_w[0])
    prm[:, 48:80] = cw.reshape(4, 8, 128).transpose(2, 1, 0).reshape(128, 32)
    gb4 = np.ascontiguousarray(f(b_if[0]).reshape(2, 4).T)
    shared = dict(
        w_in=f(w_in[0]), w_a_up=f(w_a_up[0]), w_b_up=f(w_b_up[0]), w_out=f(w_out[0]), w_mk=f(w_mk[0]),
        w_mv=f(w_mv[0]), w_cq=f(w_cq[0]), w_co=f(w_co[0]), w_pq=f(w_pq[0]),
        subT=np.ascontiguousarray(f(sub_keys[0]).reshape(16, 128, 128).transpose(2, 0, 1)),
        prm=prm, gb4=gb4, ghead=f(g_head[0]).reshape(1, 512), relb=f(rel_bias[0]), cst=_consts(),
    )
    if PHASES >= 6:
        shared['peer_uv'] = np.ascontiguousarray(np.concatenate([f(peer_u[0]), f(peer_v[0])], axis=1))
    in_maps = []
    for c in range(NCORES):
        ss = [2 * c, 2 * c + 1]
        X = np.concatenate([x_prompt[c], x_sample[ss[0]], x_sample[ss[1]]], axis=0)
        ck = f(cache_a_k[0])[ss]
        ckT_ = np.ascontiguousarray(ck.reshape(2, 512, 4, 128).transpose(0, 3, 2, 1))
        cmk = f(cache_mem_k[0])[ss]
        cmkT_ = np.ascontiguousarray(cmk.reshape(2, 256, 8, 128).transpose(0, 3, 2, 1))
        conv = f(state_b_conv[0])[ss]
        convT_ = np.ascontiguousarray(conv.reshape(2, 3, 8, 128).transpose(3, 2, 0, 1))
        d = dict(shared)
        d.update(
            xT=_fm(X), memT=_fm(mem_prompt[c]), ckT=ckT_,
            cav=np.ascontiguousarray(f(cache_a_v[0])[ss].reshape(2, 512, 512)),
            convT=convT_, C0=np.ascontiguousarray(f(state_b_C[0])[ss]),
            n0T=np.ascontiguousarray(f(state_b_n[0])[ss].transpose(2, 0, 1)),
            m0T=np.ascontiguousarray(f(state_b_m[0])[ss].reshape(2, 4, 1)),
            cmkT=cmkT_, cmv=np.ascontiguousarray(f(cache_mem_v[0])[ss].reshape(2, 256, 1024)),
        )
        in_maps.append(d)
    if "nc" not in _NC_CACHE:
        _NC_CACHE["nc"] = build_program()
    nc = _NC_CACHE["nc"]
    res = run_bass_kernel_spmd(nc, in_maps, core_ids=list(range(NCORES)))
    R = res.results
    y_prompt = np.zeros((8, 2048, 1024), np.float32)
    y_sample = np.zeros((16, 32, 1024), np.float32)
    p_a_k = np.zeros((1, 8, 512, 8, 64), np.float32)
    p_a_v = np.zeros((1, 8, 512, 8, 64), np.float32)
    p_b_conv = np.zeros((1, 8, 3, 1024), np.float32)
    p_b_C = np.zeros((1, 8, 4, 128, 128), np.float32)
    p_b_n = np.zeros((1, 8, 4, 128), np.float32)
    p_b_m = np.zeros((1, 8, 4), np.float32)
    p_mem_k = np.zeros((1, 8, 256, 4, 256), np.float32)
    p_mem_v = np.zeros((1, 8, 256, 4, 256), np.float32)
    s_a_k = np.zeros((1, 16, 32, 8, 64), np.float32)
    s_a_v = np.zeros((1, 16, 32, 8, 64), np.float32)
    s_b_conv = np.zeros((1, 16, 3, 1024), np.float32)
    s_b_C = np.zeros((1, 16, 4, 128, 128), np.float32)
    s_b_n = np.zeros((1, 16, 4, 128), np.float32)
    s_b_m = np.zeros((1, 16, 4), np.float32)
    for c in range(NCORES):
        r = R[c]
        y = _fm_inv(r["yT"])
        y_prompt[c] = y[:2048]
        y_sample[2 * c] = y[2048:2080]
        y_sample[2 * c + 1] = y[2080:2112]
        p_a_k[0, c] = r["pakT"].transpose(2, 1, 0).reshape(512, 8, 64)
        p_a_v[0, c] = r["pav"].reshape(512, 8, 64)
        p_b_conv[0, c] = r["pconvT"].transpose(2, 1, 0).reshape(3, 1024)
        p_b_C[0, c] = r["pC"].transpose(1, 0, 2)
        p_b_n[0, c] = r["pn"].T
        p_b_m[0, c] = r["pm"][:, 0]
        p_mem_k[0, c] = r["pmkT"].transpose(2, 1, 0).reshape(256, 4, 256)
        p_mem_v[0, c] = r["pmv"].reshape(256, 4, 256)
        sak = r["sakT"].transpose(2, 1, 0).reshape(64, 8, 64)
        sav = r["sav"].reshape(64, 8, 64)
        for s in range(2):
            s_a_k[0, 2 * c + s] = sak[s * 32:(s + 1) * 32]
            s_a_v[0, 2 * c + s] = sav[s * 32:(s + 1) * 32]
            s_b_conv[0, 2 * c + s] = r["sconvT"][:, :, s, :].transpose(2, 1, 0).reshape(3, 1024)
            s_b_C[0, 2 * c + s] = r["sC"][:, s].transpose(1, 0, 2)
            s_b_n[0, 2 * c + s] = r["sn"][:, s, :].T
            s_b_m[0, 2 * c + s] = r["sm"][s, :, 0]
    return (y_prompt, y_sample, p_a_k, p_a_v, p_b_conv, p_b_C, p_b_n, p_b_m, p_mem_k, p_mem_v,
            s_a_k, s_a_v, s_b_conv, s_b_C, s_b_n, s_b_m)
```

```python
import numpy as np
from contextlib import ExitStack
import concourse.bass as bass
import concourse.mybir as mybir
from concourse.bass_utils import run_bass_kernel_spmd

F32 = mybir.dt.float32
BF16 = mybir.dt.bfloat16
U32 = mybir.dt.uint32
I32 = mybir.dt.int32
AF = mybir.ActivationFunctionType
ALU = mybir.AluOpType
AX = mybir.AxisListType

T = 2112
TP = 2048
EPS = 1e-6
LTOE = 640
NCST = 1184
PHASES = 9
NCORES = 8
DBG = ''


class Res:
    __slots__ = ("name", "w", "rs", "excl")

    def __init__(self, name="", excl=False):
        self.name = name
        self.w = None
        self.rs = []
        self.excl = excl


class _Op:
    __slots__ = ("eng", "fn", "deps", "dma", "sig", "dsem", "dval")


class Prog:
    STREAMS = ("pe", "act", "dve", "pool", "sp")
    NS = 12

    def __init__(self):
        self.ops = []
        self.last = {s: None for s in self.STREAMS}
        self.dmas = []
        self.pend = {s: set() for s in self.STREAMS}

    def op(self, eng, fn, r=(), w=(), dma=False):
        i = len(self.ops)
        deps = set(self.pend[eng])
        self.pend[eng] = set()
        xr = [res for res in r if res.excl]
        if xr:
            r = [res for res in r if not res.excl]
            w = list(w) + [res for res in xr if res not in w]
        for res in r:
            if res.w is not None:
                deps.add(res.w)
        for res in w:
            if res.w is not None:
                deps.add(res.w)
            deps.update(res.rs)
        o = _Op()
        o.eng, o.fn, o.deps, o.dma, o.sig, o.dsem, o.dval = eng, fn, deps, dma, None, None, None
        self.ops.append(o)
        for res in r:
            res.rs.append(i)
        for res in w:
            res.w = i
            res.rs = []
        self.last[eng] = i
        if dma:
            self.dmas.append(i)
        return i

    def dma(self, eng, fn, r=(), w=()):
        return self.op(eng, fn, r, w, dma=True)

    def barrier(self):
        deps = set(self.dmas)
        for s in self.STREAMS:
            if self.last[s] is not None:
                deps.add(self.last[s])
        self.dmas = []
        for s in self.STREAMS:
            self.pend[s] |= deps

    def emit(self, nc):
        ops = self.ops
        for o in ops:
            if o.eng == "pe" and not o.dma:
                o.deps = {d for d in o.deps if not (ops[d].eng == "pe" and not ops[d].dma)}
        needed = set()
        for o in ops:
            needed.update(o.deps)
        with ExitStack() as st:
            esem = {s: st.enter_context(nc.semaphore("e_" + s)) for s in self.STREAMS}
            dsem = {s: [st.enter_context(nc.semaphore("d_%s%d" % (s, k))) for k in range(self.NS)]
                    for s in ("act", "pool", "sp")}
            cnt = {s: 0 for s in self.STREAMS}
            dcnt = {s: 0 for s in self.STREAMS}
            per = {s: [] for s in self.STREAMS}
            for i, o in enumerate(ops):
                per[o.eng].append(i)
                if o.dma:
                    k = dcnt[o.eng]
                    dcnt[o.eng] += 1
                    o.dsem = dsem[o.eng][k % self.NS]
                    o.dval = 16 * (k // self.NS + 1)
                elif i in needed:
                    cnt[o.eng] += 1
                    o.sig = cnt[o.eng]
            engobj = {"pe": "tensor", "act": "scalar", "dve": "vector", "pool": "gpsimd", "sp": "sync"}

            def run_stream(s, e):
                known = {}
                final = {}
                for i in per[s]:
                    o = ops[i]
                    waits = {}
                    for d in o.deps:
                        od = ops[d]
                        if od.dma:
                            sem, val = od.dsem, od.dval
                        else:
                            sem, val = esem[od.eng], od.sig
                        if waits.get(sem, 0) < val:
                            waits[sem] = val
                    if o.dma and o.dval > 16:
                        if waits.get(o.dsem, 0) < o.dval - 16:
                            waits[o.dsem] = o.dval - 16
                    for sem, val in waits.items():
                        if known.get(sem, 0) < val:
                            e.wait_ge(sem, val)
                            known[sem] = val
                    ins = o.fn(e)
                    if o.dma:
                        ins.then_inc(o.dsem, 16)
                        final[o.dsem] = o.dval
                    elif o.sig is not None:
                        ins.then_inc(esem[s], 1)
                for sem, val in final.items():
                    if known.get(sem, 0) < val:
                        e.wait_ge(sem, val)

            with nc.Block() as block:
                for s in self.STREAMS:
                    if not per[s]:
                        continue
                    getattr(block, engobj[s])(lambda e, s=s: run_stream(s, e))
        return cnt, dcnt


def build_program():
    nc = bass.Bass("TRN2", target_bir_lowering=False)
    P = Prog()
    SB_LO = 20480
    ncnt = [0]

    def din(name, shape, dt=F32):
        return nc.dram_tensor(name, list(shape), dt, kind="ExternalInput").ap()

    def dout(name, shape, dt=F32):
        return nc.dram_tensor(name, list(shape), dt, kind="ExternalOutput").ap()

    def sb(off, shape, dt):
        ncnt[0] += 1
        return nc.alloc_sbuf_tensor_at("t%d" % ncnt[0], list(shape), dt, offset=SB_LO + off)

    xT = din("xT", [128, 8, T])
    memT = din("memT", [128, 8, 256])
    ckT = din("ckT", [2, 128, 4, 512])
    cav = din("cav", [2, 512, 512])
    convT = din("convT", [128, 8, 2, 3])
    C0d = din("C0", [2, 4, 128, 128])
    n0T = din("n0T", [128, 2, 4])
    m0T = din("m0T", [2, 4, 1])
    cmkT = din("cmkT", [2, 128, 8, 256])
    cmv = din("cmv", [2, 256, 1024])
    w_in = din("w_in", [1024, 5640])
    w_a_up = din("w_a_up", [512, 1024])
    w_b_up = din("w_b_up", [512, 1024])
    w_out = din("w_out", [1024, 1024])
    w_mk = din("w_mk", [1024, 1024])
    w_mv = din("w_mv", [1024, 1024])
    w_cq = din("w_cq", [1024, 1024])
    w_co = din("w_co", [1024, 1024])
    w_pq = din("w_pq", [1024, 2048])
    subT = din("subT", [128, 16, 128])
    peer_uv = din("peer_uv", [16384, 2048]) if PHASES >= 6 else None
    prm = din("prm", [128, 80])
    gb4 = din("gb4", [4, 2])
    ghead = din("ghead", [1, 512])
    relb = din("relb", [8, 257])
    cst = din("cst", [128, NCST])

    o_yT = dout("yT", [128, 8, T])
    o_pakT = dout("pakT", [128, 4, 512])
    o_pav = dout("pav", [512, 512])
    o_pconvT = dout("pconvT", [128, 8, 3])
    o_pC = dout("pC", [128, 4, 128])
    o_pn = dout("pn", [128, 4])
    o_pm = dout("pm", [4, 1])
    o_pmkT = dout("pmkT", [128, 8, 256])
    o_pmv = dout("pmv", [256, 1024])
    o_sakT = dout("sakT", [128, 4, 64])
    o_sav = dout("sav", [64, 512])
    o_sconvT = dout("sconvT", [128, 8, 2, 3])
    o_sC = dout("sC", [128, 2, 4, 128])
    o_sn = dout("sn", [128, 2, 4])
    o_sm = dout("sm", [2, 4, 1])
    Rtoe = nc.dram_tensor("Rtoe", [8, 128, LTOE], F32, kind="Internal").ap()
    r_Rtoe = Res()

    def MM(out, lhsT, rhs, st, sp, r, w, sgc=False, tp=None):
        if tp is None:
            P.op("pe", lambda e: e.matmul(out, lhsT=lhsT, rhs=rhs, start=st, stop=sp, skip_group_check=sgc), r=r, w=w)
        else:
            P.op("pe", lambda e: e.matmul(out, lhsT=lhsT, rhs=rhs, start=st, stop=sp, skip_group_check=sgc, tile_position=tp), r=r, w=w)

    def ACT(out, in_, func, r, w, **kw):
        P.op("act", lambda e: e.activation(out=out, in_=in_, func=func, **kw), r=r, w=w)

    def V(name, r, w, eng="dve", **kw):
        P.op(eng, lambda e: getattr(e, name)(**kw), r=r, w=w)

    def DMA(out, in_, r=(), w=(), eng="sp"):
        P.dma(eng, lambda e: e.dma_start(out=out, in_=in_), r=r, w=w)

    pb = [nc.alloc_psum_tensor("pb%d" % i, [128, 512], F32) for i in range(8)]
    rpb = [Res("pb%d" % i, excl=True) for i in range(8)]

    o = 0
    CST = sb(o, [128, NCST], F32); o += NCST * 4
    PRM = sb(o, [128, 80], F32); o += 320
    EYEB = sb(o, [128, 128], BF16); o += 256
    ONESB = sb(o, [128, 128], BF16); o += 256
    EB0 = sb(o, [128, 8, 128], BF16); o += 2048
    EB1 = sb(o, [128, 8, 128], BF16); o += 2048
    EBC = sb(o, [128, 8, 128], BF16); o += 2048
    EB4 = sb(o, [128, 8, 128], BF16); o += 2048
    GHD = sb(o, [64, 512], F32); o += 2048
    GB4 = sb(o, [4, 2], F32); o += 32
    MKT = sb(o, [128, 8, 256], BF16); o += 4096
    MV = sb(o, [128, 2, 1024], BF16); o += 4096
    SQ_OFF = o
    SQ = sb(o, [128, 8, 512], BF16); o += 8192
    RS = sb(o, [128, 512], F32); o += 2048
    STG_OFF = o
    STG = [sb(o, [128, 512], F32), sb(o + 2048, [128, 512], F32)]; o += 4096
    WA_OFF = o
    WA = sb(o, [128, 8, 512], BF16); o += 8192
    WB = sb(o, [128, 8, 512], BF16); o += 8192
    XN = sb(o, [128, 8, T], BF16); o += 8 * T * 2
    MIX_OFF = o
    MIX = sb(o, [128, 8, T], BF16); o += 8 * T * 2
    PH = o
    r_cst, r_prm, r_eyeb, r_ones = Res(), Res(), Res(), Res()
    r_eb = Res()
    r_ghd, r_gb4, r_mkt, r_mv, r_sq, r_rs = Res(), Res(), Res(), Res(), Res(), Res()
    r_stg = [Res(), Res()]
    r_wa, r_wb, r_xn, r_mix = Res(), Res(), Res(), Res()
    stg_i = [0]

    EYE = CST[:, 0:128]
    MASK0 = CST[:, 128:256]
    MASK4 = CST[:, 256:384]
    MNEG64 = CST[0:64, 384:640]
    MNEG32 = CST[0:32, 640:768]
    BD64 = CST[0:4, 768:1024]
    BD32 = CST[0:4, 1024:1152]
    IOTA16 = CST[:, 1152:1168]
    IOTA256 = CST[:, 1168:1184]

    def out_stage(src_psum, rsrc, dram_ap, rows=128, cols=512):
        i = stg_i[0] % 2
        stg_i[0] += 1
        V("tensor_copy", r=[rsrc], w=[r_stg[i]], out=STG[i][0:rows, 0:cols], in_=src_psum)
        DMA(dram_ap, STG[i][0:rows, 0:cols], r=[r_stg[i]])

    def load_w(dst, rdst, src2d, c0, ncols, kchunks=8, part=128):
        src = src2d[:, c0:c0 + ncols].rearrange("(k p) n -> p k n", p=part)
        P.dma("pool", lambda e: e.dma_start(out=dst, in_=src), w=[rdst])

    DMA(CST[:], cst, w=[r_cst])
    DMA(PRM[:], prm, w=[r_prm])
    DMA(GB4[:], gb4, w=[r_gb4])
    DMA(GHD[:], ghead.to_broadcast([64, 512]), w=[r_ghd])
    V("tensor_copy", r=[r_cst], w=[r_eyeb], out=EYEB[:], in_=EYE)
    V("memset", r=[], w=[r_ones], ap=ONESB[:], constant=1.0)
    TB = sb(PH, [8, LTOE], F32); r_tb = Res()
    EBS = sb(PH + LTOE * 4, [128, 3, 8, 128], F32); r_ebs = Res()
    DMA(TB[:, 0:257], relb, w=[r_tb])
    V("tensor_copy", r=[r_tb], w=[r_tb], out=TB[:, 257:LTOE], in_=TB[:, 256:257].to_broadcast([8, LTOE - 257]))
    DMA(Rtoe, TB[:].unsqueeze(1).to_broadcast([8, 128, LTOE]), r=[r_tb], w=[r_Rtoe])
    for d in range(3):
        off = 128 + d * 128
        DMA(EBS[:, d, :, :], bass.AP(Rtoe.tensor, off, [[LTOE - 1, 128], [128 * LTOE, 8], [1, 128]]), r=[r_Rtoe], w=[r_ebs])
    ACT(EB0[:], EBS[:, 0, :, :], AF.Exp, r=[r_ebs], w=[r_eb])
    ACT(EB1[:], EBS[:, 1, :, :], AF.Exp, r=[r_ebs], w=[r_eb])
    ACT(EBC[:], EBS[:, 2, :, :], AF.Exp, r=[r_ebs], w=[r_eb])
    V("tensor_tensor", r=[r_eb, r_cst], w=[r_eb], out=EB4[:], in0=EBC[:], in1=MASK4.unsqueeze(1).to_broadcast([128, 8, 128]), op=ALU.mult)
    V("tensor_tensor", r=[r_eb, r_cst], w=[r_eb], out=EB0[:], in0=EB0[:], in1=MASK0.unsqueeze(1).to_broadcast([128, 8, 128]), op=ALU.mult)
    P.barrier()
    if PHASES <= 0:
        P.emit(nc)
        return nc
    def rmsnorm(src, rsrc, dst, rdst, gcol, n, bank=7):
        for k in range(8):
            ACT(SQ[:, k, 0:n], src(k), AF.Square, r=[rsrc], w=[r_sq])
        for k in range(8):
            MM(pb[bank][:, 0:n], ONESB[:], SQ[:, k, 0:n], k == 0, k == 7, r=[r_sq, r_ones], w=[rpb[bank]])
        ACT(RS[:, 0:n], pb[bank][:, 0:n], AF.Ln, r=[rpb[bank]], w=[r_rs], scale=1.0 / 1024, bias=EPS)
        ACT(RS[:, 0:n], RS[:, 0:n], AF.Exp, r=[r_rs], w=[r_rs], scale=-0.5)
        for k in range(8):
            V("scalar_tensor_tensor", r=[rsrc, r_rs, r_prm], w=[rdst], out=dst(k), in0=src(k),
              scalar=PRM[:, gcol + k:gcol + k + 1], in1=RS[:, 0:n], op0=ALU.mult, op1=ALU.mult)

    GROUPS = [(0, 512), (512, 512), (1024, 512), (1536, 512), (2048, 64)]

    MEMF = sb(MIX_OFF, [128, 8, 256], F32); r_memf = Res()
    MN = sb(MIX_OFF + 8192, [128, 8, 256], BF16); r_mn = Res()
    DMA(MEMF[:], memT, w=[r_memf])
    rmsnorm(lambda k: MEMF[:, k, :], r_memf, lambda k: MN[:, k, :], r_mn, 32, 256)
    if PHASES <= 0.5:
        P.emit(nc)
        return nc
    load_w(WA[:], r_wa, w_mk, 0, 512)
    load_w(WB[:], r_wb, w_mk, 512, 512)
    for fc in range(8):
        W, rW = (WA, r_wa) if fc < 4 else (WB, r_wb)
        c = (fc % 4) * 128
        b = fc % 2
        for k in range(8):
            MM(pb[b][:, 0:256], W[:, k, c:c + 128], MN[:, k, :], k == 0, k == 7, r=[rW, r_mn], w=[rpb[b]])
        if 'a' not in DBG:
            ACT(MKT[:, fc, :], pb[b][:, 0:256], AF.Copy, r=[rpb[b]], w=[r_mkt])
        if 'b' not in DBG:
            out_stage(pb[b][:, 0:256], rpb[b], o_pmkT[:, fc, :], cols=256)
    if PHASES <= 0.7:
        P.emit(nc)
        return nc
    load_w(WA[:], r_wa, w_mv, 0, 512)
    load_w(WB[:], r_wb, w_mv, 512, 512)
    for mt in range(2):
        for half in range(2):
            W, rW = (WA, r_wa) if half == 0 else (WB, r_wb)
            b = half
            for k in range(8):
                MM(pb[b][:, :], MN[:, k, mt * 128:(mt + 1) * 128], W[:, k, :], k == 0, k == 7, r=[rW, r_mn], w=[rpb[b]])
            ACT(MV[:, mt, half * 512:(half + 1) * 512], pb[b][:, :], AF.Copy, r=[rpb[b]], w=[r_mv])
            out_stage(pb[b][:, :], rpb[b], o_pmv[mt * 128:(mt + 1) * 128, half * 512:(half + 1) * 512])
    r_mix_guard = [r_memf, r_mn]
    if PHASES <= 1:
        P.emit(nc)
        return nc

    q = PH
    KA = sb(q, [128, 4, T], BF16); q += 4 * T * 2
    VA = sb(q, [128, 18, 512], BF16); q += 18 * 1024
    QZ = [sb(q, [128, 4, 512], BF16), sb(q + 4096, [128, 4, 512], BF16)]; q += 8192
    XG = sb(q, [128, 8, 512], F32); r_xg = Res()
    WAU = sb(q, [64, 8, 1024], BF16); q += 16384
    OA = sb(q, [64, 8, 512], BF16); q += 8192
    EE = [sb(q, [128, 512], BF16), sb(q + 1024, [128, 512], BF16)]; q += 2048
    RD = sb(q, [64, 512], F32); q += 2048
    r_rd_a = Res()
    CK = sb(q, [128, 4, 512], BF16); q += 4096
    CV = sb(q, [128, 4, 512], BF16); q += 4096
    RD2 = [RD, sb(q, [64, 512], F32)]; q += 2048
    r_rd2 = [r_rd_a, Res()]
    assert SB_LO + q <= 229376, q
    r_ka, r_va, r_qa, r_wau, r_oa, r_rd, r_ck, r_cv = Res(), Res(), Res(), Res(), Res(), Res(), Res(), Res()
    r_ee = [Res(), Res()]

    for (t0, n) in GROUPS:
        DMA(XG[:, :, 0:n], xT[:, :, t0:t0 + n], w=[r_xg])
        rmsnorm(lambda k: XG[:, k, 0:n], r_xg, lambda k: XN[:, k, t0:t0 + n], r_xn, 0, n)
    P.barrier()
    if PHASES <= 1.3:
        P.emit(nc)
        return nc
    load_w(WA[:], r_wa, w_in, 512, 512)
    load_w(WB[:], r_wb, w_in, 1024, 512)
    for gi, (t0, n) in enumerate(GROUPS):
        for pr in range(4):
            b = pr % 2
            for k in range(8):
                MM(pb[b][:, 0:n], WA[:, k, pr * 128:(pr + 1) * 128], XN[:, k, t0:t0 + n], k == 0, k == 7, r=[r_wa, r_xn], w=[rpb[b]])
            ACT(KA[:, pr, t0:t0 + n], pb[b][:, 0:n], AF.Copy, r=[rpb[b]], w=[r_ka])
            if gi == 3:
                out_stage(pb[b][:, 0:n], rpb[b], o_pakT[:, pr, :])
            if gi == 4:
                out_stage(pb[b][:, 0:n], rpb[b], o_sakT[:, pr, :], cols=64)
        ntile = 4 if gi < 4 else 2
        for j in range(ntile):
            ti = gi * 4 + j
            rows = 128 if gi < 4 else 32
            tk0 = ti * 128 if gi < 4 else 2048 + j * 32
            b = 2 + j % 2
            for k in range(8):
                MM(pb[b][0:rows, :], XN[:, k, tk0:tk0 + rows], WB[:, k, :], k == 0, k == 7, r=[r_wb, r_xn], w=[rpb[b]])
            ACT(VA[0:rows, ti, :], pb[b][0:rows, :], AF.Copy, r=[rpb[b]], w=[r_va])
            if 12 <= ti <= 15:
                out_stage(pb[b][:, :], rpb[b], o_pav[(ti - 12) * 128:(ti - 11) * 128, :])
            if ti >= 16:
                out_stage(pb[b][0:32, :], rpb[b], o_sav[(ti - 16) * 32:(ti - 15) * 32, :], rows=32)

    if PHASES <= 1.5:
        P.emit(nc)
        return nc
    load_w(WA[:], r_wa, w_in, 0, 512)
    V("memset", r=[], w=[r_qa], ap=QZ[0][:], constant=0.0)
    V("memset", r=[], w=[r_qa], ap=QZ[1][:], constant=0.0)
    if 'w' not in DBG:
        for hf in range(2):
            P.dma("pool", lambda e, hf=hf: e.dma_start(out=WAU[:, :, hf * 512:(hf + 1) * 512],
                  in_=w_a_up[:, hf * 512:(hf + 1) * 512].rearrange("(h p) n -> p h n", p=64)), w=[r_wau])

    ablk = [0]

    def attn_block(qcols, nq, keys, hg, q_pr_src):
        nkb = len(keys)
        bn, bd = (6, 7) if ablk[0] % 2 == 0 else (2, 3)
        RDt = RD2[ablk[0] % 2]
        r_rdt = r_rd2[ablk[0] % 2]
        ablk[0] += 1

        def scores(idx):
            kfn, nk, vfn, ebfn = keys[idx]
            sb_i = 4 + idx % 2
            for hh in range(4):
                h = hg * 4 + hh
                MM(pb[sb_i][0:nk, hh * 128:hh * 128 + nq], kfn(h), q_pr_src(h), True, True, r=[r_ka, r_qa, r_ck], w=[rpb[sb_i]])

        scores(0)
        for idx, (kfn, nk, vfn, ebfn) in enumerate(keys):
            sb_i = 4 + idx % 2
            if idx + 1 < nkb:
                scores(idx + 1)
            E = EE[idx % 2]
            rE = r_ee[idx % 2]
            Ev = E[0:nk, :].rearrange("p (h q) -> p h q", h=4)[:, :, 0:nq]
            ACT(Ev, pb[sb_i][0:nk, :].rearrange("p (h q) -> p h q", h=4)[:, :, 0:nq], AF.Exp, r=[rpb[sb_i]], w=[rE], scale=0.125)
            V("tensor_tensor", r=[rE, r_eb], w=[rE], out=Ev, in0=Ev, in1=ebfn(hg), op=ALU.mult)
            for hh in range(4):
                h = hg * 4 + hh
                first = (idx == 0 and hh == 0)
                MM(pb[bn][0:64, hh * 128:hh * 128 + nq], vfn(h), E[0:nk, hh * 128:hh * 128 + nq], first, idx == nkb - 1,
                   r=[rE, r_va, r_cv], w=[rpb[bn]], sgc=True)
            for hh in range(4):
                first = (idx == 0 and hh == 0)
                MM(pb[bd][0:64, hh * 128:hh * 128 + nq], ONESB[0:nk, 0:64], E[0:nk, hh * 128:hh * 128 + nq], first, idx == nkb - 1,
                   r=[rE, r_ones], w=[rpb[bd]], sgc=True)
        pn = pb[bn][0:64, :].rearrange("p (h q) -> p h q", h=4)[:, :, 0:nq]
        pd = pb[bd][0:64, :].rearrange("p (h q) -> p h q", h=4)[:, :, 0:nq]
        rdv = RDt[:, :].rearrange("p (h q) -> p h q", h=4)[:, :, 0:nq]
        V("reciprocal", r=[rpb[bd]], w=[r_rdt], out=rdv, in_=pd)
        V("tensor_tensor", r=[rpb[bn], r_rdt], w=[r_oa], out=OA[:, hg * 4:(hg + 1) * 4, qcols:qcols + nq], in0=pn, in1=rdv, op=ALU.mult)

    def up_proj_A(t0, n):
        for fc in range(8):
            b = fc % 2
            for h in range(8):
                MM(pb[b][:, 0:n], WAU[:, h, fc * 128:(fc + 1) * 128], OA[:, h, 0:n], h == 0, h == 7, r=[r_wau, r_oa], w=[rpb[b]])
            ACT(MIX[:, fc, t0:t0 + n], pb[b][:, 0:n], AF.Copy, r=[rpb[b]], w=[r_mix])

    def ebsel(tile, nk, nq):
        return lambda hg: tile[0:nk, hg * 4:(hg + 1) * 4, 0:nq]

    for gi, (t0, n) in enumerate(GROUPS):
        for pr in range(4):
            b = pr % 2
            for k in range(8):
                MM(pb[b][:, 0:n], WA[:, k, pr * 128:(pr + 1) * 128], XN[:, k, t0:t0 + n], k == 0, k == 7, r=[r_wa, r_xn], w=[rpb[b]])
            ACT(QZ[0][0:64, pr, 0:n], pb[b][0:64, 0:n], AF.Copy, r=[rpb[b]], w=[r_qa])
            ACT(QZ[1][64:128, pr, 0:n], pb[b][64:128, 0:n], AF.Copy, r=[rpb[b]], w=[r_qa])
        if gi < 4:
            for j in range(4):
                i = gi * 4 + j
                for hg in range(2):
                    keys = []
                    for kb in range(max(0, i - 4), i + 1):
                        ebt = {0: EB0, 1: EB1, 2: EBC, 3: EBC, 4: EB4}[i - kb]
                        keys.append((lambda h, kb=kb: KA[:, h // 2, kb * 128:(kb + 1) * 128], 128,
                                     lambda h, kb=kb: VA[:, kb, h * 64:(h + 1) * 64], ebsel(ebt, 128, 128)))
                    attn_block(j * 128, 128, keys, hg,
                               lambda h, j=j: QZ[h % 2][:, h // 2, j * 128:(j + 1) * 128])
        else:
            for s in range(2):
                P.dma("pool", lambda e, s=s: e.dma_start(out=CK[:], in_=ckT[s]), w=[r_ck])
                P.dma("pool", lambda e, s=s: e.dma_start(out=CV[:], in_=cav[s].rearrange("(b p) n -> p b n", p=128)), w=[r_cv])
                for hg in range(2):
                    keys = []
                    for kb in range(4):
                        ebt = EB1 if kb == 3 else EBC
                        keys.append((lambda h, kb=kb: CK[:, h // 2, kb * 128:(kb + 1) * 128], 128,
                                     lambda h, kb=kb: CV[:, kb, h * 64:(h + 1) * 64], ebsel(ebt, 128, 32)))
                    tk = 2048 + s * 32
                    keys.append((lambda h, tk=tk: KA[:, h // 2, tk:tk + 32], 32,
                                 lambda h, s=s: VA[0:32, 16 + s, h * 64:(h + 1) * 64], ebsel(EB0, 32, 32)))
                    attn_block(s * 32, 32, keys, hg,
                               lambda h, s=s: QZ[h % 2][:, h // 2, s * 32:s * 32 + 32])
        if gi == 0:
            P.barrier()
        if 'u' not in DBG:
            up_proj_A(t0, n)
        if PHASES <= 1.7 and gi == 0:
            break
        if PHASES <= 1.8 and gi == 3:
            break
    P.barrier()
    if PHASES <= 1.9:
        P.emit(nc)
        return nc
    SG = sb(PH, [128, 512], BF16); r_sg = Res()
    for blk in range(2):
        load_w(WA[:], r_wa, w_in, 3592 + blk * 512, 512)
        for (t0, n) in GROUPS:
            for c4 in range(4):
                fc = blk * 4 + c4
                b = c4 % 2
                for k in range(8):
                    MM(pb[b][:, 0:n], WA[:, k, c4 * 128:(c4 + 1) * 128], XN[:, k, t0:t0 + n], k == 0, k == 7, r=[r_wa, r_xn], w=[rpb[b]])
                ACT(SG[:, 0:n], pb[b][:, 0:n], AF.Sigmoid, r=[rpb[b]], w=[r_sg])
                V("tensor_tensor", r=[r_sg, r_mix], w=[r_mix], out=MIX[:, fc, t0:t0 + n], in0=MIX[:, fc, t0:t0 + n], in1=SG[:, 0:n], op=ALU.mult)
    P.barrier()
    if PHASES <= 2:
        P.emit(nc)
        return nc

    q = PH
    WQK = sb(q, [128, 8, 1024], BF16); q += 16384
    HB = sb(q, [128, 4, T], BF16); q += 4 * T * 2
    STGC = [sb(q, [128, 520], F32), sb(q + 2080, [128, 520], F32)]; q += 4160
    CT_OFF = q
    CT = sb(q, [128, 512], F32); q += 2048
    HH = sb(CT_OFF, [64, 512], F32)
    QB = sb(q, [128, 4, 512], BF16); q += 4096
    KB = sb(q, [128, 4, 512], BF16); q += 4096
    V1 = sb(q, [64, 8, 4, 132], BF16); q += 8 * 4 * 132 * 2
    SOB = SQ
    WG = sb(q, [128, 8, 8], BF16); q += 128
    HALO = sb(q, [128, 8, 4], F32); q += 128
    IGR = sb(q, [4, 512], F32); q += 2048
    NBR = sb(q, [4, 512], F32); q += 2048
    XR = sb(q, [4, 512], F32); q += 2048
    MR = sb(q, [4, 512], F32); q += 2048
    TR = sb(q, [4, 512], F32); q += 2048
    ONE4 = sb(q, [4, 512], F32); q += 2048
    SM = sb(q, [4, 64], F32); q += 256
    XC = sb(q, [4, 64], F32); q += 256
    NMC = sb(q, [4, 64], F32); q += 256
    WSR = sb(q, [4, 64], F32); q += 256
    ENM = sb(q, [4, 64], F32); q += 256
    BDM = sb(q, [4, 256], F32); q += 1024
    CN = sb(q, [128, 4, 132], F32); q += 2112
    CNB = sb(q, [128, 4, 132], BF16); q += 1056
    WT = sb(q, [64, 256], F32); q += 1024
    AT = sb(q, [64, 256], BF16); q += 512
    COLS = sb(q, [64, 12], F32); q += 64
    A0 = sb(q, [128, 4], F32); q += 32
    NT4 = sb(q, [128, 4], F32); q += 32
    r_nt4 = Res()
    HI_OFF = q
    HI = sb(q, [64, 4, 132], F32); q += 2112
    HSQ = sb(HI_OFF, [64, 512], F32)
    ND = sb(q, [64, 4, 132], F32); q += 2112
    DA = sb(q, [64, 4], F32); q += 32
    SS4 = sb(q, [64, 4], F32); q += 32
    HBT = sb(q, [64, 512], BF16); q += 1024
    KS = sb(q, [64, 512], BF16); q += 1024
    XCs = [XC, sb(q, [4, 64], F32)]; q += 256
    NMCs = [NMC, sb(q, [4, 64], F32)]; q += 256
    WSRs = [WSR, sb(q, [4, 64], F32)]; q += 256
    ENMs = [ENM, sb(q, [4, 64], F32)]; q += 256
    BDMs = [BDM, sb(q, [4, 256], F32)]; q += 1024
    D4s = [sb(q, [4, 8], F32), sb(q + 32, [4, 8], F32)]; q += 64
    WTs = [WT, sb(q, [64, 256], F32)]; q += 1024
    ATs = [AT, sb(q, [64, 256], BF16)]; q += 512
    COLSs = [COLS, sb(q, [64, 12], F32)]; q += 64
    A0s = [A0, sb(q, [128, 4], F32)]; q += 32
    r_crow2, r_wt2, r_at2, r_cols2 = [Res(), Res()], [Res(), Res()], [Res(), Res()], [Res(), Res()]
    assert SB_LO + q <= 229376, q
    (r_wqk, r_hb, r_ct, r_qb, r_kb, r_v1, r_sob, r_wg, r_halo, r_rows, r_sm, r_crow, r_bdm, r_cn, r_cnb, r_wt,
     r_at, r_cols, r_a0, r_hi, r_nd, r_da, r_hh, r_hsq, r_hbt, r_ks, r_one4) = [Res() for _ in range(27)]
    r_stgc = [Res(), Res()]
    r_hsq = r_hi
    r_hh = r_ct
    SOBv = SOB[0:64, :, :]
    EYE4 = CST[0:4, 0:4]
    r_wqk2 = [Res(), Res()]
    load_w(WQK[:, :, 0:512], r_wqk2[0], w_in, 1536, 512)
    load_w(WQK[:, :, 512:1024], r_wqk2[1], w_in, 2048, 512)
    load_w(WA[:], r_wa, w_in, 2560, 512)
    load_w(WB[:], r_wb, w_in, 3072, 512)
    P.dma("pool", lambda e: e.dma_start(out=WG[:], in_=w_in[:, 3584:3592].rearrange("(k p) n -> p k n", p=128)), w=[r_wg])
    V("memset", r=[], w=[r_one4], ap=ONE4[:], constant=1.0)
    V("memset", r=[], w=[r_v1], ap=V1[:], constant=1.0)
    V("memset", r=[], w=[r_cn], ap=CN[:], constant=0.0)
    V("memset", r=[], w=[r_cnb], ap=CNB[:], constant=0.0)
    V("memset", r=[], w=[r_halo], ap=HALO[:], constant=0.0)
    V("memset", r=[], w=[r_sm], ap=SM[:], constant=0.0)
    SC = 128 ** -0.5

    def mlstm_group(t0, n, L, first, conv_out):
        nch = n // L
        BD = BD64 if L == 64 else BD32
        MNEG = MNEG64 if L == 64 else MNEG32
        for gi_, dst in ((0, IGR), (1, TR)):
            for k in range(8):
                MM(pb[0][0:4, 0:n], WG[:, k, gi_ * 4:gi_ * 4 + 4], XN[:, k, t0:t0 + n], k == 0, k == 7, r=[r_wg, r_xn], w=[rpb[0]])
            ACT(dst[:, 0:n], pb[0][0:4, 0:n], AF.Identity, r=[rpb[0], r_gb4], w=[r_rows], bias=GB4[:, gi_:gi_ + 1], scale=1.0)
        ACT(TR[:, 0:n], TR[:, 0:n], AF.Exp, r=[r_rows], w=[r_rows], scale=-1.0)
        ACT(TR[:, 0:n], TR[:, 0:n], AF.Ln, r=[r_rows], w=[r_rows], bias=1.0, scale=1.0)
        V("tensor_tensor_scan", r=[r_rows, r_one4, r_sm], w=[r_rows], out=NBR[:, 0:n], data0=ONE4[:, 0:n], data1=TR[:, 0:n],
          initial=(0.0 if first else SM[:, 0:1]), op0=ALU.mult, op1=ALU.add)
        V("tensor_tensor", r=[r_rows], w=[r_rows], out=XR[:, 0:n], in0=IGR[:, 0:n], in1=NBR[:, 0:n], op=ALU.add)
        V("tensor_tensor_scan", r=[r_rows, r_sm], w=[r_rows], out=MR[:, 0:n], data0=XR[:, 0:n], data1=XR[:, 0:n],
          initial=SM[:, 1:2], op0=ALU.max, op1=ALU.max)
        for fc in range(8):
            b = 1 + fc % 2
            sg, rsg = STGC[fc % 2], r_stgc[fc % 2]
            for k in range(8):
                MM(pb[b][:, 0:n], WQK[:, k, fc * 128:(fc + 1) * 128], XN[:, k, t0:t0 + n], k == 0, k == 7, r=[r_wqk2[fc // 4], r_xn], w=[rpb[b]])
            V("tensor_copy", r=[r_halo], w=[rsg], out=sg[:, 0:3], in_=HALO[:, fc, 0:3])
            ACT(sg[:, 3:3 + n], pb[b][:, 0:n], AF.Copy, r=[rpb[b]], w=[rsg])
            V("tensor_copy", r=[rsg], w=[r_halo], out=HALO[:, fc, 0:3], in_=sg[:, n:n + 3])
            V("tensor_scalar", r=[rsg, r_prm], w=[r_ct], out=CT[:, 0:n], in0=sg[:, 0:n], scalar1=PRM[:, 48 + fc * 4:49 + fc * 4],
              scalar2=PRM[:, 40 + fc:41 + fc], op0=ALU.mult, op1=ALU.add)
            for j in range(1, 4):
                V("scalar_tensor_tensor", r=[rsg, r_prm, r_ct], w=[r_ct], out=CT[:, 0:n], in0=sg[:, j:j + n],
                  scalar=PRM[:, 48 + fc * 4 + j:49 + fc * 4 + j], in1=CT[:, 0:n], op0=ALU.mult, op1=ALU.add)
            if fc < 4:
                ACT(QB[:, fc, 0:n], CT[:, 0:n], AF.Silu, r=[r_ct], w=[r_qb])
            else:
                ACT(CT[:, 0:n], CT[:, 0:n], AF.Silu, r=[r_ct], w=[r_ct])
                V("tensor_scalar", r=[r_ct], w=[r_kb], out=KB[:, fc - 4, 0:n], in0=CT[:, 0:n], scalar1=SC, scalar2=None, op0=ALU.mult)
        if conv_out is not None:
            DMA(conv_out, HALO[:, :, 0:3], r=[r_halo])
        for c in range(nch):
            tc = t0 + c * L
            for k in range(8):
                MM(pb[1][0:L, :], XN[:, k, tc:tc + L], WA[:, k, :], k == 0, k == 7, r=[r_wa, r_xn], w=[rpb[1]])
            ACT(V1[0:L, c, :, 0:128], pb[1][0:L, :].rearrange("p (h d) -> p h d", h=4), AF.Copy, r=[rpb[1]], w=[r_v1])
            for k in range(8):
                MM(pb[2][0:L, :], XN[:, k, tc:tc + L], WB[:, k, :], k == 0, k == 7, r=[r_wb, r_xn], w=[rpb[2]])
            ACT(SOBv[0:L, c, :], pb[2][0:L, :], AF.Sigmoid, r=[rpb[2]], w=[r_sob])
        def front_a(c):
            pc = c % 2
            cs = slice(c * L, (c + 1) * L)
            Mp = SM[:, 1:2] if c == 0 else MR[:, c * L - 1:c * L]
            xc, nmc, wsr, enm, bdm, d4 = XCs[pc], NMCs[pc], WSRs[pc], ENMs[pc], BDMs[pc], D4s[pc]
            rc = r_crow2[pc]
            V("tensor_scalar", r=[r_rows, r_sm], w=[rc], out=xc[:, 0:L], in0=XR[:, cs], scalar1=Mp, scalar2=None, op0=ALU.subtract)
            V("tensor_scalar", r=[r_rows, r_sm], w=[rc], out=nmc[:, 0:L], in0=MR[:, cs], scalar1=Mp, scalar2=-1.0, op0=ALU.subtract, op1=ALU.mult)
            V("tensor_tensor", r=[rc, r_cst], w=[rc], out=bdm[:, 0:4 * L].rearrange("p (h t) -> p h t", h=4),
              in0=BD.rearrange("p (h t) -> p h t", h=4), in1=nmc[:, 0:L].unsqueeze(1).to_broadcast([4, 4, L]), op=ALU.mult)
            V("tensor_scalar", r=[rc], w=[rc], out=wsr[:, 0:L], in0=xc[:, 0:L], scalar1=nmc[:, L - 1:L], scalar2=None, op0=ALU.add)
            V("tensor_tensor", r=[r_rows], w=[rc], out=enm[:, 0:L], in0=NBR[:, cs], in1=MR[:, cs], op=ALU.subtract)
            V("tensor_scalar", r=[rc, r_cst], w=[rc], out=d4[:, 0:4], in0=EYE4, scalar1=nmc[:, L - 1:L], scalar2=None, op0=ALU.mult)
            MM(pb[0][0:L, 0:4 * L], xc[:, 0:L], BD, True, False, r=[rc, r_cst], w=[rpb[0]])
            MM(pb[0][0:L, 0:4 * L], ONE4[:, 0:L], bdm[:, 0:4 * L], False, False, r=[r_one4, rc], w=[rpb[0]])
            MM(pb[0][0:L, 0:4 * L], EYE[0:L, 0:L], MNEG, False, True, r=[r_cst], w=[rpb[0]])
            ACT(WTs[pc][0:L, 0:4 * L], pb[0][0:L, 0:4 * L], AF.Exp, r=[rpb[0]], w=[r_wt2[pc]])
            MM(pb[7][0:L, 0:4], nmc[:, 0:L], EYE4, True, True, r=[rc, r_cst], w=[rpb[7]])
            MM(pb[7][0:L, 4:8], enm[:, 0:L], EYE4, True, True, r=[rc, r_cst], w=[rpb[7]])
            MM(pb[7][0:L, 8:12], wsr[:, 0:L], EYE4, True, True, r=[rc, r_cst], w=[rpb[7]])
            MM(pb[7][:, 16:20], ONE4[:, 0:128], d4[:, 0:4], True, True, r=[r_one4, rc], w=[rpb[7]])
            ACT(COLSs[pc][0:L, :], pb[7][0:L, 0:12], AF.Exp, r=[rpb[7]], w=[r_cols2[pc]])
            ACT(A0s[pc][:, :], pb[7][:, 16:20], AF.Exp, r=[rpb[7]], w=[r_cols2[pc]])
            for h in range(4):
                MM(pb[1][0:L, h * L:(h + 1) * L], KB[:, h, cs], QB[:, h, cs], True, True, r=[r_kb, r_qb], w=[rpb[1]])

        def front_b(c):
            pc = c % 2
            V("tensor_tensor", r=[rpb[1], r_wt2[pc]], w=[r_at2[pc]], out=ATs[pc][0:L, 0:4 * L], in0=pb[1][0:L, 0:4 * L], in1=WTs[pc][0:L, 0:4 * L], op=ALU.mult)

        def back(c):
            pc = c % 2
            cs = slice(c * L, (c + 1) * L)
            tc = t0 + c * L
            COLS, A0, AT = COLSs[pc], A0s[pc], ATs[pc]
            r_cols, r_at = r_cols2[pc], r_at2[pc]
            for h in range(4):
                bk, c0 = 3 + h // 2, (h % 2) * 256
                MM(pb[bk][0:L, c0:c0 + 129], QB[:, h, cs], CNB[:, h, 0:129], True, True, r=[r_qb, r_cnb], w=[rpb[bk]])
            for h in range(4):
                bk, c0 = 5 + h // 2, (h % 2) * 256
                MM(pb[bk][0:L, c0:c0 + 129], AT[0:L, h * L:(h + 1) * L], V1[0:L, c, h, 0:129], True, True, r=[r_at, r_v1], w=[rpb[bk]])
            for h in range(4):
                bk, c0 = 5 + h // 2, (h % 2) * 256
                ACT(HI[0:L, h, 0:129], pb[bk][0:L, c0:c0 + 129], AF.Copy, r=[rpb[bk]], w=[r_hi])
            for h in range(4):
                bk, c0 = 3 + h // 2, (h % 2) * 256
                V("scalar_tensor_tensor", r=[rpb[bk], r_cols, r_hi], w=[r_nd], out=ND[0:L, h, 0:129], in0=pb[bk][0:L, c0:c0 + 129],
                  scalar=COLS[0:L, h:h + 1], in1=HI[0:L, h, 0:129], op0=ALU.mult, op1=ALU.add)
            V("tensor_scalar", r=[r_nd], w=[r_da], out=DA[0:L, :], in0=ND[0:L, :, 128], scalar1=-1.0, scalar2=None, op0=ALU.mult)
            V("tensor_tensor", r=[r_nd, r_da], w=[r_da], out=DA[0:L, :], in0=DA[0:L, :], in1=ND[0:L, :, 128], op=ALU.max)
            V("tensor_tensor", r=[r_da, r_cols], w=[r_da], out=DA[0:L, :], in0=DA[0:L, :], in1=COLS[0:L, 4:8], op=ALU.max)
            V("reciprocal", r=[r_da], w=[r_da], out=DA[0:L, :], in_=DA[0:L, :])
            HHv = HH[0:L, :].rearrange("p (h d) -> p h d", h=4)
            V("tensor_tensor", r=[r_nd, r_da], w=[r_hh], out=HHv, in0=ND[0:L, :, 0:128], in1=DA[0:L, :].unsqueeze(2).to_broadcast([L, 4, 128]), op=ALU.mult)
            V("tensor_tensor", r=[r_hh], w=[r_hsq], out=HSQ[0:L, :], in0=HH[0:L, :], in1=HH[0:L, :], op=ALU.mult)
            V("tensor_reduce", r=[r_hsq], w=[r_da], out=SS4[0:L, :], in_=HSQ[0:L, :].rearrange("p (h d) -> p h d", h=4), axis=AX.X, op=ALU.add)
            ACT(SS4[0:L, :], SS4[0:L, :], AF.Sqrt, r=[r_da], w=[r_da], scale=1.0 / 128, bias=EPS)
            V("reciprocal", r=[r_da], w=[r_da], out=SS4[0:L, :], in_=SS4[0:L, :])
            V("tensor_tensor", r=[r_hh, r_da], w=[r_hh], out=HHv, in0=HHv, in1=SS4[0:L, :].unsqueeze(2).to_broadcast([L, 4, 128]), op=ALU.mult)
            V("tensor_tensor", r=[r_hh, r_ghd], w=[r_hh], out=HH[0:L, :], in0=HH[0:L, :], in1=GHD[0:L, :], op=ALU.mult)
            V("tensor_tensor", r=[r_hh, r_sob], w=[r_hbt], out=HBT[0:L, :], in0=HH[0:L, :], in1=SOBv[0:L, c, :], op=ALU.mult)
            for h in range(4):
                MM(pb[3][:, h * L:(h + 1) * L], HBT[0:L, h * 128:(h + 1) * 128], EYEB[0:L, 0:L], True, True, r=[r_hbt, r_eyeb], w=[rpb[3]])
            ACT(HB[:, :, tc:tc + L], pb[3][:, 0:4 * L].rearrange("p (h t) -> p h t", h=4), AF.Copy, r=[rpb[3]], w=[r_hb])
            for h in range(4):
                MM(pb[2][0:L, h * 128:(h + 1) * 128], KB[:, h, cs], EYEB[:, :], True, True, r=[r_kb, r_eyeb], w=[rpb[2]])
            V("tensor_tensor", r=[rpb[2], r_cols], w=[r_ks], out=KS[0:L, :].rearrange("p (h d) -> p h d", h=4),
              in0=pb[2][0:L, :].rearrange("p (h d) -> p h d", h=4), in1=COLS[0:L, 8:12].unsqueeze(2).to_broadcast([L, 4, 128]), op=ALU.mult)
            for h in range(4):
                bk, c0 = 5 + h // 2, (h % 2) * 256
                MM(pb[bk][:, c0:c0 + 129], KS[0:L, h * 128:(h + 1) * 128], V1[0:L, c, h, 0:129], True, True, r=[r_ks, r_v1], w=[rpb[bk]])
            for h in range(4):
                bk, c0 = 5 + h // 2, (h % 2) * 256
                V("scalar_tensor_tensor", r=[r_cn, r_cols, rpb[bk]], w=[r_cn], out=CN[:, h, 0:129], in0=CN[:, h, 0:129],
                  scalar=A0[:, h:h + 1], in1=pb[bk][:, c0:c0 + 129], op0=ALU.mult, op1=ALU.add)
            ACT(CNB[:], CN[:], AF.Copy, r=[r_cn], w=[r_cnb])

        front_a(0)
        front_b(0)
        for c in range(nch):
            if c + 1 < nch:
                front_a(c + 1)
            back(c)
            if c + 1 < nch:
                front_b(c + 1)
        V("tensor_copy", r=[r_rows], w=[r_sm], out=SM[:, 0:1], in_=NBR[:, n - 1:n])
        V("tensor_copy", r=[r_rows], w=[r_sm], out=SM[:, 1:2], in_=MR[:, n - 1:n])

    def state_out(oC, on, om):
        DMA(oC, CN[:, :, 0:128], r=[r_cn])
        V("tensor_copy", r=[r_cn], w=[r_nt4], out=NT4[:, :], in_=CN[:, :, 128])
        DMA(on, NT4[:, :], r=[r_nt4])
        V("tensor_tensor", r=[r_sm], w=[r_sm], out=SM[:, 16:17], in0=SM[:, 1:2], in1=SM[:, 0:1], op=ALU.subtract)
        DMA(om, SM[:, 16:17], r=[r_sm])

    for gi, (t0, n) in enumerate(GROUPS[:4]):
        mlstm_group(t0, n, 64, gi == 0, o_pconvT if gi == 3 else None)
    state_out(o_pC, o_pn, o_pm)
    for s in range(2):
        DMA(CN[:, :, 0:128], C0d[s].rearrange("h k v -> k h v"), w=[r_cn])
        DMA(NT4[:, :], n0T[:, s, :], w=[r_nt4])
        V("tensor_copy", r=[r_nt4], w=[r_cn], out=CN[:, :, 128], in_=NT4[:, :])
        ACT(CNB[:], CN[:], AF.Copy, r=[r_cn], w=[r_cnb])
        DMA(HALO[:, :, 0:3], convT[:, :, s, :], w=[r_halo])
        V("memset", r=[], w=[r_sm], ap=SM[:, 0:1], constant=0.0)
        DMA(SM[:, 1:2], m0T[s], w=[r_sm])
        mlstm_group(2048 + s * 32, 32, 32, True, o_sconvT[:, :, s, :])
        state_out(o_sC[:, s], o_sn[:, s, :], o_sm[s])
    P.barrier()
    if PHASES <= 3:
        P.emit(nc)
        return nc
    SG2 = CT
    for blk in range(2):
        P.dma("pool", lambda e, blk=blk: e.dma_start(out=WA[:, 0:4, :], in_=w_b_up[:, blk * 512:(blk + 1) * 512].rearrange("(k p) n -> p k n", p=128)), w=[r_wa])
        load_w(WB[:], r_wb, w_in, 4616 + blk * 512, 512)
        for (t0, n) in GROUPS:
            for c4 in range(4):
                fc = blk * 4 + c4
                for k in range(8):
                    MM(pb[0][:, 0:n], WB[:, k, c4 * 128:(c4 + 1) * 128], XN[:, k, t0:t0 + n], k == 0, k == 7, r=[r_wb, r_xn], w=[rpb[0]])
                ACT(SG2[:, 0:n], pb[0][:, 0:n], AF.Sigmoid, r=[rpb[0]], w=[r_ct])
                for k in range(4):
                    MM(pb[1][:, 0:n], WA[:, k, c4 * 128:(c4 + 1) * 128], HB[:, k, t0:t0 + n], k == 0, k == 3, r=[r_wa, r_hb], w=[rpb[1]])
                V("tensor_tensor", r=[r_ct, rpb[1]], w=[r_ct], out=SG2[:, 0:n], in0=SG2[:, 0:n], in1=pb[1][:, 0:n], op=ALU.mult)
                V("tensor_tensor", r=[r_ct, r_mix], w=[r_mix], out=MIX[:, fc, t0:t0 + n], in0=MIX[:, fc, t0:t0 + n], in1=SG2[:, 0:n], op=ALU.add)
    P.barrier()

    X1 = sb(PH, [128, 8, T], F32); r_x1 = Res()
    DMA(X1[:], xT, w=[r_x1])
    for blk in range(2):
        load_w(WA[:], r_wa, w_out, blk * 512, 512)
        for (t0, n) in GROUPS:
            for c4 in range(4):
                fc = blk * 4 + c4
                b = c4 % 2
                for k in range(8):
                    MM(pb[b][:, 0:n], WA[:, k, c4 * 128:(c4 + 1) * 128], MIX[:, k, t0:t0 + n], k == 0, k == 7, r=[r_wa, r_mix], w=[rpb[b]])
                V("tensor_tensor", r=[r_x1, rpb[b]], w=[r_x1], out=X1[:, fc, t0:t0 + n], in0=X1[:, fc, t0:t0 + n], in1=pb[b][:, 0:n], op=ALU.add)
    P.barrier()

    for (t0, n) in GROUPS:
        rmsnorm(lambda k: X1[:, k, t0:t0 + n], r_x1, lambda k: XN[:, k, t0:t0 + n], r_xn, 8, n)
    P.barrier()
    mixbase = MIX_OFF
    WCQ = sb(mixbase, [128, 8, 1024], BF16); r_wcq = Res()
    WCO = sb(mixbase + 16384, [128, 8, 1024], BF16); r_wco = Res()
    q = PH + 8 * T * 4
    QC = sb(q, [128, 8, 512], BF16); q += 8192
    OC = sb(q, [128, 8, 512], BF16); q += 8192
    EC = [sb(q, [128, 512], BF16), sb(q + 1024, [128, 512], BF16)]; q += 2048
    RDC = RS
    CMK = sb(SQ_OFF, [128, 8, 256], BF16)
    CMV = sb(SQ_OFF + 4096, [128, 2, 1024], BF16)
    assert SB_LO + q <= 229376, q
    r_qc, r_oc, r_rdc, r_cmk, r_cmv = Res(), Res(), Res(), Res(), Res()
    r_ec = [Res(), Res()]
    r_wcq2 = [Res(), Res()]
    r_wco2 = [Res(), Res()]
    for hf in range(2):
        load_w(WCQ[:, :, hf * 512:(hf + 1) * 512], r_wcq2[hf], w_cq, hf * 512, 512)
        load_w(WCO[:, :, hf * 512:(hf + 1) * 512], r_wco2[hf], w_co, hf * 512, 512)

    def xattn(cols0, nq, MKx, rmk, MVx, rmv):
        for h in range(4):
            for mc in range(2):
                for dc in range(2):
                    MM(pb[4 + mc][:, 0:nq], MKx[:, 2 * h + dc, mc * 128:(mc + 1) * 128], QC[:, 2 * h + dc, cols0:cols0 + nq], dc == 0, dc == 1,
                       r=[rmk, r_qc], w=[rpb[4 + mc]])
                ACT(EC[mc][:, 0:nq], pb[4 + mc][:, 0:nq], AF.Exp, r=[rpb[4 + mc]], w=[r_ec[mc]], scale=1.0 / 16)
            for mc in range(2):
                MM(pb[7][:, 0:nq], ONESB[:, :], EC[mc][:, 0:nq], mc == 0, mc == 1, r=[r_ones, r_ec[mc]], w=[rpb[7]])
            V("reciprocal", r=[rpb[7]], w=[r_rdc], out=RDC[:, 0:nq], in_=pb[7][:, 0:nq])
            for dvc in range(2):
                for mc in range(2):
                    MM(pb[6][:, 0:nq], MVx[:, mc, h * 256 + dvc * 128:h * 256 + (dvc + 1) * 128], EC[mc][:, 0:nq], mc == 0, mc == 1,
                       r=[rmv, r_ec[mc]], w=[rpb[6]])
                V("tensor_tensor", r=[rpb[6], r_rdc], w=[r_oc], out=OC[:, 2 * h + dvc, cols0:cols0 + nq], in0=pb[6][:, 0:nq], in1=RDC[:, 0:nq], op=ALU.mult)

    for gi, (t0, n) in enumerate(GROUPS):
        for fc in range(8):
            b = fc % 2
            for k in range(8):
                MM(pb[b][:, 0:n], WCQ[:, k, fc * 128:(fc + 1) * 128], XN[:, k, t0:t0 + n], k == 0, k == 7, r=[r_wcq2[fc // 4], r_xn], w=[rpb[b]])
            ACT(QC[:, fc, 0:n], pb[b][:, 0:n], AF.Copy, r=[rpb[b]], w=[r_qc])
        if gi < 4:
            xattn(0, n, MKT, r_mkt, MV, r_mv)
        else:
            for s in range(2):
                P.dma("pool", lambda e, s=s: e.dma_start(out=CMK[:], in_=cmkT[s]), w=[r_cmk])
                for hf in range(2):
                    P.dma("pool", lambda e, s=s, hf=hf: e.dma_start(out=CMV[:, :, hf * 512:(hf + 1) * 512],
                          in_=cmv[s][:, hf * 512:(hf + 1) * 512].rearrange("(b p) n -> p b n", p=128)), w=[r_cmv])
                xattn(s * 32, 32, CMK, r_cmk, CMV, r_cmv)
        for fc in range(8):
            b = fc % 2
            for k in range(8):
                MM(pb[b][:, 0:n], WCO[:, k, fc * 128:(fc + 1) * 128], OC[:, k, 0:n], k == 0, k == 7, r=[r_wco2[fc // 4], r_oc], w=[rpb[b]])
            V("tensor_tensor", r=[r_x1, rpb[b]], w=[r_x1], out=X1[:, fc, t0:t0 + n], in0=X1[:, fc, t0:t0 + n], in1=pb[b][:, 0:n], op=ALU.add)
    P.barrier()

    for (t0, n) in GROUPS:
        rmsnorm(lambda k: X1[:, k, t0:t0 + n], r_x1, lambda k: XN[:, k, t0:t0 + n], r_xn, 16, n)
    X1d = nc.dram_tensor("X1d", [128, 8, T], F32, kind="Internal").ap()
    r_x1d = Res()
    DMA(X1d, X1[:], r=[r_x1], w=[r_x1d])
    P.barrier()
    WPQ = sb(MIX_OFF, [128, 8, 2048], BF16); r_wpq = Res()
    QP = sb(SQ_OFF, [128, 16, 512], BF16); r_qp = Res()
    SUBT = sb(SQ_OFF + 16384, [128, 16, 128], BF16); r_subt = Res()
    SCO = sb(SQ_OFF + 20480, [128, 16, 128], F32); r_sc = Res()
    q = PH
    EIDX = sb(q, [128, 17, 128], I32); q += 8704
    GALL = sb(q, [128, 17, 128], F32); q += 8704
    SC2 = sb(q, [128, 2048], F32); q += 8192
    EQ = sb(q, [128, 2048], F32); q += 8192
    SV = sb(q, [128, 16, 16], F32); q += 1024
    SI = sb(q, [128, 16, 16], U32); q += 1024
    SIF = sb(q, [128, 16, 16], F32); q += 1024
    CVP = sb(q, [128, 8, 16], F32); q += 512
    CI = sb(q, [128, 8, 16], U32); q += 512
    CIF = sb(q, [128, 8, 16], F32); q += 512
    AF_ = sb(q, [128, 8, 16], F32); q += 512
    BF_ = sb(q, [128, 8, 16], F32); q += 512
    ISEL = sb(q, [128, 8, 16], F32); q += 512
    JSEL = sb(q, [128, 8, 16], F32); q += 512
    ZZ = sb(q, [128, 8], F32); q += 32
    Q6A_END = q
    Q6B = PH + 17408
    (r_eidx, r_gall, r_sc2, r_eq, r_sv, r_si, r_cv, r_ci, r_ab, r_ij, r_zz) = [Res() for _ in range(11)]
    r_svp, r_sip, r_sc2p, r_cvp, r_cip, r_eqp = [[Res(), Res()] for _ in range(6)]
    r_wpq4 = [Res() for _ in range(4)]
    for hf in range(4):
        load_w(WPQ[:, :, hf * 512:(hf + 1) * 512], r_wpq4[hf], w_pq, hf * 512, 512)
    P.dma("pool", lambda e: e.dma_start(out=SUBT[:], in_=subT), w=[r_subt])
    PUVB = nc.dram_tensor("PUVB", [16384, 2048], BF16, kind="Internal").ap()
    r_puvb = Res()
    NCAST = 16
    r_cast = [Res() for _ in range(NCAST)]
    for ic in range(NCAST):
        rows = 16384 // NCAST
        P.dma("pool", lambda e, ic=ic, rows=rows: e.dma_start(
            out=PUVB[ic * rows:(ic + 1) * rows, :].rearrange("r (a b) -> r a b", b=512),
            in_=peer_uv[ic * rows:(ic + 1) * rows, :].rearrange("r (a b) -> r a b", b=512)), w=[r_cast[ic]])
    V("memset", r=[], w=[r_eidx], ap=EIDX[:], constant=0)
    V("memset", r=[], w=[r_gall], ap=GALL[:], constant=0.0)
    CAND = SC2
    QPb = sb(Q6A_END, [128, 16, 512], BF16)
    SCOb = sb(Q6A_END + 16384, [128, 16, 128], F32)
    assert SB_LO + Q6A_END + 16384 + 8192 <= 229376
    QP2, r_qp2 = [QP, QPb], [r_qp, Res()]
    SCO2, r_scb = [SCO, SCOb], [r_sc, Res()]
    TILES = []
    for gi, (t0, n) in enumerate(GROUPS):
        for jt in range(4 if gi < 4 else 1):
            TILES.append((gi, jt, gi * 4 + jt, 128 if gi < 4 else 64))

    def do_qp(gi):
        t0, n = GROUPS[gi]
        for j in range(16):
            b = j % 2
            for k in range(8):
                MM(pb[b][:, 0:n], WPQ[:, k, j * 128:(j + 1) * 128], XN[:, k, t0:t0 + n], k == 0, k == 7, r=[r_wpq4[j // 4], r_xn], w=[rpb[b]])
            ACT(QP2[gi % 2][:, j, 0:n], pb[b][:, 0:n], AF.Copy, r=[rpb[b]], w=[r_qp2[gi % 2]])

    def do_scores(tq):
        gi, jt, ti, nt = TILES[tq]
        c0 = jt * 128
        for j4 in range(4):
            bk = 2 + j4
            for jj in range(4):
                j = j4 * 4 + jj
                MM(pb[bk][0:nt, jj * 128:(jj + 1) * 128], QP2[gi % 2][:, j, c0:c0 + nt], SUBT[:, j, :], True, True, r=[r_qp2[gi % 2], r_subt], w=[rpb[bk]])
            ACT(SCO2[tq % 2][0:nt, j4 * 4:(j4 + 1) * 4, :], pb[bk][0:nt, :].rearrange("p (j k) -> p j k", j=4), AF.Copy, r=[rpb[bk]], w=[r_scb[tq % 2]])

    do_qp(0)
    do_scores(0)
    for tq, (gi, jt, ti, nt) in enumerate(TILES):
        if True:
            SCOx, r_scx = SCO2[tq % 2], r_scb[tq % 2]
            if jt == 0 and gi + 1 < len(GROUPS):
                do_qp(gi + 1)
            if tq + 1 < len(TILES):
                do_scores(tq + 1)
            for j2 in range(0, 16, 2):
                js = (j2, j2 + 1)
                for pj, j in enumerate(js):
                    V("max", r=[r_scx], w=[r_svp[pj]], out=SV[0:nt, j, 0:8], in_=SCOx[0:nt, j, :])
                for pj, j in enumerate(js):
                    V("max_index", r=[r_scx, r_svp[pj]], w=[r_sip[pj]], out=SI[0:nt, j, 0:8], in_max=SV[0:nt, j, 0:8], in_values=SCOx[0:nt, j, :])
                for pj, j in enumerate(js):
                    V("match_replace", r=[r_scx, r_svp[pj]], w=[r_sc2p[pj]], out=SC2[0:nt, pj * 128:(pj + 1) * 128], in_to_replace=SV[0:nt, j, 0:8], in_values=SCOx[0:nt, j, :], imm_value=-1e30)
                for pj, j in enumerate(js):
                    V("max", r=[r_sc2p[pj]], w=[r_svp[pj]], out=SV[0:nt, j, 8:16], in_=SC2[0:nt, pj * 128:(pj + 1) * 128])
                for pj, j in enumerate(js):
                    V("max_index", r=[r_sc2p[pj], r_svp[pj]], w=[r_sip[pj]], out=SI[0:nt, j, 8:16], in_max=SV[0:nt, j, 8:16], in_values=SC2[0:nt, pj * 128:(pj + 1) * 128])
            V("tensor_copy", r=[r_sip[0], r_sip[1], r_si], w=[r_si], out=SIF[0:nt, :, 0:1], in_=SIF[0:nt, :, 0:1])
            V("tensor_copy", r=[r_si], w=[r_si], out=SIF[0:nt], in_=SI[0:nt])
            SV4 = SV[0:nt].rearrange("p (h c) a -> p h c a", c=2)
            SIF4 = SIF[0:nt].rearrange("p (h c) a -> p h c a", c=2)
            V("tensor_tensor", r=[r_sv, r_svp[0], r_svp[1], r_sc2p[0], r_sc2p[1], r_sip[0], r_sip[1]], w=[r_sc2], out=CAND[0:nt, :].rearrange("p (h a b) -> p h a b", h=8, a=16),
              in0=SV4[:, :, 0, :].unsqueeze(3).to_broadcast([nt, 8, 16, 16]), in1=SV4[:, :, 1, :].unsqueeze(2).to_broadcast([nt, 8, 16, 16]), op=ALU.add)
            for h2 in range(0, 8, 2):
                hs = (h2, h2 + 1)
                chs = [CAND[0:nt, h * 256:(h + 1) * 256] for h in hs]
                for ph, h in enumerate(hs):
                    V("max", r=[r_sc2], w=[r_cvp[ph]], out=CVP[0:nt, h, 0:8], in_=chs[ph])
                for ph, h in enumerate(hs):
                    V("max_index", r=[r_sc2, r_cvp[ph]], w=[r_cip[ph]], out=CI[0:nt, h, 0:8], in_max=CVP[0:nt, h, 0:8], in_values=chs[ph])
                for ph, h in enumerate(hs):
                    V("match_replace", r=[r_sc2, r_cvp[ph]], w=[r_eqp[ph]], out=EQ[0:nt, ph * 256:(ph + 1) * 256], in_to_replace=CVP[0:nt, h, 0:8], in_values=chs[ph], imm_value=-1e30)
                for ph, h in enumerate(hs):
                    V("max", r=[r_eqp[ph]], w=[r_cvp[ph]], out=CVP[0:nt, h, 8:16], in_=EQ[0:nt, ph * 256:(ph + 1) * 256])
                for ph, h in enumerate(hs):
                    V("max_index", r=[r_eqp[ph], r_cvp[ph]], w=[r_cip[ph]], out=CI[0:nt, h, 8:16], in_max=CVP[0:nt, h, 8:16], in_values=EQ[0:nt, ph * 256:(ph + 1) * 256])
            V("tensor_copy", r=[r_cip[0], r_cip[1], r_cvp[0], r_cvp[1], r_eqp[0], r_eqp[1], r_ci, r_cv, r_eq], w=[r_ci, r_cv, r_eq], out=CIF[0:nt, :, 0:1], in_=CIF[0:nt, :, 0:1])
            V("tensor_copy", r=[r_ci], w=[r_ci], out=CIF[0:nt], in_=CI[0:nt])
            EQ4 = EQ[0:nt, :].rearrange("p (h k a) -> p h k a", h=8, k=16)
            io16 = IOTA16[0:nt, :].unsqueeze(1).unsqueeze(1).to_broadcast([nt, 8, 16, 16])
            io256 = IOTA256[0:nt, :].unsqueeze(1).unsqueeze(1).to_broadcast([nt, 8, 16, 16])
            V("tensor_tensor", r=[r_ci, r_cst], w=[r_eq], out=EQ4, in0=CIF[0:nt].unsqueeze(3).to_broadcast([nt, 8, 16, 16]), in1=io256, op=ALU.is_ge)
            V("tensor_reduce", r=[r_eq], w=[r_ab], out=AF_[0:nt], in_=EQ4, axis=AX.X, op=ALU.add)
            V("tensor_scalar", r=[r_ab], w=[r_ab], out=AF_[0:nt], in0=AF_[0:nt], scalar1=-1.0, scalar2=None, op0=ALU.add)
            V("scalar_tensor_tensor", r=[r_ab, r_ci], w=[r_ab], out=BF_[0:nt].rearrange("p h k -> p (h k)"), in0=AF_[0:nt].rearrange("p h k -> p (h k)"), scalar=-16.0,
              in1=CIF[0:nt].rearrange("p h k -> p (h k)"), op0=ALU.mult, op1=ALU.add)
            for (sel, src, c_) in ((ISEL, AF_, 0), (JSEL, BF_, 1)):
                V("tensor_tensor", r=[r_ab, r_cst], w=[r_eq], out=EQ4, in0=src[0:nt].unsqueeze(3).to_broadcast([nt, 8, 16, 16]), in1=io16, op=ALU.is_equal)
                V("tensor_tensor", r=[r_eq, r_si], w=[r_eq], out=EQ4, in0=EQ4, in1=SIF4[:, :, c_, :].unsqueeze(2).to_broadcast([nt, 8, 16, 16]), op=ALU.mult)
                V("tensor_reduce", r=[r_eq], w=[r_ij], out=sel[0:nt], in_=EQ4, axis=AX.X, op=ALU.add)
            V("scalar_tensor_tensor", r=[r_ij], w=[r_ij], out=ISEL[0:nt].rearrange("p h k -> p (h k)"), in0=ISEL[0:nt].rearrange("p h k -> p (h k)"), scalar=128.0,
              in1=JSEL[0:nt].rearrange("p h k -> p (h k)"), op0=ALU.mult, op1=ALU.add)
            V("tensor_copy", r=[r_ij], w=[r_eidx], out=EIDX[0:nt, ti, :], in_=ISEL[0:nt].rearrange("p h k -> p (h k)"))
            V("tensor_copy", r=[r_cv], w=[r_zz], out=ZZ[0:nt], in_=CVP[0:nt, :, 0])
            V("tensor_tensor", r=[r_cv, r_zz], w=[r_cv], out=CVP[0:nt], in0=CVP[0:nt], in1=ZZ[0:nt].unsqueeze(2).to_broadcast([nt, 8, 16]), op=ALU.subtract)
            ACT(CVP[0:nt], CVP[0:nt], AF.Exp, r=[r_cv], w=[r_cv])
            V("tensor_reduce", r=[r_cv], w=[r_zz], out=ZZ[0:nt], in_=CVP[0:nt], axis=AX.X, op=ALU.add)
            V("reciprocal", r=[r_zz], w=[r_zz], out=ZZ[0:nt], in_=ZZ[0:nt])
            V("tensor_tensor", r=[r_cv, r_zz], w=[r_gall], out=GALL[0:nt, ti, :].rearrange("p (h k) -> p h k", h=8), in0=CVP[0:nt],
              in1=ZZ[0:nt].unsqueeze(2).to_broadcast([nt, 8, 16]), op=ALU.mult)
    P.barrier()
    if PHASES <= 6.5:
        P.emit(nc)
        return nc
    q = Q6B
    NGB = 6
    GBUF = [sb(MIX_OFF, [128, 4, 2048], BF16), sb(MIX_OFF + 16384, [128, 4, 2048], BF16), sb(WA_OFF, [128, 4, 2048], BF16)]
    for _ in range(NGB - 3):
        GBUF.append(sb(q, [128, 4, 2048], BF16)); q += 16384
    XTOK = [sb(q, [128, 1024], F32), sb(q + 4096, [128, 1024], F32)]; q += 8192
    JUNK = sb(q, [128, 1024], BF16); q += 2048
    X1T = sb(q, [128, 8, 128], F32); q += 4096
    OUTS = sb(q, [128, 1024], F32); q += 4096
    ACTV = sb(STG_OFF, [128, 128], F32)
    GLU = sb(STG_OFF + 512, [128, 128], F32)
    GW = sb(STG_OFF + 1024, [128, 128], F32)
    DG = [sb(STG_OFF + 1536 + i * 256, [128, 128], BF16) for i in range(8)]
    assert SB_LO + q <= 229376, q
    r_gbuf = [[Res() for _ in range(4)] for _ in range(NGB)]
    r_av = [Res() for _ in range(NGB)]
    r_gl = [Res() for _ in range(NGB)]
    r_gw = [Res() for _ in range(NGB)]
    r_dg = [Res() for _ in range(8)]
    r_xtok = [Res(), Res()]
    r_x1t, r_outs = Res(), Res()
    V("memset", r=[], w=[r_xtok[0]], ap=XTOK[0][:], constant=0.0)
    V("memset", r=[], w=[r_xtok[1]], ap=XTOK[1][:], constant=0.0)
    NG = 32
    gcount = [0]
    dgc = [0]

    def make_xtok(ti):
        nt = 128 if ti < 16 else 64
        tk0 = ti * 128
        xt, rxt = XTOK[ti % 2], r_xtok[ti % 2]
        for k in range(8):
            bk = k // 4
            MM(pb[bk][0:nt, (k % 4) * 128:(k % 4 + 1) * 128], XN[:, k, tk0:tk0 + nt], EYEB[:, :], True, True, r=[r_xn, r_eyeb], w=[rpb[bk]])
        for bk in range(2):
            ACT(xt[0:nt, bk * 512:(bk + 1) * 512], pb[bk][0:nt, :], AF.Copy, r=[rpb[bk]], w=[rxt])

    def epilogue(ti):
        nt = 128 if ti < 16 else 64
        tk0 = ti * 128
        DMA(X1T[:, :, 0:nt], X1d[:, :, tk0:tk0 + nt], r=[r_x1d], w=[r_x1t])
        for k in range(8):
            bk = 4 + k // 4
            MM(pb[bk][:, (k % 4) * 128:(k % 4) * 128 + nt], OUTS[0:nt, k * 128:(k + 1) * 128], EYE[0:nt, 0:nt], True, True, r=[r_outs, r_cst], w=[rpb[bk]])
        for bk in range(2):
            V("tensor_tensor", r=[r_x1t, rpb[4 + bk]], w=[r_x1t], out=X1T[:, bk * 4:(bk + 1) * 4, 0:nt], in0=X1T[:, bk * 4:(bk + 1) * 4, 0:nt],
              in1=pb[4 + bk][:, :].rearrange("p (k t) -> p k t", k=4)[:, :, 0:nt], op=ALU.add)
        YTv = OUTS[:, :].rearrange("p (k t) -> p k t", k=8)
        rmsnorm(lambda k: X1T[:, k, 0:nt], r_x1t, lambda k: YTv[:, k, 0:nt], r_outs, 24, nt, bank=6)
        DMA(o_yT[:, :, tk0:tk0 + nt], YTv[:, :, 0:nt], r=[r_outs])

    pending = None
    make_xtok(0)
    for ti in range(17):
        xt, rxt = XTOK[ti % 2], r_xtok[ti % 2]
        gbase = gcount[0]
        for g in range(NG + 1):
            if g < NG:
                bi = (gbase + g) % NGB
                e0 = 4 * g
                for i in range(4):
                    P.dma("pool", lambda e, bi=bi, i=i, e0=e0, ti=ti: e.indirect_dma_start(
                        out=GBUF[bi][:, i, :], out_offset=None, in_=PUVB,
                        in_offset=bass.IndirectOffsetOnAxis(ap=EIDX[:, ti, e0 + i:e0 + i + 1], axis=0)), r=[r_eidx] + r_cast, w=[r_gbuf[bi][i]])
                for i in range(4):
                    V("scalar_tensor_tensor", r=[r_gbuf[bi][i], rxt], w=[r_av[bi]], out=JUNK[:, :], in0=GBUF[bi][:, i, 0:1024], scalar=1.0,
                      in1=xt[:, :], op0=ALU.mult, op1=ALU.mult, accum_out=ACTV[:, e0 + i:e0 + i + 1])
                ACT(GLU[:, e0:e0 + 4], ACTV[:, e0:e0 + 4], AF.Gelu, r=[r_av[bi]], w=[r_gl[bi]])
            if g >= 1:
                gg = g - 1
                bi = (gbase + gg) % NGB
                e0 = 4 * gg
                V("tensor_tensor", r=[r_gl[bi], r_gall], w=[r_gw[bi]], out=GW[:, e0:e0 + 4], in0=GLU[:, e0:e0 + 4], in1=GALL[:, ti, e0:e0 + 4], op=ALU.mult)
                for i in range(4):
                    e_ = e0 + i
                    di = dgc[0] % 8
                    dgc[0] += 1
                    ACT(DG[di][:, :], EYEB[:, :], AF.Copy, r=[r_eyeb, r_gw[bi]], w=[r_dg[di]], scale=GW[:, e_:e_ + 1])
                    for hf in range(2):
                        MM(pb[2 + hf][:, :], DG[di][:, :], GBUF[bi][:, i, 1024 + hf * 512:1024 + (hf + 1) * 512], e_ == 0, e_ == 127,
                           r=[r_dg[di], r_gbuf[bi][i]], w=[rpb[2 + hf]])
            if g == 6 and pending is not None:
                epilogue(pending)
                pending = None
            if g == 20 and ti + 1 < 17:
                make_xtok(ti + 1)
        gcount[0] += NG
        for hf in range(2):
            ACT(OUTS[:, hf * 512:(hf + 1) * 512], pb[2 + hf][:, :], AF.Copy, r=[rpb[2 + hf]], w=[r_outs])
        pending = ti
    epilogue(pending)
    P.emit(nc)
    return nc


def _consts():
    c = np.zeros((128, NCST), np.float32)
    c[:, 0:128] = np.eye(128, dtype=np.float32)
    k = np.arange(128)[:, None]
    qq = np.arange(128)[None, :]
    c[:, 128:256] = 1.0 - ((qq < 64) & (k >= 64))
    c[:, 256:384] = 1.0 - ((qq >= 64) & (k < 64))
    s = np.arange(64)[:, None]
    t = np.arange(64)[None, :]
    m64 = np.where(s <= t, 0.0, -30000.0).astype(np.float32)
    c[0:64, 384:640] = np.tile(m64, (1, 4))
    s = np.arange(32)[:, None]
    t = np.arange(32)[None, :]
    m32 = np.where(s <= t, 0.0, -30000.0).astype(np.float32)
    c[0:32, 640:768] = np.tile(m32, (1, 4))
    for h in range(4):
        c[h, 768 + h * 64:768 + (h + 1) * 64] = 1.0
        c[h, 1024 + h * 32:1024 + (h + 1) * 32] = 1.0
    c[:, 1152:1168] = np.arange(16, dtype=np.float32)[None, :]
    c[:, 1168:1184] = 16.0 * np.arange(16, dtype=np.float32)[None, :]
    return c


def _fm(a):
    return np.ascontiguousarray(a.reshape(a.shape[0], 8, 128).transpose(2, 1, 0))


def _fm_inv(a):
    return np.ascontiguousarray(a.transpose(2, 1, 0).reshape(a.shape[2], 1024))


_NC_CACHE = {}


def kernel(x_prompt, x_sample, mem_prompt, cache_a_k, cache_a_v, state_b_conv, state_b_C, state_b_n,
           state_b_m, cache_mem_k, cache_mem_v, g_mix, w_in, conv_w, conv_b, b_if, g_head, rel_bias,
           w_a_up, w_b_up, w_out, g_mem, w_mk, w_mv, g_cross, w_cq, w_co, g_ffn, w_pq, sub_keys,
           peer_u, peer_v, g_final):
    f = lambda a: np.asarray(a, dtype=np.float32)
    x_prompt, x_sample, mem_prompt = f(x_prompt), f(x_sample), f(mem_prompt)
    prm = np.zeros((128, 80), np.float32)
    for i, g in enumerate([g_mix[0], g_cross[0], g_ffn[0], g_final, g_mem[0], conv_b[0]]):
        prm[:, i * 8:(i + 1) * 8] = f(g).reshape(8, 128).T
    cw = f(conv_w[0])
    prm[:, 48:80] = cw.reshape(4, 8, 128).transpose(2, 1, 0).reshape(128, 32)
    gb4 = np.ascontiguousarray(f(b_if[0]).reshape(2, 4).T)
    shared = dict(
        w_in=f(w_in[0]), w_a_up=f(w_a_up[0]), w_b_up=f(w_b_up[0]), w_out=f(w_out[0]), w_mk=f(w_mk[0]),
        w_mv=f(w_mv[0]), w_cq=f(w_cq[0]), w_co=f(w_co[0]), w_pq=f(w_pq[0]),
        subT=np.ascontiguousarray(f(sub_keys[0]).reshape(16, 128, 128).transpose(2, 0, 1)),
        prm=prm, gb4=gb4, ghead=f(g_head[0]).reshape(1, 512), relb=f(rel_bias[0]), cst=_consts(),
    )
    if PHASES >= 6:
        shared['peer_uv'] = np.ascontiguousarray(np.concatenate([f(peer_u[0]), f(peer_v[0])], axis=1))
    in_maps = []
    for c in range(NCORES):
        ss = [2 * c, 2 * c + 1]
        X = np.concatenate([x_prompt[c], x_sample[ss[0]], x_sample[ss[1]]], axis=0)
        ck = f(cache_a_k[0])[ss]
        ckT_ = np.ascontiguousarray(ck.reshape(2, 512, 4, 128).transpose(0, 3, 2, 1))
        cmk = f(cache_mem_k[0])[ss]
        cmkT_ = np.ascontiguousarray(cmk.reshape(2, 256, 8, 128).transpose(0, 3, 2, 1))
        conv = f(state_b_conv[0])[ss]
        convT_ = np.ascontiguousarray(conv.reshape(2, 3, 8, 128).transpose(3, 2, 0, 1))
        d = dict(shared)
        d.update(
            xT=_fm(X), memT=_fm(mem_prompt[c]), ckT=ckT_,
            cav=np.ascontiguousarray(f(cache_a_v[0])[ss].reshape(2, 512, 512)),
            convT=convT_, C0=np.ascontiguousarray(f(state_b_C[0])[ss]),
            n0T=np.ascontiguousarray(f(state_b_n[0])[ss].transpose(2, 0, 1)),
            m0T=np.ascontiguousarray(f(state_b_m[0])[ss].reshape(2, 4, 1)),
            cmkT=cmkT_, cmv=np.ascontiguousarray(f(cache_mem_v[0])[ss].reshape(2, 256, 1024)),
        )
        in_maps.append(d)
    if "nc" not in _NC_CACHE:
        _NC_CACHE["nc"] = build_program()
    nc = _NC_CACHE["nc"]
    res = run_bass_kernel_spmd(nc, in_maps, core_ids=list(range(NCORES)))
    R = res.results
    y_prompt = np.zeros((8, 2048, 1024), np.float32)
    y_sample = np.zeros((16, 32, 1024), np.float32)
    p_a_k = np.zeros((1, 8, 512, 8, 64), np.float32)
    p_a_v = np.zeros((1, 8, 512, 8, 64), np.float32)
    p_b_conv = np.zeros((1, 8, 3, 1024), np.float32)
    p_b_C = np.zeros((1, 8, 4, 128, 128), np.float32)
    p_b_n = np.zeros((1, 8, 4, 128), np.float32)
    p_b_m = np.zeros((1, 8, 4), np.float32)
    p_mem_k = np.zeros((1, 8, 256, 4, 256), np.float32)
    p_mem_v = np.zeros((1, 8, 256, 4, 256), np.float32)
    s_a_k = np.zeros((1, 16, 32, 8, 64), np.float32)
    s_a_v = np.zeros((1, 16, 32, 8, 64), np.float32)
    s_b_conv = np.zeros((1, 16, 3, 1024), np.float32)
    s_b_C = np.zeros((1, 16, 4, 128, 128), np.float32)
    s_b_n = np.zeros((1, 16, 4, 128), np.float32)
    s_b_m = np.zeros((1, 16, 4), np.float32)
    for c in range(NCORES):
        r = R[c]
        y = _fm_inv(r["yT"])
        y_prompt[c] = y[:2048]
        y_sample[2 * c] = y[2048:2080]
        y_sample[2 * c + 1] = y[2080:2112]
        p_a_k[0, c] = r["pakT"].transpose(2, 1, 0).reshape(512, 8, 64)
        p_a_v[0, c] = r["pav"].reshape(512, 8, 64)
        p_b_conv[0, c] = r["pconvT"].transpose(2, 1, 0).reshape(3, 1024)
        p_b_C[0, c] = r["pC"].transpose(1, 0, 2)
        p_b_n[0, c] = r["pn"].T
        p_b_m[0, c] = r["pm"][:, 0]
        p_mem_k[0, c] = r["pmkT"].transpose(2, 1, 0).reshape(256, 4, 256)
        p_mem_v[0, c] = r["pmv"].reshape(256, 4, 256)
        sak = r["sakT"].transpose(2, 1, 0).reshape(64, 8, 64)
        sav = r["sav"].reshape(64, 8, 64)
        for s in range(2):
            s_a_k[0, 2 * c + s] = sak[s * 32:(s + 1) * 32]
            s_a_v[0, 2 * c + s] = sav[s * 32:(s + 1) * 32]
            s_b_conv[0, 2 * c + s] = r["sconvT"][:, :, s, :].transpose(2, 1, 0).reshape(3, 1024)
            s_b_C[0, 2 * c + s] = r["sC"][:, s].transpose(1, 0, 2)
            s_b_n[0, 2 * c + s] = r["sn"][:, s, :].T
            s_b_m[0, 2 * c + s] = r["sm"][s, :, 0]
    return (y_prompt, y_sample, p_a_k, p_a_v, p_b_conv, p_b_C, p_b_n, p_b_m, p_mem_k, p_mem_v,
            s_a_k, s_a_v, s_b_conv, s_b_C, s_b_n, s_b_m)
```
